# Optimizing a Trainium2 kernel written in Bass

```python
import math
import jax, jax.numpy as jnp
from jax import lax
import numpy as np

D_MODEL = 1024
BATCH = 4
SEQ = 8192
DEPTH = 2

N_MEM = 256
GRID_W = 64
EPS = 1e-6
Q_BLOCK = 128
NEG_INF = -1e30

D_FF = 2816

MLA_HEADS = 8
MLA_Q_LORA = 256
MLA_KV_LORA = 128
MLA_NOPE = 64
MLA_ROPE = 32
MLA_QK = MLA_NOPE + MLA_ROPE
MLA_V = 64
ROPE_THETA = 10000.0

SWA_HEADS = 4
SWA_KV_HEADS = 2
SWA_GROUP = SWA_HEADS // SWA_KV_HEADS
SWA_DH = 64
SWA_WINDOW = 128

NA_HEADS = 4
NA_DH = 64
NA_KR = 8
NA_KC = 16

MEM_HEADS = 4
MEM_DH = 64

MLA_IN = MLA_Q_LORA + MLA_KV_LORA + MLA_ROPE
SWA_IN = (SWA_HEADS + 2 * SWA_KV_HEADS) * SWA_DH
NA_IN = 3 * NA_HEADS * NA_DH
MIX_IN = MLA_IN + SWA_IN + NA_IN
MLA_OUT = MLA_HEADS * MLA_V
SWA_OUT = SWA_HEADS * SWA_DH
NA_OUT = NA_HEADS * NA_DH
MIX_WIDTH = MLA_OUT + SWA_OUT + NA_OUT

kernel_name = "hybrid_mla_swa_natten_macaron_encoder"


def rms_norm(x, g):
    xf = x.astype(jnp.float32)
    y = xf * lax.rsqrt(jnp.mean(xf * xf, axis=-1, keepdims=True) + EPS)
    return (y * g.astype(jnp.float32)).astype(x.dtype)


def softmax_f32(s, dtype):
    return jax.nn.softmax(s.astype(jnp.float32), axis=-1).astype(dtype)


def swiglu(x, w_in, w_out):
    g, u = jnp.split(x @ w_in, 2, axis=-1)
    return (jax.nn.silu(g) * u) @ w_out


def alibi_slopes(n):
    return 2.0 ** (-8.0 * jnp.arange(1, n + 1, dtype=jnp.float32) / n)


def rope_tables(S):
    inv = 1.0 / (ROPE_THETA ** (jnp.arange(0, MLA_ROPE, 2, dtype=jnp.float32) / MLA_ROPE))
    ang = jnp.arange(S, dtype=jnp.float32)[:, None] * inv[None, :]
    return jnp.cos(ang), jnp.sin(ang)


def apply_rope(x, cos, sin):
    x1, x2 = jnp.split(x, 2, axis=-1)
    c = cos[None, :, None, :].astype(x.dtype)
    s = sin[None, :, None, :].astype(x.dtype)
    return jnp.concatenate([x1 * c - x2 * s, x1 * s + x2 * c], axis=-1)


def mla_mixer(z, q_norm_g, w_uq, kv_norm_g, w_ukv, q_gain, k_gain):
    B, S, _ = z.shape
    cq, ckv, k_rope = jnp.split(z, [MLA_Q_LORA, MLA_Q_LORA + MLA_KV_LORA], axis=-1)
    q = (rms_norm(cq, q_norm_g) @ w_uq).reshape(B, S, MLA_HEADS, MLA_QK)
    kv = (rms_norm(ckv, kv_norm_g) @ w_ukv).reshape(B, S, MLA_HEADS, MLA_NOPE + MLA_V)
    k_nope, v = jnp.split(kv, [MLA_NOPE], axis=-1)
    k_rope = jnp.broadcast_to(k_rope[:, :, None, :], (B, S, MLA_HEADS, MLA_ROPE))
    k = jnp.concatenate([k_nope, k_rope], axis=-1)
    q = rms_norm(q, q_gain)
    k = rms_norm(k, k_gain)
    cos, sin = rope_tables(S)
    q = jnp.concatenate([q[..., :MLA_NOPE], apply_rope(q[..., MLA_NOPE:], cos, sin)], axis=-1)
    k = jnp.concatenate([k[..., :MLA_NOPE], apply_rope(k[..., MLA_NOPE:], cos, sin)], axis=-1)
    scale = MLA_QK ** -0.5
    nb = S // Q_BLOCK
    qb = q.reshape(B, nb, Q_BLOCK, MLA_HEADS, MLA_QK).transpose(1, 0, 2, 3, 4)

    def block(qi):
        s = jnp.einsum('bqhd,bkhd->bhqk', qi, k).astype(jnp.float32) * scale
        p = softmax_f32(s, v.dtype)
        return jnp.einsum('bhqk,bkhv->bqhv', p, v)

    o = lax.map(block, qb)
    return o.transpose(1, 0, 2, 3, 4).reshape(B, S, MLA_OUT)


def swa_mixer(z, q_gain, k_gain, sink):
    B, S, _ = z.shape
    q, k, v = jnp.split(z, [SWA_HEADS * SWA_DH, (SWA_HEADS + SWA_KV_HEADS) * SWA_DH], axis=-1)
    q = rms_norm(q.reshape(B, S, SWA_HEADS, SWA_DH), q_gain)
    k = rms_norm(k.reshape(B, S, SWA_KV_HEADS, SWA_DH), k_gain)
    v = v.reshape(B, S, SWA_KV_HEADS, SWA_DH)
    nb = S // Q_BLOCK
    qb = q.reshape(B, nb, Q_BLOCK, SWA_KV_HEADS, SWA_GROUP, SWA_DH)

    def band(t):
        tp = jnp.pad(t, ((0, 0), (Q_BLOCK, Q_BLOCK), (0, 0), (0, 0)))
        tp = tp.reshape(B, nb + 2, Q_BLOCK, SWA_KV_HEADS, SWA_DH)
        return jnp.concatenate([tp[:, :nb], tp[:, 1:nb + 1], tp[:, 2:]], axis=2)

    kb, vb = band(k), band(v)
    s = jnp.einsum('bnqhgd,bnkhd->bnhgqk', qb, kb).astype(jnp.float32) * (SWA_DH ** -0.5)
    qpos = jnp.arange(S, dtype=jnp.int32).reshape(nb, Q_BLOCK)
    kpos = jnp.arange(nb, dtype=jnp.int32)[:, None] * Q_BLOCK - Q_BLOCK + jnp.arange(3 * Q_BLOCK, dtype=jnp.int32)[None, :]
    dist = jnp.abs(qpos[:, :, None] - kpos[:, None, :])
    valid = (dist <= SWA_WINDOW) & (kpos[:, None, :] >= 0) & (kpos[:, None, :] < S)
    slopes = alibi_slopes(SWA_HEADS).reshape(SWA_KV_HEADS, SWA_GROUP)
    s = s - slopes[None, None, :, :, None, None] * dist.astype(jnp.float32)[None, :, None, None, :, :]
    s = jnp.where(valid[None, :, None, None, :, :], s, NEG_INF)
    sink_col = jnp.broadcast_to(
        sink.astype(jnp.float32).reshape(SWA_KV_HEADS, SWA_GROUP)[None, None, :, :, None, None],
        s.shape[:-1] + (1,))
    p = softmax_f32(jnp.concatenate([s, sink_col], axis=-1), v.dtype)[..., :-1]
    o = jnp.einsum('bnhgqk,bnkhd->bnqhgd', p, vb)
    return o.reshape(B, S, SWA_OUT)


def na_mixer(z, q_gain, k_gain, rel_bias):
    B, S, _ = z.shape
    rows = S // GRID_W
    kr = min(NA_KR, rows)
    q, k, v = jnp.split(z, 3, axis=-1)
    q = rms_norm(q.reshape(B, rows, GRID_W, NA_HEADS, NA_DH), q_gain)
    k = rms_norm(k.reshape(B, rows, GRID_W, NA_HEADS, NA_DH), k_gain)
    v = v.reshape(B, rows, GRID_W, NA_HEADS, NA_DH)
    cols = jnp.arange(GRID_W, dtype=jnp.int32)
    c0 = jnp.clip(cols - NA_KC // 2, 0, GRID_W - NA_KC)
    col_idx = c0[:, None] + jnp.arange(NA_KC, dtype=jnp.int32)[None, :]
    dc_idx = col_idx - cols[:, None] + (NA_KC - 1)
    scale = NA_DH ** -0.5

    def row_block(args):
        r, q_r = args
        r0 = jnp.clip(r - kr // 2, 0, rows - kr)
        k_rows = lax.dynamic_slice_in_dim(k, r0, kr, axis=1)
        v_rows = lax.dynamic_slice_in_dim(v, r0, kr, axis=1)
        k_win = k_rows[:, :, col_idx]
        v_win = v_rows[:, :, col_idx]
        s = jnp.einsum('bqhd,brqkhd->bhqrk', q_r, k_win).astype(jnp.float32) * scale
        dr_idx = r0 + jnp.arange(kr, dtype=jnp.int32) - r + (NA_KR - 1)
        bias = rel_bias[:, dr_idx][:, :, dc_idx]
        s = s + bias.transpose(0, 2, 1, 3).astype(jnp.float32)[None]
        p = softmax_f32(s.reshape(B, NA_HEADS, GRID_W, kr * NA_KC), v.dtype)
        p = p.reshape(B, NA_HEADS, GRID_W, kr, NA_KC)
        return jnp.einsum('bhqrk,brqkhd->bqhd', p, v_win)

    o = lax.map(row_block, (jnp.arange(rows, dtype=jnp.int32), q.transpose(1, 0, 2, 3, 4)))
    return o.transpose(1, 0, 2, 3, 4).reshape(B, S, NA_OUT)


def mem_xattn(h, mem, mem_g, w_q, w_kv, q_gain, k_gain, w_o):
    B, S, _ = h.shape
    M = mem.shape[1]
    m = rms_norm(mem, mem_g)
    q = rms_norm((h @ w_q).reshape(B, S, MEM_HEADS, MEM_DH), q_gain)
    kv = (m @ w_kv).reshape(B, M, 2, MEM_HEADS, MEM_DH)
    k = rms_norm(kv[:, :, 0], k_gain)
    v = kv[:, :, 1]
    s = jnp.einsum('bshd,bmhd->bhsm', q, k).astype(jnp.float32) * (MEM_DH ** -0.5)
    p = softmax_f32(s, v.dtype)
    o = jnp.einsum('bhsm,bmhd->bshd', p, v).reshape(B, S, MEM_HEADS * MEM_DH)
    return o @ w_o


def setup_inputs(seed: int = 0) -> dict:
    key = jax.random.key(seed)
    ks = iter(jax.random.split(key, 40))
    L = DEPTH

    def w(shape, fan_in):
        return jax.random.normal(next(ks), shape, jnp.float32) * (fan_in ** -0.5)

    def gain(shape):
        return 1.0 + 0.02 * jax.random.normal(next(ks), shape, jnp.float32)

    return {
        "x": jax.random.normal(next(ks), (BATCH, SEQ, D_MODEL), jnp.float32),
        "mem": jax.random.normal(next(ks), (BATCH, N_MEM, D_MODEL), jnp.float32),
        "ffn1_norm": gain((L, D_MODEL)),
        "ffn1_w_in": w((L, D_MODEL, 2 * D_FF), D_MODEL),
        "ffn1_w_out": w((L, D_FF, D_MODEL), D_FF),
        "mix_norm": gain((L, D_MODEL)),
        "w_mix_in": w((L, D_MODEL, MIX_IN), D_MODEL),
        "mla_q_norm": gain((L, MLA_Q_LORA)),
        "mla_w_uq": w((L, MLA_Q_LORA, MLA_HEADS * MLA_QK), MLA_Q_LORA),
        "mla_kv_norm": gain((L, MLA_KV_LORA)),
        "mla_w_ukv": w((L, MLA_KV_LORA, MLA_HEADS * (MLA_NOPE + MLA_V)), MLA_KV_LORA),
        "mla_q_gain": gain((L, MLA_QK)),
        "mla_k_gain": gain((L, MLA_QK)),
        "swa_q_gain": gain((L, SWA_DH)),
        "swa_k_gain": gain((L, SWA_DH)),
        "swa_sink": 0.5 * jax.random.normal(next(ks), (L, SWA_HEADS), jnp.float32),
        "na_q_gain": gain((L, NA_DH)),
        "na_k_gain": gain((L, NA_DH)),
        "na_rel_bias": 0.1 * jax.random.normal(next(ks), (L, NA_HEADS, 2 * NA_KR - 1, 2 * NA_KC - 1), jnp.float32),
        "grp_out_gain": gain((L, MIX_WIDTH)),
        "w_mix_out": w((L, MIX_WIDTH, D_MODEL), MIX_WIDTH),
        "mem_norm_x": gain((L, D_MODEL)),
        "mem_norm_m": gain((L, D_MODEL)),
        "mem_w_q": w((L, D_MODEL, MEM_HEADS * MEM_DH), D_MODEL),
        "mem_w_kv": w((L, D_MODEL, 2 * MEM_HEADS * MEM_DH), D_MODEL),
        "mem_q_gain": gain((L, MEM_DH)),
        "mem_k_gain": gain((L, MEM_DH)),
        "mem_w_o": w((L, MEM_HEADS * MEM_DH, D_MODEL), MEM_HEADS * MEM_DH),
        "ffn2_norm": gain((L, D_MODEL)),
        "ffn2_w_in": w((L, D_MODEL, 2 * D_FF), D_MODEL),
        "ffn2_w_out": w((L, D_FF, D_MODEL), D_FF),
        "block_norm": gain((L, D_MODEL)),
    }


def reference(x, mem, ffn1_norm, ffn1_w_in, ffn1_w_out, mix_norm, w_mix_in,
              mla_q_norm, mla_w_uq, mla_kv_norm, mla_w_ukv, mla_q_gain, mla_k_gain,
              swa_q_gain, swa_k_gain, swa_sink, na_q_gain, na_k_gain, na_rel_bias,
              grp_out_gain, w_mix_out, mem_norm_x, mem_norm_m, mem_w_q, mem_w_kv,
              mem_q_gain, mem_k_gain, mem_w_o, ffn2_norm, ffn2_w_in, ffn2_w_out,
              block_norm):
    for l in range(DEPTH):
        x = x + 0.5 * swiglu(rms_norm(x, ffn1_norm[l]), ffn1_w_in[l], ffn1_w_out[l])
        h = rms_norm(x, mix_norm[l])
        z = h @ w_mix_in[l]
        z_mla, z_swa, z_na = jnp.split(z, [MLA_IN, MLA_IN + SWA_IN], axis=-1)
        o_mla = mla_mixer(z_mla, mla_q_norm[l], mla_w_uq[l], mla_kv_norm[l], mla_w_ukv[l],
                          mla_q_gain[l], mla_k_gain[l])
        o_swa = swa_mixer(z_swa, swa_q_gain[l], swa_k_gain[l], swa_sink[l])
        o_na = na_mixer(z_na, na_q_gain[l], na_k_gain[l], na_rel_bias[l])
        g = grp_out_gain[l]
        o = jnp.concatenate([
            rms_norm(o_mla, g[:MLA_OUT]),
            rms_norm(o_swa, g[MLA_OUT:MLA_OUT + SWA_OUT]),
            rms_norm(o_na, g[MLA_OUT + SWA_OUT:]),
        ], axis=-1)
        x = x + o @ w_mix_out[l]
        x = x + mem_xattn(rms_norm(x, mem_norm_x[l]), mem, mem_norm_m[l], mem_w_q[l], mem_w_kv[l],
                          mem_q_gain[l], mem_k_gain[l], mem_w_o[l])
        x = x + 0.5 * swiglu(rms_norm(x, ffn2_norm[l]), ffn2_w_in[l], ffn2_w_out[l])
        x = rms_norm(x, block_norm[l])
    return x
```

```python
import contextlib
import numpy as np
import concourse.bass as bass
import concourse.mybir as mybir
from concourse.bass_utils import run_bass_kernel_spmd

F32 = mybir.dt.float32
BF16 = mybir.dt.bfloat16
AF = mybir.ActivationFunctionType
ALU = mybir.AluOpType
AX = mybir.AxisListType

D = 1024
BATCH = 4
SEQ = 8192
DEPTH = 2
NMEM = 256
GW = 64
EPS = 1e-6
DFF = 2816
NFC = DFF // 128
TOK = SEQ // 2
NT = TOK // 128
NEG = -30000.0

MLA_H = 8
MLA_QL = 256
MLA_KVL = 128
MLA_NOPE = 64
MLA_ROPE = 32
MLA_QK = 96
MLA_V = 64
MIX_IN = 1696


class Res:
    __slots__ = ("name", "w", "rs")

    def __init__(self, name):
        self.name = name
        self.w = None
        self.rs = []


class DSem:
    __slots__ = ("sem", "issued", "inc")

    def __init__(self, sem):
        self.sem = sem
        self.issued = 0
        self.inc = 16


class Ins:
    __slots__ = ("eng", "fn", "waits", "cdeps", "seq", "marked", "dsem")

    def __init__(self, eng, fn):
        self.eng = eng
        self.fn = fn
        self.waits = []
        self.cdeps = []
        self.seq = 0
        self.marked = False
        self.dsem = None


class T:
    __slots__ = ("ap", "res")

    def __init__(self, ap, res):
        self.ap = ap
        self.res = res

    def __getitem__(self, k):
        return T(self.ap[k], self.res)

    def r(self, pat, **kw):
        return T(self.ap.rearrange(pat, **kw), self.res)

    def bc(self, dt):
        return T(self.ap.bitcast(dt), self.res)

    def bcast(self, axis, shape):
        return T(self.ap.unsqueeze(axis).to_broadcast(shape), self.res)


ENGS = ("pe", "act", "dve", "pool", "sp")


class Prog:
    def __init__(self, nc, stack):
        self.nc = nc
        self.stack = stack
        self.q = {e: [] for e in ENGS}
        self.esem = {e: stack.enter_context(nc.semaphore("es_" + e)) for e in ENGS}
        self.dsems = []
        self.dsem_by_name = {}
        self.locks = {}
        self.last = {e: None for e in ENGS}

    def dsem(self, name, inc=16):
        if name in self.dsem_by_name:
            return self.dsem_by_name[name]
        d = DSem(self.stack.enter_context(self.nc.semaphore("ds_" + name)))
        d.inc = inc
        self.dsems.append(d)
        self.dsem_by_name[name] = d
        return d

    def op(self, eng, fn, reads=(), writes=(), dsem=None):
        ins = Ins(eng, fn)
        ins.dsem = dsem
        if eng in ("act", "dve"):
            locks = []
            for x in list(reads) + list(writes):
                rr = x.res if isinstance(x, T) else x
                if rr.name.startswith("bank"):
                    lk = self.locks.setdefault(rr.name, Res("lock_" + rr.name))
                    if lk not in locks:
                        locks.append(lk)
            writes = list(writes) + locks
        deps = []
        for r in reads:
            r = r.res if isinstance(r, T) else r
            if r.w is not None:
                deps.append(r.w)
        for w in writes:
            w = w.res if isinstance(w, T) else w
            if w.w is not None:
                deps.append(w.w)
            deps.extend(w.rs)
        for r in reads:
            r = r.res if isinstance(r, T) else r
            r.rs.append(ins)
        for w in writes:
            w = w.res if isinstance(w, T) else w
            w.w = ins
            w.rs = []
        seen = set()
        for d in deps:
            if d is ins or id(d) in seen:
                continue
            seen.add(id(d))
            if d.dsem is not None:
                ins.waits.append((d.dsem, d.dsem.issued * d.dsem.inc))
            elif d.eng == eng and eng == "pe":
                continue
            else:
                d.marked = True
                ins.cdeps.append(d)
        if dsem is not None:
            dsem.issued += 1
        else:
            self.last[eng] = ins
        self.q[eng].append(ins)
        return ins

    def barrier(self):
        lasts = [self.last[e] for e in ENGS if self.last[e] is not None]
        for e in ENGS:
            ins = Ins(e, None)
            for d in lasts:
                if d.eng != e:
                    d.marked = True
                    ins.cdeps.append(d)
            for ds in self.dsems:
                if ds.issued:
                    ins.waits.append((ds, ds.issued * ds.inc))
            self.q[e].append(ins)

    def emit(self, block):
        for e in ENGS:
            n = 0
            for ins in self.q[e]:
                if ins.marked:
                    n += 1
                    ins.seq = n
        esem = self.esem

        def run(ename):
            def body(eng):
                waited = {}
                for ins in self.q[ename]:
                    ws = {}
                    for ds, v in ins.waits:
                        k = id(ds.sem)
                        if v > ws.get(k, (None, 0))[1]:
                            ws[k] = (ds.sem, v)
                    for d in ins.cdeps:
                        s = esem[d.eng]
                        k = id(s)
                        if d.seq > ws.get(k, (None, 0))[1]:
                            ws[k] = (s, d.seq)
                    for k, (s, v) in ws.items():
                        if waited.get(k, 0) >= v:
                            continue
                        waited[k] = v
                        eng.wait_ge(s, v)
                    if ins.fn is None:
                        continue
                    bi = ins.fn(eng)
                    if ins.dsem is not None:
                        bi.then_inc(ins.dsem.sem, ins.dsem.inc)
                    elif ins.marked:
                        bi.then_inc(esem[ename], 1)
            return body

        block.tensor(run("pe"))
        block.scalar(run("act"))
        block.vector(run("dve"))
        block.gpsimd(run("pool"))
        block.sync(run("sp"))


class Arena:
    def __init__(self, tensor, nbytes):
        self.t = tensor
        self.n = nbytes
        self.off = 0

    def mark(self):
        return self.off

    def release(self, m):
        self.off = m

    def alloc(self, name, cols, dt=F32, parts=128):
        esz = 4 if dt == F32 else 2
        nb = (cols * esz + 63) // 64 * 64
        assert self.off + nb <= self.n, f"SBUF arena overflow at {name}: {self.off}+{nb}>{self.n}"
        a = self.t[0:parts, self.off // 4:(self.off + nb) // 4]
        self.off += nb
        if dt != F32:
            a = a.bitcast(dt)
        a = a[:, 0:cols]
        return T(a, Res(name))


class K:
    def __init__(self, nc, P, arena, psum):
        self.nc = nc
        self.P = P
        self.A = arena
        self.psum = psum
        self.bank = [T(psum[:, b * 512:(b + 1) * 512], Res(f"bank{b}")) for b in range(8)]

    def dma(self, q, out, in_, dsem, reads=None, writes=None, slow=False):
        rd = [in_] if reads is None else reads
        wr = [out] if writes is None else writes
        if slow:
            return self.P.op(q, lambda e: e.dma_start(out=out.ap, in_=in_.ap, allow_slow_non_contiguous=True),
                             reads=rd, writes=wr, dsem=dsem)
        return self.P.op(q, lambda e: e.dma_start(out=out.ap, in_=in_.ap), reads=rd, writes=wr, dsem=dsem)

    def mm(self, out, lhsT, rhs, start=True, stop=True):
        return self.P.op("pe", lambda e: e.matmul(out.ap, lhsT.ap, rhs.ap, start=start, stop=stop),
                         reads=[lhsT, rhs], writes=[out])

    def tr(self, out, in_, ident):
        return self.P.op("pe", lambda e: e.transpose(out.ap, in_.ap, ident.ap), reads=[in_, ident], writes=[out])

    def act(self, out, in_, func, scale=1.0, bias=0.0, accum=None):
        rd = [in_]
        wr = [out]
        sc = scale
        if isinstance(scale, T):
            rd.append(scale)
            sc = scale.ap
        bi = bias
        if isinstance(bias, T):
            rd.append(bias)
            bi = bias.ap
        if accum is not None:
            wr.append(accum)
            return self.P.op("act", lambda e: e.activation(out=out.ap, in_=in_.ap, func=func, bias=bi, scale=sc,
                                                           accum_out=accum.ap), reads=rd, writes=wr)
        return self.P.op("act", lambda e: e.activation(out=out.ap, in_=in_.ap, func=func, bias=bi, scale=sc),
                         reads=rd, writes=wr)

    def tt(self, out, a, b, op, eng="dve"):
        return self.P.op(eng, lambda e: e.tensor_tensor(out=out.ap, in0=a.ap, in1=b.ap, op=op), reads=[a, b], writes=[out])

    def ts(self, out, a, s1, op0, s2=None, op1=None, eng="dve", accum=None):
        rd = [a]
        v1 = s1
        if isinstance(s1, T):
            rd.append(s1)
            v1 = s1.ap
        v2 = s2
        if isinstance(s2, T):
            rd.append(s2)
            v2 = s2.ap
        wr = [out]
        kw = {}
        if op1 is not None:
            kw["op1"] = op1
        if accum is not None:
            wr.append(accum)
            kw["accum_out"] = accum.ap
        return self.P.op(eng, lambda e: e.tensor_scalar(out=out.ap, in0=a.ap, scalar1=v1, scalar2=v2, op0=op0, **kw),
                         reads=rd, writes=wr)

    def stt(self, out, a, s, b, op0, op1, accum=None):
        rd = [a, b]
        sv = s
        if isinstance(s, T):
            rd.append(s)
            sv = s.ap
        wr = [out]
        kw = {}
        if accum is not None:
            wr.append(accum)
            kw["accum_out"] = accum.ap
        return self.P.op("dve", lambda e: e.scalar_tensor_tensor(out=out.ap, in0=a.ap, scalar=sv, in1=b.ap, op0=op0,
                                                                 op1=op1, **kw), reads=rd, writes=wr)

    def copy(self, out, in_, eng="dve"):
        return self.P.op(eng, lambda e: e.tensor_copy(out=out.ap, in_=in_.ap), reads=[in_], writes=[out])

    def reduce(self, out, in_, op=ALU.add, axis=AX.X):
        return self.P.op("dve", lambda e: e.tensor_reduce(out=out.ap, in_=in_.ap, axis=axis, op=op), reads=[in_], writes=[out])

    def recip(self, out, in_):
        return self.P.op("dve", lambda e: e.reciprocal(out=out.ap, in_=in_.ap), reads=[in_], writes=[out])

    def memset(self, out, val, eng="dve"):
        return self.P.op(eng, lambda e: e.memset(out.ap, val), reads=[], writes=[out])

    def rstd(self, out, ss, invd):
        if isinstance(invd, T):
            self.tt(out, ss, invd, ALU.mult)
            self.ts(out, out, EPS, ALU.add)
        else:
            self.ts(out, ss, invd, ALU.mult, EPS, ALU.add)
        self.act(out, out, AF.Sqrt)
        self.recip(out, out)


class Ctx:
    pass


def norm_to_hT(k, c, xt_j, ss_col, rstd_col, xn, tbank, hT_dst, gT, junk):
    k.act(junk, xt_j, AF.Square)
    k.reduce(ss_col, junk)
    k.rstd(rstd_col, ss_col, 1.0 / D)
    k.ts(xn, xt_j, rstd_col, ALU.mult)
    tb = tbank.bc(BF16)
    for ch in range(8):
        k.tr(tb[:, ch * 128:(ch + 1) * 128], xn[:, ch * 128:(ch + 1) * 128], c.ident_bf)
    k.tt(hT_dst, tb.r("p (c t) -> p c t", c=8), gT.bcast(2, [128, 8, 128]), ALU.mult)


def ffn_phase(k, c, src, src_res, dst, dst_res, w_in_d, w_out_d, g_d, fin_g_d=None):
    A, P = k.A, k.P
    m = A.mark()
    Win = A.alloc("Win", 8 * 2 * DFF, BF16)
    Wout = A.alloc("Wout", NFC * D, BF16)
    gT = A.alloc("gT", 8)
    junk = A.alloc("junk", D)
    fing = A.alloc("fing", D) if fin_g_d is not None else None
    xts = [A.alloc(f"xt{i}", 2 * D) for i in range(2)]
    xns = [A.alloc(f"xn{i}", D, BF16) for i in range(2)]
    hTs = [A.alloc(f"hT{i}", 8 * 256, BF16) for i in range(2)]
    sgs = [A.alloc(f"sg{i}", 256) for i in range(2)]
    aTs = [A.alloc(f"aT{i}", 256, BF16) for i in range(3)]
    sst = A.alloc("sst", 8)
    ds_w = [P.dsem(f"ffw{i}") for i in range(2)]
    ds_x = [P.dsem(f"ffx{i}") for i in range(2)]
    ds_g = P.dsem("ffg")
    Win3 = Win.r("p (c n) -> p c n", c=8)
    Wout3 = Wout.r("p (c n) -> p c n", c=NFC)
    k.dma("sp", gT, T(g_d.rearrange("(c p) -> p c", p=128), c.wres), ds_g, slow=True)
    if fing is not None:
        k.dma("sp", fing, T(fin_g_d.partition_broadcast(128), c.wres), ds_g)
    for ch in range(8):
        k.dma("pool", Win3[:, ch, :], T(w_in_d[ch * 128:(ch + 1) * 128, :], c.wres), ds_w[0])
    k.dma("pool", Wout3, T(w_out_d.rearrange("(c p) n -> p c n", p=128), c.wres), ds_w[1])
    src_t = src.rearrange("(t p) d -> p t d", p=128)
    dst_t = dst.rearrange("(t p) d -> p t d", p=128)
    NG = NT // 2
    gu_banks = [k.bank[0], k.bank[1]]
    tbanks = [k.bank[2], k.bank[3]]
    obanks = [k.bank[4], k.bank[5], k.bank[6], k.bank[7]]
    def ld(g):
        x3 = xts[g % 2].r("p (j d) -> p j d", j=2)
        k.dma("sp", x3, T(src_t[:, 2 * g:2 * g + 2, :], src_res[2 * g]), ds_x[g % 2],
              reads=[src_res[2 * g], src_res[2 * g + 1]])

    for g in range(NG):
        xt = xts[g % 2]
        hT = hTs[g % 2]
        xt3 = xt.r("p (j d) -> p j d", j=2)
        hT3 = hT.r("p (c t) -> p c t", c=8)
        if g == 0:
            ld(0)
        if g + 1 < NG:
            ld(g + 1)
        for j in range(2):
            norm_to_hT(k, c, xt3[:, j, :], sst[:, j:j + 1], sst[:, 2 + j:3 + j], xns[j], tbanks[j],
                       hT3[:, :, j * 128:(j + 1) * 128], gT, junk)

        def gu(fc):
            bk = gu_banks[fc % 2]
            for half in range(2):
                col = half * DFF + fc * 128
                for ch in range(8):
                    k.mm(bk[:, half * 256:(half + 1) * 256], Win3[:, ch, col:col + 128], hT3[:, ch, :],
                         start=(ch == 0), stop=(ch == 7))

        def outmm(fc):
            bk = gu_banks[fc % 2]
            sg = sgs[fc % 2]
            aT = aTs[fc % 3]
            k.act(sg, bk[:, 0:256], AF.Silu)
            k.tt(aT, sg, bk[:, 256:512], ALU.mult)
            for j in range(2):
                for dh in range(2):
                    k.mm(obanks[j * 2 + dh], aT[:, j * 128:(j + 1) * 128], Wout3[:, fc, dh * 512:(dh + 1) * 512],
                         start=(fc == 0), stop=(fc == NFC - 1))

        gu(0)
        for fc in range(NFC):
            if fc + 1 < NFC:
                gu(fc + 1)
            outmm(fc)
        for j in range(2):
            for dh in range(2):
                xs_ = xt3[:, j, dh * 512:(dh + 1) * 512]
                k.stt(xs_, obanks[j * 2 + dh], 0.5, xs_, ALU.mult, ALU.add)
            if fing is not None:
                xj = xt3[:, j, :]
                k.act(junk, xj, AF.Square)
                k.reduce(sst[:, 4 + j:5 + j], junk)
                k.rstd(sst[:, 6 + j:7 + j], sst[:, 4 + j:5 + j], 1.0 / D)
                k.stt(xj, xj, sst[:, 6 + j:7 + j], fing, ALU.mult, ALU.mult)
        k.dma("sp", T(dst_t[:, 2 * g:2 * g + 2, :], dst_res[2 * g]), xt3, ds_x[g % 2],
              writes=[dst_res[2 * g], dst_res[2 * g + 1]])
    P.barrier()
    A.release(m)


def sumsq_heads(k, src, n, hd, junk, ss):
    k.act(junk[:, 0:n * hd], src, AF.Square)
    k.reduce(ss, junk[:, 0:n * hd].r("p (h d) -> p h d", h=n))


def mla_head_path(k, c, f, g_rep, t, outb, junk, tmp, ss8, r8):
    f3 = f.r("p (h d) -> p h d", h=8)
    o3 = outb.r("p (h d) -> p h d", h=8)
    k.tt(junk[:, 0:768], f, f, ALU.mult)
    k.reduce(ss8, junk[:, 0:768].r("p (h d) -> p h d", h=8))
    k.rstd(r8, ss8, 1.0 / 96)
    k.tt(f3, f3, r8.bcast(2, [128, 8, 96]), ALU.mult)
    k.tt(f3, f3, g_rep.bcast(1, [128, 8, 96]), ALU.mult)
    cs = c.cos[:, t * 16:(t + 1) * 16].bcast(1, [128, 8, 16])
    sn = c.sin[:, t * 16:(t + 1) * 16].bcast(1, [128, 8, 16])
    x1 = f3[:, :, 64:80]
    x2 = f3[:, :, 80:96]
    t4 = tmp.r("p (a h d) -> p a h d", a=4, h=8)
    k.tt(t4[:, 0], x1, cs, ALU.mult)
    k.tt(t4[:, 1], x2, sn, ALU.mult)
    k.tt(t4[:, 2], x1, sn, ALU.mult)
    k.tt(t4[:, 3], x2, cs, ALU.mult)
    k.tt(o3[:, :, 64:80], t4[:, 0], t4[:, 1], ALU.subtract)
    k.tt(o3[:, :, 80:96], t4[:, 2], t4[:, 3], ALU.add)
    k.act(o3[:, :, 0:64], f3[:, :, 0:64], AF.Copy)


def phase_a(k, c, l):
    A, P = k.A, k.P
    w = c.w
    m = A.mark()
    Wmi = A.alloc("Wmi", 8 * 2048, BF16)
    Wuq = A.alloc("Wuq", 2 * 768, BF16)
    Wukv = A.alloc("Wukv", 1024, BF16)
    gT = A.alloc("gTmix", 8)
    gqn = A.alloc("gqn", 2)
    gkvn = A.alloc("gkvn", 1)
    gq_rep = A.alloc("gq_rep", 96)
    gk_rep = A.alloc("gk_rep", 96)
    gcol = A.alloc("gcol", 4)
    junk = A.alloc("junkA", D)
    tmp = A.alloc("tmpA", 4 * 8 * 16)
    sst = A.alloc("sstA", 64)
    xts = [A.alloc(f"xtA{i}", D) for i in range(2)]
    xn = A.alloc("xnA", D, BF16)
    hT = A.alloc("hTA", 8 * 128, BF16)
    cqn = A.alloc("cqn", 256, BF16)
    cqT = A.alloc("cqT", 256, BF16)
    ckvn = A.alloc("ckvn", 128, BF16)
    ckvT = A.alloc("ckvT", 128, BF16)
    qf = A.alloc("qf", 768)
    kf = A.alloc("kf", 768)
    qb = A.alloc("qb", 768, BF16)
    kb = A.alloc("kb", 768, BF16)
    qkb = A.alloc("qkb", 14 * 64, BF16)
    QTst = [A.alloc(f"QTst{i}", 8 * 128, BF16) for i in range(2)]
    KTst = [A.alloc(f"KTst{i}", 8 * 128, BF16) for i in range(2)]
    QKst = [A.alloc(f"QKst{i}", 14 * 128, BF16) for i in range(2)]
    Vmst = [A.alloc(f"Vmst{i}", 8 * 66, BF16) for i in range(2)]
    Vsst = [A.alloc(f"Vsst{i}", 2 * 65, BF16) for i in range(2)]
    Vnst = [A.alloc(f"Vnst{i}", 4 * 65, BF16) for i in range(2)]
    ds_w = P.dsem("aw")
    ds_x = [P.dsem(f"ax{i}") for i in range(2)]
    ds_st = [P.dsem(f"ast{i}") for i in range(2)]
    Wmi3 = Wmi.r("p (c n) -> p c n", c=8)
    Wuq3 = Wuq.r("p (c n) -> p c n", c=2)
    wmi = w["w_mix_in"][l].rearrange("(c p) n -> p c n", p=128)
    for (c0, s0, n) in ((0, 0, 416), (512, 416, 512), (1024, 928, 512), (1536, 1440, 256)):
        k.dma("pool", Wmi3[:, :, c0:c0 + n], T(wmi[:, :, s0:s0 + n], c.wres), ds_w)
    k.dma("pool", Wuq3, T(w["mla_w_uq"][l].rearrange("(c p) n -> p c n", p=128), c.wres), ds_w)
    k.dma("pool", Wukv, T(w["mla_w_ukv"][l], c.wres), ds_w)
    k.dma("sp", gT, T(w["mix_norm"][l].rearrange("(c p) -> p c", p=128), c.wres), ds_w, slow=True)
    k.dma("sp", gqn, T(w["mla_q_norm"][l].rearrange("(c p) -> p c", p=128), c.wres), ds_w, slow=True)
    k.dma("sp", gkvn, T(w["mla_kv_norm"][l].rearrange("(c p) -> p c", p=128), c.wres), ds_w, slow=True)
    k.dma("sp", gq_rep, T(w["mla_q_gain"][l].partition_broadcast(128), c.wres), ds_w)
    k.dma("sp", gk_rep, T(w["mla_k_gain"][l].partition_broadcast(128), c.wres), ds_w)
    for i, nm in enumerate(("swa_q_gain", "swa_k_gain", "na_q_gain", "na_k_gain")):
        k.dma("sp", gcol[0:64, i:i + 1], T(w[nm][l].rearrange("(p o) -> p o", o=1), c.wres), ds_w, slow=True)
    for i in range(2):
        for vt, nh, e in ((Vmst[i], 8, 66), (Vsst[i], 2, 65), (Vnst[i], 4, 65)):
            k.ts(vt.r("p (h e) -> p h e", h=nh)[:, :, 64:65], c.ident_f[:, 0:nh].r("p (h o) -> p h o", o=1),
                 0.0, ALU.mult, 1.0, ALU.add)
    xs_t = c.xs.rearrange("(t p) d -> p t d", p=128)
    B = k.bank

    def ld(t):
        k.dma("sp", xts[t % 2], T(xs_t[:, t, :], c.xs_res[t]), ds_x[t % 2])

    for t in range(NT):
        if t == 0:
            ld(0)
        if t + 1 < NT:
            ld(t + 1)
        xt = xts[t % 2]
        i2 = t % 2
        tsl = slice(t * 128, (t + 1) * 128)
        hT3 = hT.r("p (c t) -> p c t", c=8)
        norm_to_hT(k, c, xt, sst[:, 0:1], sst[:, 1:2], xn, B[4], hT3, gT, junk)
        for b, (c0, n) in enumerate(((0, 416), (512, 512), (1024, 512), (1536, 256))):
            for ch in range(8):
                k.mm(B[b][:, 0:n], hT3[:, ch, :], Wmi3[:, ch, c0:c0 + n], start=(ch == 0), stop=(ch == 7))
        if c.stop == "a1":
            continue
        sumsq_heads(k, B[0][:, 0:256], 1, 256, junk, sst[:, 2:3])
        k.rstd(sst[:, 3:4], sst[:, 2:3], 1.0 / 256)
        k.ts(cqn, B[0][:, 0:256], sst[:, 3:4], ALU.mult)
        tb4 = B[4].bc(BF16)
        for c2 in range(2):
            k.tr(tb4[:, c2 * 128:(c2 + 1) * 128], cqn[:, c2 * 128:(c2 + 1) * 128], c.ident_bf)
        k.tt(cqT.r("p (c t) -> p c t", c=2), tb4[:, 0:256].r("p (c t) -> p c t", c=2),
             gqn.bcast(2, [128, 2, 128]), ALU.mult)
        cqT3 = cqT.r("p (c t) -> p c t", c=2)
        for half in range(2):
            for c2 in range(2):
                k.mm(B[6 + half][:, 0:384], cqT3[:, c2, :], Wuq3[:, c2, half * 384:(half + 1) * 384],
                     start=(c2 == 0), stop=(c2 == 1))
        k.act(qf[:, 0:384], B[6][:, 0:384], AF.Copy)
        k.act(qf[:, 384:768], B[7][:, 0:384], AF.Copy)
        mla_head_path(k, c, qf, gq_rep, t, qb, junk, tmp, sst[:, 8:16], sst[:, 16:24])
        tb5 = B[5].bc(BF16)
        for h in range(8):
            k.tr(tb5[0:96, h * 128:(h + 1) * 128], qb[:, h * 96:(h + 1) * 96], c.ident_bf)
        k.copy(QTst[i2][0:96, :], tb5[0:96, :])
        k.dma("sp", T(c.QTm[:, :, tsl].rearrange("h d t -> d h t"), c.QTm_res),
              QTst[i2][0:96, :].r("p (h t) -> p h t", h=8), ds_st[i2])
        if c.stop == "a2":
            continue
        sumsq_heads(k, B[0][:, 256:384], 1, 128, junk, sst[:, 4:5])
        k.rstd(sst[:, 5:6], sst[:, 4:5], 1.0 / 128)
        k.ts(ckvn, B[0][:, 256:384], sst[:, 5:6], ALU.mult)
        if c.stop == "k0":
            continue
        k.tr(tb4[:, 0:128], ckvn, c.ident_bf)
        k.ts(ckvT, tb4[:, 0:128], gkvn[:, 0:1], ALU.mult)
        if c.stop == "k1":
            continue
        for half in range(2):
            k.mm(B[6 + half], ckvT, Wukv[:, half * 512:(half + 1) * 512])
        kf3 = kf.r("p (h d) -> p h d", h=8)
        vm3 = Vmst[i2].r("p (h e) -> p h e", h=8)
        if c.stop == "k2":
            continue
        for half in range(2):
            kv4 = B[6 + half].r("p (h d) -> p h d", h=4)
            if c.stop != "k3d":
                k.act(kf3[:, half * 4:(half + 1) * 4, 0:64], kv4[:, :, 0:64], AF.Copy)
            if c.stop != "k3a":
                k.copy(vm3[:, half * 4:(half + 1) * 4, 0:64], kv4[:, :, 64:128])
        if c.stop in ("k3", "k3a", "k3d"):
            continue
        if c.stop == "k4x":
            k.copy(tmp[:, 0:32], B[0][:, 384:416])
            continue
        if c.stop == "k4y":
            k.copy(kf3[:, :, 64:96], tmp[:, 0:32].bcast(1, [128, 8, 32]))
            continue
        k.copy(kf3[:, :, 64:96], B[0][:, 384:416].bcast(1, [128, 8, 32]))
        if c.stop == "k4":
            continue
        mla_head_path(k, c, kf, gk_rep, t, kb, junk, tmp, sst[:, 24:32], sst[:, 32:40])
        if c.stop == "k5":
            continue
        for h in range(8):
            k.tr(tb5[0:96, h * 128:(h + 1) * 128], kb[:, h * 96:(h + 1) * 96], c.ident_bf)
        k.copy(KTst[i2][0:96, :], tb5[0:96, :])
        if c.stop == "k6":
            continue
        kst3 = KTst[i2][0:96, :].r("p (h t) -> p h t", h=8)
        for j in range(4):
            k.dma("sp", T(c.KTm_l[j][:, :, tsl].rearrange("h d t -> d h t"), c.KTm_l_res),
                  kst3[:, 2 * j:2 * j + 2, :], ds_st[i2])
            k.dma("sp", T(c.Vm_l[j][:, :, t, :].rearrange("h p e -> p h e"), c.Vm_l_res),
                  vm3[:, 2 * j:2 * j + 2, :], ds_st[i2])
        if c.stop == "a3":
            continue
        k.act(junk[:, 0:384], B[1][:, 0:384], AF.Square)
        k.act(junk[:, 384:896], B[2][:, 0:512], AF.Square)
        k.reduce(sst[:, 40:54], junk[:, 0:896].r("p (h d) -> p h d", h=14))
        k.rstd(sst[:, 40:54], sst[:, 40:54], 1.0 / 64)
        qkb3 = qkb.r("p (h d) -> p h d", h=14)
        k.tt(qkb3[:, 0:6, :], B[1][:, 0:384].r("p (h d) -> p h d", h=6), sst[:, 40:46].bcast(2, [128, 6, 64]), ALU.mult)
        k.tt(qkb3[:, 6:14, :], B[2][:, 0:512].r("p (h d) -> p h d", h=8), sst[:, 46:54].bcast(2, [128, 8, 64]), ALU.mult)
        st = QKst[i2]
        for h in range(8):
            k.tr(tb5[0:64, h * 128:(h + 1) * 128], qkb[:, h * 64:(h + 1) * 64], c.ident_bf)
        for h in range(8, 14):
            k.tr(tb4[0:64, (h - 8) * 128:(h - 7) * 128], qkb[:, h * 64:(h + 1) * 64], c.ident_bf)
        k.ts(st[0:64, 0:512], tb5[0:64, 0:512], gcol[0:64, 0:1], ALU.mult)
        k.ts(st[0:64, 512:768], tb5[0:64, 512:768], gcol[0:64, 1:2], ALU.mult)
        k.ts(st[0:64, 768:1024], tb5[0:64, 768:1024], gcol[0:64, 2:3], ALU.mult)
        k.ts(st[0:64, 1024:1280], tb4[0:64, 0:256], gcol[0:64, 2:3], ALU.mult)
        k.ts(st[0:64, 1280:1792], tb4[0:64, 256:768], gcol[0:64, 3:4], ALU.mult)
        st3 = st[0:64, :].r("p (h t) -> p h t", h=14)
        k.dma("sp", T(c.QTs[:, :, tsl].rearrange("h d t -> d h t"), c.QTs_res), st3[:, 0:4, :], ds_st[i2])
        k.dma("sp", T(c.KTs_l[0][:, :, tsl].rearrange("h d t -> d h t"), c.KTs_l_res), st3[:, 4:6, :], ds_st[i2])
        k.dma("sp", T(c.QTn[:, :, tsl].rearrange("h d t -> d h t"), c.QTn_res), st3[:, 6:10, :], ds_st[i2])
        k.dma("sp", T(c.KTn_l[0][:, :, tsl].rearrange("h d t -> d h t"), c.KTn_l_res), st3[:, 10:14, :], ds_st[i2])
        vs3 = Vsst[i2].r("p (h e) -> p h e", h=2)
        vn3 = Vnst[i2].r("p (h e) -> p h e", h=4)
        k.copy(vs3[:, :, 0:64], B[1][:, 384:512].r("p (h d) -> p h d", h=2))
        k.copy(vn3[:, :, 0:64], B[3][:, 0:256].r("p (h d) -> p h d", h=4))
        k.dma("sp", T(c.Vs_l[0][t], c.Vs_l_res), Vsst[i2], ds_st[i2])
        k.dma("sp", T(c.Vn_l[t // (NT // 2)][t % (NT // 2)], c.Vn_l_res), Vnst[i2], ds_st[i2])
    P.barrier()
    A.release(m)
    if c.stop in ("a0", "a1", "a2", "a3") or c.stop.startswith("k"):
        return
    for (cn, t_l, t_g, nm) in c.cc_list:
        ds = P.dsem(f"cc_{cn}_{l}", inc=1)
        P.op("pool", (lambda s_, d_: (lambda e: e.collective_compute(
            "AllGather", ALU.bypass, replica_groups=[[0, 1], [2, 3], [4, 5], [6, 7]][:c.ncores // 2],
            ins=[s_.ap().opt()], outs=[d_.ap().opt()])))(t_l, t_g),
            reads=[getattr(c, nm + "_l_res")], writes=[getattr(c, nm + "_g_res")], dsem=ds)
    P.barrier()


def phase_m(k, c, l):
    A, P = k.A, k.P
    m = A.mark()
    KTs_ = [A.alloc(f"KTm{i}", 2 * TOK, BF16) for i in range(2)]
    VAs = [A.alloc(f"VAm{i}", 2 * NT * 66, BF16) for i in range(2)]
    QTs_ = [A.alloc(f"QTm{i}", TOK, BF16) for i in range(2)]
    PTs = [A.alloc(f"PT{i}", 1024, BF16) for i in range(3)]
    OTs = [A.alloc(f"OT{i}", 512) for i in range(2)]
    ost = [A.alloc(f"ost{i}", 4 * 64) for i in range(2)]
    rc = A.alloc("rcm", 8)
    ds_h = [P.dsem(f"mh{i}") for i in range(2)]
    ds_o = [P.dsem(f"mo{i}") for i in range(2)]
    B = k.bank
    scale = 96 ** -0.5
    om = c.omla.rearrange("(t p) h e -> p t h e", p=128)

    def ld(h):
        i = h % 2
        k.dma("sp", KTs_[i][0:96, :].r("p (r t) -> p r t", r=2), T(c.KTm_g[h // 2][:, h % 2].rearrange("r d t -> d r t"), c.KTm_g_res), ds_h[i])
        k.dma("sp", VAs[i].r("p (r t e) -> p r t e", r=2, t=NT), T(c.Vm_g[h // 2][:, h % 2].rearrange("r p t e -> p r t e"), c.Vm_g_res), ds_h[i])
        k.dma("sp", QTs_[i][0:96, :], T(c.QTm[h], c.QTm_res), ds_h[i])

    cnt = [0]
    pend = []

    def fin_pe(args):
        qg, h, ob, n = args
        OT = OTs[n % 2]
        tb = B[5]
        for j in range(4):
            k.tr(tb[:, j * 65:(j + 1) * 65], OT[0:65, j * 128:(j + 1) * 128], c.ident_f[0:65, 0:65])
        t3 = tb[:, 0:260].r("p (j e) -> p j e", j=4)
        k.recip(rc[:, 0:4], t3[:, :, 64])
        o3 = ost[n % 2].r("p (j e) -> p j e", j=4)
        k.tt(o3, t3[:, :, 0:64], rc[:, 0:4].bcast(2, [128, 4, 64]), ALU.mult)
        k.dma("sp", T(om[:, qg * 4:(qg + 1) * 4, h, :], c.omla_res), o3, ds_o[n % 2])

    ld(0)
    for h in range(MLA_H):
        if h + 1 < MLA_H:
            ld(h + 1)
        KT = KTs_[h % 2]
        VA = VAs[h % 2].r("p (r t e) -> p r t e", r=2, t=NT)
        QT = QTs_[h % 2]
        for qg in range(8):
            n = cnt[0]
            cnt[0] += 1
            ob = B[6 + n % 2]
            for kp in range(32):
                b0 = (kp % 2) * 2
                for j in range(2):
                    kt = 2 * kp + j
                    r, lt = kt // NT, kt % NT
                    k.mm(B[b0 + j], KT[0:96, r * TOK + lt * 128:r * TOK + (lt + 1) * 128], QT[0:96, qg * 512:(qg + 1) * 512])
                PT = PTs[kp % 3]
                src2 = T(k.psum[:, b0 * 512:(b0 + 2) * 512], B[b0].res)
                P.op("act", (lambda o_, i_: (lambda e: e.activation(out=o_.ap, in_=i_.ap, func=AF.Exp, scale=scale)))(PT, src2),
                     reads=[B[b0], B[b0 + 1]], writes=[PT])
                for j in range(2):
                    kt = 2 * kp + j
                    r, lt = kt // NT, kt % NT
                    k.mm(ob[0:65, :], VA[:, r, lt, 0:65], PT[:, j * 512:(j + 1) * 512], start=(kt == 0), stop=(kt == 63))
            k.copy(OTs[n % 2][0:65, :], ob[0:65, :])
            if pend:
                fin_pe(pend.pop())
            pend.append((qg, h, ob, n))
    fin_pe(pend.pop())
    P.barrier()
    A.release(m)


def phase_b(k, c, l):
    A, P = k.A, k.P
    w = c.w
    m = A.mark()
    B = k.bank
    Wmo = A.alloc("Wmo", 8 * D, BF16)
    Wq = A.alloc("Wq", 8 * 256, BF16)
    Wo = A.alloc("Wo", 2 * D, BF16)
    Wkv = A.alloc("Wkv", 8 * 512, BF16)
    gT_grp = A.alloc("gT_grp", 8)
    gT_mx = A.alloc("gT_mx", 8)
    gT_mm = A.alloc("gT_mm", 8)
    gcm = A.alloc("gcm", 2)
    esink = A.alloc("esink", 4)
    KTmem = A.alloc("KTmem", 4 * 256, BF16)
    VAmem = A.alloc("VAmem", 2 * 4 * 65, BF16)
    swat = A.alloc("swat", 5 * 512)
    nagen = A.alloc("nagen", 5 * 512)
    naedge = A.alloc("naedge", 7 * 512)
    junk = A.alloc("junkB", D)
    sst = A.alloc("sstB", 32)
    xn = A.alloc("xnB", D, BF16)
    hT = A.alloc("hTB", 8 * 128, BF16)
    ocb = A.alloc("ocb", D, BF16)
    oT = A.alloc("oTB", 8 * 128, BF16)
    qmb = A.alloc("qmb", 256, BF16)
    QTmem = A.alloc("QTmem", 4 * 128, BF16)
    omb = A.alloc("omb", 256, BF16)
    omf = A.alloc("omf", 256)
    omT = A.alloc("omT", 2 * 128, BF16)
    sbs = [A.alloc(f"sb{i}", 512) for i in range(2)]
    PTs = [A.alloc(f"PTb{i}", 512, BF16) for i in range(7)]
    xts = [A.alloc(f"xtB{i}", D) for i in range(2)]
    ocs = [A.alloc(f"oc{i}", D) for i in range(2)]
    QTs_ = [A.alloc(f"QTsB{i}", 4 * 128, BF16) for i in range(2)]
    KTs_ = [A.alloc(f"KTsB{i}", 2 * 3 * 128, BF16) for i in range(2)]
    Vs_ = [A.alloc(f"VsB{i}", 3 * 130, BF16) for i in range(2)]
    QTn_ = [A.alloc(f"QTnB{i}", 4 * 128, BF16) for i in range(2)]
    KTn_ = [A.alloc(f"KTnB{i}", 4 * 7 * 128, BF16) for i in range(2)]
    Vn_ = [A.alloc(f"VnB{i}", 7 * 260, BF16) for i in range(2)]
    ds_w = P.dsem("bw")
    ds_t = [P.dsem(f"bt{i}") for i in range(2)]
    ds_e = P.dsem("be")
    ds_s = [P.dsem(f"bs{i}") for i in range(2)]
    Wmo3 = Wmo.r("p (c n) -> p c n", c=8)
    Wq3 = Wq.r("p (c n) -> p c n", c=8)
    Wo3 = Wo.r("p (c n) -> p c n", c=2)
    Wkv3 = Wkv.r("p (c n) -> p c n", c=8)
    k.dma("pool", Wmo3, T(w["w_mix_out"][l].rearrange("(c p) n -> p c n", p=128), c.wres), ds_w)
    k.dma("pool", Wq3, T(w["mem_w_q"][l].rearrange("(c p) n -> p c n", p=128), c.wres), ds_w)
    k.dma("pool", Wo3, T(w["mem_w_o"][l].rearrange("(c p) n -> p c n", p=128), c.wres), ds_w)
    k.dma("pool", Wkv3, T(w["mem_w_kv"][l].rearrange("(c p) n -> p c n", p=128), c.wres), ds_w)
    for tl, nm in ((gT_grp, "grp_out_gain"), (gT_mx, "mem_norm_x"), (gT_mm, "mem_norm_m")):
        k.dma("sp", tl, T(w[nm][l].rearrange("(c p) -> p c", p=128), c.wres), ds_w, slow=True)
    k.dma("sp", gcm[0:64, 0:1], T(w["mem_q_gain"][l].rearrange("(p o) -> p o", o=1), c.wres), ds_w, slow=True)
    k.dma("sp", gcm[0:64, 1:2], T(w["mem_k_gain"][l].rearrange("(p o) -> p o", o=1), c.wres), ds_w, slow=True)
    k.dma("sp", esink, T(w["swa_sink"][l].partition_broadcast(128), c.wres), ds_w)
    k.dma("sp", swat.r("p (i n) -> p i n", i=5), T(c.swa_tab.rearrange("i p n -> p i n"), c.wres), ds_w)
    k.dma("sp", nagen.r("p (i n) -> p i n", i=5), T(c.na_gen[l].rearrange("i p n -> p i n"), c.wres), ds_w)
    k.act(esink, esink, AF.Exp)
    k.ts(VAmem.r("p (m e) -> p m e", m=8)[:, :, 64:65], c.ident_f[:, 0:8].r("p (h o) -> p h o", o=1),
         0.0, ALU.mult, 1.0, ALU.add)
    va4 = VAmem.r("p (m h e) -> p m h e", m=2, h=4)
    hT3 = hT.r("p (c t) -> p c t", c=8)
    tb4 = B[4].bc(BF16)
    for mt in range(2):
        k.dma("sp", xts[mt], T(c.mem[mt * 128:(mt + 1) * 128, :], c.wres), ds_t[mt])
        norm_to_hT(k, c, xts[mt], sst[:, 0:1], sst[:, 1:2], xn, B[4], hT3, gT_mm, junk)
        for ch in range(8):
            k.mm(B[5], hT3[:, ch, :], Wkv3[:, ch, :], start=(ch == 0), stop=(ch == 7))
        sumsq_heads(k, B[5][:, 0:256], 4, 64, junk, sst[:, 2:6])
        k.rstd(sst[:, 2:6], sst[:, 2:6], 1.0 / 64)
        k.tt(qmb.r("p (h d) -> p h d", h=4), B[5][:, 0:256].r("p (h d) -> p h d", h=4),
             sst[:, 2:6].bcast(2, [128, 4, 64]), ALU.mult)
        for hd in range(4):
            k.tr(tb4[0:64, hd * 128:(hd + 1) * 128], qmb[:, hd * 64:(hd + 1) * 64], c.ident_bf)
        k.ts(KTmem[0:64, :].r("p (h m) -> p h m", h=4)[:, :, mt * 128:(mt + 1) * 128],
             tb4[0:64, 0:512].r("p (h m) -> p h m", h=4), gcm[0:64, 1:2], ALU.mult)
        k.copy(va4[:, mt, :, 0:64], B[5][:, 256:512].r("p (h d) -> p h d", h=4))
    KTmem3 = KTmem[0:64, :].r("p (h m) -> p h m", h=4)
    xs_t = c.xs.rearrange("(t p) d -> p t d", p=128)
    om_t = c.omla.rearrange("(t p) h e -> p t (h e)", p=128)
    na_edge_t = {0: 0, 1: 1, NT - 2: 2, NT - 1: 3}

    def ktile_src(loc, gat, lt):
        if 0 <= lt < NT:
            return loc[0][:, :, lt * 128:(lt + 1) * 128]
        if lt < 0:
            return gat[0][0][:, :, (NT + lt) * 128:(NT + lt + 1) * 128]
        return gat[0][1][:, :, (lt - NT) * 128:(lt - NT + 1) * 128]

    def vtile_src(loc, gat, lt):
        nch = len(loc)
        per = NT // nch

        def pick(lst, tile, r=None):
            a = lst[tile // per]
            return a[tile % per] if r is None else a[r][tile % per]
        if 0 <= lt < NT:
            return pick(loc, lt)
        if lt < 0:
            return pick(gat, NT + lt, 0)
        return pick(gat, lt - NT, 1)

    def win_n(t):
        return list(range(-2, 3)) if t not in na_edge_t else list(range(-3, 4))

    def ld(t):
        i2 = t % 2
        ds = ds_t[i2]
        tsl = slice(t * 128, (t + 1) * 128)
        k.dma("sp", xts[i2], T(xs_t[:, t, :], c.xs_res[t]), ds)
        k.dma("sp", ocs[i2][:, 0:512], T(om_t[:, t, :], c.omla_res), ds)
        k.dma("sp", QTs_[i2][0:64, :].r("p (h t) -> p h t", h=4), T(c.QTs[:, :, tsl].rearrange("h d t -> d h t"), c.QTs_res), ds)
        k.dma("sp", QTn_[i2][0:64, :].r("p (h t) -> p h t", h=4), T(c.QTn[:, :, tsl].rearrange("h d t -> d h t"), c.QTn_res), ds)
        ks3 = KTs_[i2][0:64, :].r("p (h i t) -> p h i t", h=2, i=3)
        vs3 = Vs_[i2].r("p (i e) -> p i e", i=3)
        for j, i in enumerate((-1, 0, 1)):
            k.dma("sp", ks3[:, :, j, :], T(ktile_src(c.KTs_l, c.KTs_g, t + i).rearrange("h d t -> d h t"), c.KTs_l_res), ds,
                  reads=[c.KTs_l_res, c.KTs_g_res])
            k.dma("sp", vs3[:, j, :], T(vtile_src(c.Vs_l, c.Vs_g, t + i), c.Vs_l_res), ds, reads=[c.Vs_l_res, c.Vs_g_res])
        kn3 = KTn_[i2][0:64, :].r("p (h i t) -> p h i t", h=4, i=7)
        vn3 = Vn_[i2].r("p (i e) -> p i e", i=7)
        for j, i in enumerate(win_n(t)):
            k.dma("sp", kn3[:, :, j, :], T(ktile_src(c.KTn_l, c.KTn_g, t + i).rearrange("h d t -> d h t"), c.KTn_l_res), ds,
                  reads=[c.KTn_l_res, c.KTn_g_res])
            k.dma("sp", vn3[:, j, :], T(vtile_src(c.Vn_l, c.Vn_g, t + i), c.Vn_l_res), ds, reads=[c.Vn_l_res, c.Vn_g_res])

    sbc = [0]

    def attend(Q3, Ksrc, Vsrc, tabs, nk, hkv, scale, Ob):
        for j in range(nk):
            n = sbc[0]
            sbc[0] += 1
            Sb = B[n % 2]
            Kt = Ksrc(j)
            for hd in range(4):
                k.mm(Sb[:, hd * 128:(hd + 1) * 128], Kt[:, hd * hkv // 4, :], Q3[:, hd, :])
            PT = PTs[j]
            tab = tabs(j)
            if tab is not None:
                sb = sbs[n % 2]
                k.stt(sb, Sb, scale, tab, ALU.mult, ALU.add)
                k.act(PT, sb, AF.Exp)
            else:
                k.act(PT, Sb, AF.Exp, scale=scale)
        for hd in range(4):
            for j in range(nk):
                Vt = Vsrc(j)
                k.mm(Ob[:, hd * 65:(hd + 1) * 65], PTs[j][:, hd * 128:(hd + 1) * 128], Vt[:, hd * hkv // 4, :],
                     start=(j == 0), stop=(j == nk - 1))

    def finish(Ob, dst, den_add, rcol):
        o3 = Ob[:, 0:260].r("p (h e) -> p h e", h=4)
        if den_add is not None:
            k.tt(rcol, o3[:, :, 64], den_add, ALU.add)
            k.recip(rcol, rcol)
        else:
            k.recip(rcol, o3[:, :, 64])
        k.tt(dst.r("p (h d) -> p h d", h=4), o3[:, :, 0:64], rcol.bcast(2, [128, 4, 64]), ALU.mult)

    ld(0)
    for t in range(NT):
        if t + 1 < NT:
            ld(t + 1)
        i2 = t % 2
        xt = xts[i2]
        oc = ocs[i2]
        Qs3 = QTs_[i2][0:64, :].r("p (h t) -> p h t", h=4)
        ks3 = KTs_[i2][0:64, :].r("p (h i t) -> p h i t", h=2, i=3)
        vs4 = Vs_[i2].r("p (i h e) -> p i h e", i=3, h=2)
        sw3 = swat.r("p (i n) -> p i n", i=5)

        def swa_tab(j, t=t):
            if j == 0 and t == 0:
                return sw3[:, 3, :]
            if j == 2 and t == NT - 1:
                return sw3[:, 4, :]
            return sw3[:, j, :]

        attend(Qs3, lambda j: ks3[:, :, j, :], lambda j: vs4[:, j, :, :], swa_tab, 3, 2, 0.125, B[2])
        finish(B[2], oc[:, 512:768], esink, sst[:, 8:12])
        Qn3 = QTn_[i2][0:64, :].r("p (h t) -> p h t", h=4)
        kn3 = KTn_[i2][0:64, :].r("p (h i t) -> p h i t", h=4, i=7)
        vn4 = Vn_[i2].r("p (i h e) -> p i h e", i=7, h=4)
        if t in na_edge_t:
            ne3 = naedge.r("p (i n) -> p i n", i=7)
            k.dma("sp", ne3, T(c.na_edge[l, na_edge_t[t]].rearrange("i p n -> p i n"), c.wres), ds_e)
            nk = 7
            ntab = lambda j: ne3[:, j, :]
        else:
            ng3 = nagen.r("p (i n) -> p i n", i=5)
            nk = 5
            ntab = lambda j: ng3[:, j, :]
        attend(Qn3, lambda j: kn3[:, :, j, :], lambda j: vn4[:, j, :, :], ntab, nk, 4, 0.125, B[3])
        finish(B[3], oc[:, 768:1024], None, sst[:, 12:16])
        if c.debug:
            k.dma("sp", T(c.ocat_dbg.rearrange("(t p) d -> p t d", p=128)[:, t, :], Res("dbg")), oc, ds_s[i2])
        k.act(junk, oc, AF.Square)
        k.reduce(sst[:, 16:17], junk[:, 0:512])
        k.reduce(sst[:, 17:19], junk[:, 512:1024].r("p (g d) -> p g d", g=2))
        k.rstd(sst[:, 20:21], sst[:, 16:17], 1.0 / 512)
        k.rstd(sst[:, 21:23], sst[:, 17:19], 1.0 / 256)
        k.ts(ocb[:, 0:512], oc[:, 0:512], sst[:, 20:21], ALU.mult)
        k.ts(ocb[:, 512:768], oc[:, 512:768], sst[:, 21:22], ALU.mult)
        k.ts(ocb[:, 768:1024], oc[:, 768:1024], sst[:, 22:23], ALU.mult)
        for ch in range(8):
            k.tr(tb4[:, ch * 128:(ch + 1) * 128], ocb[:, ch * 128:(ch + 1) * 128], c.ident_bf)
        oT3 = oT.r("p (c t) -> p c t", c=8)
        k.tt(oT3, tb4.r("p (c t) -> p c t", c=8), gT_grp.bcast(2, [128, 8, 128]), ALU.mult)
        for dh in range(2):
            for ch in range(8):
                k.mm(B[6 + dh], oT3[:, ch, :], Wmo3[:, ch, dh * 512:(dh + 1) * 512], start=(ch == 0), stop=(ch == 7))
        for dh in range(2):
            xs_ = xt[:, dh * 512:(dh + 1) * 512]
            k.tt(xs_, xs_, B[6 + dh], ALU.add)
        norm_to_hT(k, c, xt, sst[:, 0:1], sst[:, 1:2], xn, B[4], hT3, gT_mx, junk)
        for ch in range(8):
            k.mm(B[5][:, 0:256], hT3[:, ch, :], Wq3[:, ch, :], start=(ch == 0), stop=(ch == 7))
        sumsq_heads(k, B[5][:, 0:256], 4, 64, junk, sst[:, 2:6])
        k.rstd(sst[:, 2:6], sst[:, 2:6], 1.0 / 64)
        k.tt(qmb.r("p (h d) -> p h d", h=4), B[5][:, 0:256].r("p (h d) -> p h d", h=4),
             sst[:, 2:6].bcast(2, [128, 4, 64]), ALU.mult)
        for hd in range(4):
            k.tr(tb4[0:64, hd * 128:(hd + 1) * 128], qmb[:, hd * 64:(hd + 1) * 64], c.ident_bf)
        k.ts(QTmem[0:64, :], tb4[0:64, 0:512], gcm[0:64, 0:1], ALU.mult)
        Qm3 = QTmem[0:64, :].r("p (h t) -> p h t", h=4)
        attend(Qm3, lambda j: KTmem3[:, :, j * 128:(j + 1) * 128], lambda j: va4[:, j, :, :], lambda j: None, 2, 4, 0.125, B[2])
        finish(B[2], omf, None, sst[:, 24:28])
        k.act(omb, omf, AF.Copy)
        for c2 in range(2):
            k.tr(tb4[:, c2 * 128:(c2 + 1) * 128], omb[:, c2 * 128:(c2 + 1) * 128], c.ident_bf)
        k.copy(omT, tb4[:, 0:256])
        omT3 = omT.r("p (c t) -> p c t", c=2)
        for dh in range(2):
            for c2 in range(2):
                k.mm(B[6 + dh], omT3[:, c2, :], Wo3[:, c2, dh * 512:(dh + 1) * 512], start=(c2 == 0), stop=(c2 == 1))
        for dh in range(2):
            xs_ = xt[:, dh * 512:(dh + 1) * 512]
            k.tt(xs_, xs_, B[6 + dh], ALU.add)
        k.dma("sp", T(xs_t[:, t, :], c.xs_res[t]), xt, ds_s[i2])
    P.barrier()
    A.release(m)


WNAMES = ["ffn1_norm", "ffn1_w_in", "ffn1_w_out", "mix_norm", "w_mix_in", "mla_q_norm", "mla_w_uq", "mla_kv_norm",
          "mla_w_ukv", "mla_q_gain", "mla_k_gain", "swa_q_gain", "swa_k_gain", "swa_sink", "na_q_gain", "na_k_gain",
          "grp_out_gain", "w_mix_out", "mem_norm_x", "mem_norm_m", "mem_w_q", "mem_w_kv", "mem_q_gain",
          "mem_k_gain", "mem_w_o", "ffn2_norm", "ffn2_w_in", "ffn2_w_out", "block_norm"]
WSHAPES = {
    "ffn1_norm": [DEPTH, D], "ffn1_w_in": [DEPTH, D, 2 * DFF], "ffn1_w_out": [DEPTH, DFF, D], "mix_norm": [DEPTH, D],
    "w_mix_in": [DEPTH, D, MIX_IN], "mla_q_norm": [DEPTH, 256], "mla_w_uq": [DEPTH, 256, 768],
    "mla_kv_norm": [DEPTH, 128], "mla_w_ukv": [DEPTH, 128, 1024], "mla_q_gain": [DEPTH, 96], "mla_k_gain": [DEPTH, 96],
    "swa_q_gain": [DEPTH, 64], "swa_k_gain": [DEPTH, 64], "swa_sink": [DEPTH, 4], "na_q_gain": [DEPTH, 64],
    "na_k_gain": [DEPTH, 64], "grp_out_gain": [DEPTH, D], "w_mix_out": [DEPTH, D, D], "mem_norm_x": [DEPTH, D],
    "mem_norm_m": [DEPTH, D], "mem_w_q": [DEPTH, D, 256], "mem_w_kv": [DEPTH, D, 512], "mem_q_gain": [DEPTH, 64],
    "mem_k_gain": [DEPTH, 64], "mem_w_o": [DEPTH, 256, D], "ffn2_norm": [DEPTH, D], "ffn2_w_in": [DEPTH, D, 2 * DFF],
    "ffn2_w_out": [DEPTH, DFF, D], "block_norm": [DEPTH, D],
}
ARENA_BYTES = 204800


def build(stop="full", nlayers=DEPTH, ncores=8, debug=False):
    nc = bass.Bass("TRN2", target_bir_lowering=False)
    c = Ctx()
    c.ncores = ncores
    c.debug = debug
    dk = {"kind": "ExternalOutput"} if debug else {}
    c.stop = stop
    c.x_in = nc.dram_tensor("x", [TOK, D], F32, kind="ExternalInput").ap()
    c.mem = nc.dram_tensor("mem", [NMEM, D], F32, kind="ExternalInput").ap()
    c.ident_d = nc.dram_tensor("ident", [128, 128], F32, kind="ExternalInput").ap()
    c.cos_d = nc.dram_tensor("cos", [TOK, 16], F32, kind="ExternalInput").ap()
    c.sin_d = nc.dram_tensor("sin", [TOK, 16], F32, kind="ExternalInput").ap()
    c.swa_tab = nc.dram_tensor("swa_tab", [5, 128, 512], F32, kind="ExternalInput").ap()
    c.na_gen = nc.dram_tensor("na_gen", [DEPTH, 5, 128, 512], F32, kind="ExternalInput").ap()
    c.na_edge = nc.dram_tensor("na_edge", [DEPTH, 4, 7, 128, 512], F32, kind="ExternalInput").ap()
    c.w = {n: nc.dram_tensor(n, WSHAPES[n], F32, kind="ExternalInput").ap() for n in WNAMES}
    c.y = nc.dram_tensor("y", [TOK, D], F32, kind="ExternalOutput").ap()
    c.xs = nc.dram_tensor("xs", [TOK, D], F32, **dk).ap()
    c.omla = nc.dram_tensor("omla", [TOK, 8, 64], F32, **dk).ap()
    if debug:
        c.ocat_dbg = nc.dram_tensor("ocat_dbg", [TOK, D], F32, **dk).ap()
    c.omla_res = Res("omla")

    c.cc_list = []

    def scratch(name, nchunk, rows, cols, pat_l, pat_g, **kw):
        ls, gs = [], []
        for j in range(nchunk):
            t_l = nc.dram_tensor(f"{name}_l{j}", [rows, cols], BF16)
            t_g = nc.dram_tensor(f"{name}_g{j}", [2 * rows, cols], BF16)
            ls.append(t_l.ap().rearrange(pat_l, **kw))
            gs.append(t_g.ap().rearrange(pat_g, r=2, **kw))
            c.cc_list.append((f"{name}{j}", t_l, t_g, name))
        setattr(c, name + "_l", ls)
        setattr(c, name + "_g", gs)
        setattr(c, name + "_l_res", Res(name + "_l"))
        setattr(c, name + "_g_res", Res(name + "_g"))

    scratch("KTm", 4, 2 * 96, TOK, "(h d) t -> h d t", "(r h d) t -> r h d t", h=2)
    scratch("Vm", 4, 2 * 128, NT * 66, "(h p) (t e) -> h p t e", "(r h p) (t e) -> r h p t e", h=2, t=NT)
    scratch("KTs", 1, 2 * 64, TOK, "(h d) t -> h d t", "(r h d) t -> r h d t", h=2)
    scratch("Vs", 1, NT * 128, 130, "(t p) e -> t p e", "(r t p) e -> r t p e", t=NT)
    scratch("KTn", 1, 4 * 64, TOK, "(h d) t -> h d t", "(r h d) t -> r h d t", h=4)
    scratch("Vn", 2, (NT // 2) * 128, 260, "(t p) e -> t p e", "(r t p) e -> r t p e", t=NT // 2)
    for nm, shp in (("QTm", [8, 96, TOK]), ("QTs", [4, 64, TOK]), ("QTn", [4, 64, TOK])):
        setattr(c, nm, nc.dram_tensor(nm, shp, BF16, **dk).ap())
        setattr(c, nm + "_res", Res(nm))
    c.wres = Res("weights")
    c.xin_res = [Res(f"xin{t}") for t in range(NT)]
    c.xs_res = [Res(f"xs{t}") for t in range(NT)]
    c.y_res = [Res(f"y{t}") for t in range(NT)]
    with contextlib.ExitStack() as stack:
        arena_t = stack.enter_context(nc.sbuf_tensor("arena", [128, ARENA_BYTES // 4], F32))
        psum = stack.enter_context(nc.psum_tensor("ps", [128, 4096], F32))
        P = Prog(nc, stack)
        A = Arena(arena_t, ARENA_BYTES)
        k = K(nc, P, A, psum)
        ds_c = P.dsem("const")
        c.ident_f = A.alloc("ident_f", 128)
        c.ident_bf = A.alloc("ident_bf", 128, BF16)
        c.cos = A.alloc("cos", NT * 16)
        c.sin = A.alloc("sin", NT * 16)
        k.dma("sp", c.ident_f, T(c.ident_d, c.wres), ds_c)
        k.dma("pool", c.ident_bf, T(c.ident_d, c.wres), ds_c)
        k.dma("sp", c.cos.r("p (t d) -> p t d", t=NT), T(c.cos_d.rearrange("(t p) d -> p t d", p=128), c.wres), ds_c)
        k.dma("sp", c.sin.r("p (t d) -> p t d", t=NT), T(c.sin_d.rearrange("(t p) d -> p t d", p=128), c.wres), ds_c)
        w = c.w
        if stop == "ffn1":
            ffn_phase(k, c, c.x_in, c.xin_res, c.y, c.y_res, w["ffn1_w_in"][0], w["ffn1_w_out"][0], w["ffn1_norm"][0])
        else:
            for l in range(nlayers):
                last = (l == nlayers - 1)
                src, src_res = (c.x_in, c.xin_res) if l == 0 else (c.xs, c.xs_res)
                ffn_phase(k, c, src, src_res, c.xs, c.xs_res, w["ffn1_w_in"][l], w["ffn1_w_out"][l], w["ffn1_norm"][l])
                phase_a(k, c, l)
                if stop in ("a", "a0", "a1", "a2", "a3") or stop.startswith("k"):
                    break
                phase_m(k, c, l)
                if stop == "m":
                    break
                phase_b(k, c, l)
                if stop == "b":
                    break
                dst, dst_res = (c.y, c.y_res) if last else (c.xs, c.xs_res)
                ffn_phase(k, c, c.xs, c.xs_res, dst, dst_res, w["ffn2_w_in"][l], w["ffn2_w_out"][l], w["ffn2_norm"][l],
                          fin_g_d=w["block_norm"][l])
        P.barrier()
        block = stack.enter_context(nc.Block())
        P.emit(block)
    return nc


def _swa_table(gq, gk):
    slopes = (2.0 ** (-8.0 * np.arange(1, 5, dtype=np.float32) / 4)).astype(np.float32)
    q = np.arange(128)[None, :]
    kk = np.arange(128)[:, None]
    dist = np.abs((gq * 128 + q) - (gk * 128 + kk))
    valid = (dist <= 128) & (0 <= gk < SEQ // 128)
    val = -slopes[None, :, None] * dist[:, None, :].astype(np.float32)
    out = np.where(valid[:, None, :], val, np.float32(NEG)).astype(np.float32)
    return out.reshape(128, 512)


def _na_table(rb, gq, gk):
    rows = SEQ // GW
    q = np.arange(128)[None, :]
    kk = np.arange(128)[:, None]
    qrow, qcol = 2 * gq + q // 64, q % 64
    krow, kcol = 2 * gk + kk // 64, kk % 64
    r0 = np.clip(qrow - 4, 0, rows - 8)
    c0 = np.clip(qcol - 8, 0, GW - 16)
    valid = (krow >= r0) & (krow < r0 + 8) & (kcol >= c0) & (kcol < c0 + 16) & (0 <= gk < SEQ // 128)
    dr = np.clip(krow - qrow + 7, 0, 14)
    dc = np.clip(kcol - qcol + 15, 0, 30)
    g = rb[:, dr, dc]
    out = np.where(valid[None], g, np.float32(NEG)).astype(np.float32)
    return np.ascontiguousarray(out.transpose(1, 0, 2)).reshape(128, 512)


def make_tables(inputs, h):
    rb = np.asarray(inputs["na_rel_bias"], dtype=np.float32)
    gmid = 32 * h + 10
    swa = np.stack([_swa_table(gmid, gmid - 1), _swa_table(gmid, gmid), _swa_table(gmid, gmid + 1),
                    _swa_table(32 * h, 32 * h - 1), _swa_table(32 * h + 31, 32 * h + 32)])
    na_gen = np.stack([np.stack([_na_table(rb[l], gmid, gmid + i) for i in range(-2, 3)]) for l in range(DEPTH)])
    na_edge = np.stack([np.stack([np.stack([_na_table(rb[l], 32 * h + t, 32 * h + t + i) for i in range(-3, 4)])
                                  for t in (0, 1, NT - 2, NT - 1)]) for l in range(DEPTH)])
    pos = (h * TOK + np.arange(TOK, dtype=np.float32)).astype(np.float32)
    inv = (1.0 / (np.float32(10000.0) ** (np.arange(0, 32, 2, dtype=np.float32) / np.float32(32)))).astype(np.float32)
    ang = (pos[:, None] * inv[None, :]).astype(np.float32)
    return {"swa_tab": swa, "na_gen": na_gen, "na_edge": na_edge,
            "cos": np.cos(ang).astype(np.float32), "sin": np.sin(ang).astype(np.float32)}


def make_in_maps(inputs):
    x = np.ascontiguousarray(np.asarray(inputs["x"], dtype=np.float32))
    mem = np.ascontiguousarray(np.asarray(inputs["mem"], dtype=np.float32))
    ident = np.eye(128, dtype=np.float32)
    ws = {n: np.ascontiguousarray(np.asarray(inputs[n], dtype=np.float32)) for n in WNAMES}
    tabs = [make_tables(inputs, 0), make_tables(inputs, 1)]
    maps = []
    for core in range(8):
        b, h = core // 2, core % 2
        m = {"x": x[b, h * TOK:(h + 1) * TOK], "mem": mem[b], "ident": ident}
        m.update(tabs[h])
        m.update(ws)
        maps.append(m)
    return maps


def kernel(**inputs):
    nc = build("full")
    res = run_bass_kernel_spmd(nc, make_in_maps(inputs), core_ids=list(range(8)))
    out = np.empty((BATCH, SEQ, D), np.float32)
    for core in range(8):
        b, h = core // 2, core % 2
        out[b, h * TOK:(h + 1) * TOK] = np.asarray(res.results[core]["y"])
    return out
```

```python
import contextlib
import numpy as np
import concourse.bass as bass
import concourse.mybir as mybir
from concourse.bass_utils import run_bass_kernel_spmd

F32 = mybir.dt.float32
BF16 = mybir.dt.bfloat16
AF = mybir.ActivationFunctionType
ALU = mybir.AluOpType
AX = mybir.AxisListType

D = 1024
BATCH = 4
SEQ = 8192
DEPTH = 2
NMEM = 256
GW = 64
EPS = 1e-6
DFF = 2816
NFC = DFF // 128
TOK = SEQ // 2
NT = TOK // 128
NEG = -30000.0

MLA_H = 8
MLA_QL = 256
MLA_KVL = 128
MLA_NOPE = 64
MLA_ROPE = 32
MLA_QK = 96
MLA_V = 64
MIX_IN = 1696


class Res:
    __slots__ = ("name", "w", "rs")

    def __init__(self, name):
        self.name = name
        self.w = None
        self.rs = []


class DSem:
    __slots__ = ("sem", "issued", "inc")

    def __init__(self, sem):
        self.sem = sem
        self.issued = 0
        self.inc = 16


class Ins:
    __slots__ = ("eng", "fn", "waits", "cdeps", "seq", "marked", "dsem")

    def __init__(self, eng, fn):
        self.eng = eng
        self.fn = fn
        self.waits = []
        self.cdeps = []
        self.seq = 0
        self.marked = False
        self.dsem = None


class T:
    __slots__ = ("ap", "res")

    def __init__(self, ap, res):
        self.ap = ap
        self.res = res

    def __getitem__(self, k):
        return T(self.ap[k], self.res)

    def r(self, pat, **kw):
        return T(self.ap.rearrange(pat, **kw), self.res)

    def bc(self, dt):
        return T(self.ap.bitcast(dt), self.res)

    def bcast(self, axis, shape):
        return T(self.ap.unsqueeze(axis).to_broadcast(shape), self.res)


ENGS = ("pe", "act", "dve", "pool", "sp")


class Prog:
    def __init__(self, nc, stack):
        self.nc = nc
        self.stack = stack
        self.q = {e: [] for e in ENGS}
        self.esem = {e: stack.enter_context(nc.semaphore("es_" + e)) for e in ENGS}
        self.dsems = []
        self.dsem_by_name = {}
        self.locks = {}
        self.last = {e: None for e in ENGS}

    def dsem(self, name, inc=16):
        if name in self.dsem_by_name:
            return self.dsem_by_name[name]
        d = DSem(self.stack.enter_context(self.nc.semaphore("ds_" + name)))
        d.inc = inc
        self.dsems.append(d)
        self.dsem_by_name[name] = d
        return d

    def op(self, eng, fn, reads=(), writes=(), dsem=None):
        ins = Ins(eng, fn)
        ins.dsem = dsem
        if eng in ("act", "dve"):
            locks = []
            for x in list(reads) + list(writes):
                rr = x.res if isinstance(x, T) else x
                if rr.name.startswith("bank"):
                    lk = self.locks.setdefault(rr.name, Res("lock_" + rr.name))
                    if lk not in locks:
                        locks.append(lk)
            writes = list(writes) + locks
        deps = []
        for r in reads:
            r = r.res if isinstance(r, T) else r
            if r.w is not None:
                deps.append(r.w)
        for w in writes:
            w = w.res if isinstance(w, T) else w
            if w.w is not None:
                deps.append(w.w)
            deps.extend(w.rs)
        for r in reads:
            r = r.res if isinstance(r, T) else r
            r.rs.append(ins)
        for w in writes:
            w = w.res if isinstance(w, T) else w
            w.w = ins
            w.rs = []
        seen = set()
        for d in deps:
            if d is ins or id(d) in seen:
                continue
            seen.add(id(d))
            if d.dsem is not None:
                ins.waits.append((d.dsem, d.dsem.issued * d.dsem.inc))
            elif d.eng == eng and eng == "pe":
                continue
            else:
                d.marked = True
                ins.cdeps.append(d)
        if dsem is not None:
            dsem.issued += 1
        else:
            self.last[eng] = ins
        self.q[eng].append(ins)
        return ins

    def barrier(self):
        lasts = [self.last[e] for e in ENGS if self.last[e] is not None]
        for e in ENGS:
            ins = Ins(e, None)
            for d in lasts:
                if d.eng != e:
                    d.marked = True
                    ins.cdeps.append(d)
            for ds in self.dsems:
                if ds.issued:
                    ins.waits.append((ds, ds.issued * ds.inc))
            self.q[e].append(ins)

    def emit(self, block):
        for e in ENGS:
            n = 0
            for ins in self.q[e]:
                if ins.marked:
                    n += 1
                    ins.seq = n
        esem = self.esem

        def run(ename):
            def body(eng):
                waited = {}
                for ins in self.q[ename]:
                    ws = {}
                    for ds, v in ins.waits:
                        k = id(ds.sem)
                        if v > ws.get(k, (None, 0))[1]:
                            ws[k] = (ds.sem, v)
                    for d in ins.cdeps:
                        s = esem[d.eng]
                        k = id(s)
                        if d.seq > ws.get(k, (None, 0))[1]:
                            ws[k] = (s, d.seq)
                    for k, (s, v) in ws.items():
                        if waited.get(k, 0) >= v:
                            continue
                        waited[k] = v
                        eng.wait_ge(s, v)
                    if ins.fn is None:
                        continue
                    bi = ins.fn(eng)
                    if ins.dsem is not None:
                        bi.then_inc(ins.dsem.sem, ins.dsem.inc)
                    elif ins.marked:
                        bi.then_inc(esem[ename], 1)
            return body

        block.tensor(run("pe"))
        block.scalar(run("act"))
        block.vector(run("dve"))
        block.gpsimd(run("pool"))
        block.sync(run("sp"))


class Arena:
    def __init__(self, tensor, nbytes):
        self.t = tensor
        self.n = nbytes
        self.off = 0

    def mark(self):
        return self.off

    def release(self, m):
        self.off = m

    def alloc(self, name, cols, dt=F32, parts=128):
        esz = 4 if dt == F32 else 2
        nb = (cols * esz + 63) // 64 * 64
        assert self.off + nb <= self.n, f"SBUF arena overflow at {name}: {self.off}+{nb}>{self.n}"
        a = self.t[0:parts, self.off // 4:(self.off + nb) // 4]
        self.off += nb
        if dt != F32:
            a = a.bitcast(dt)
        a = a[:, 0:cols]
        return T(a, Res(name))


class K:
    def __init__(self, nc, P, arena, psum):
        self.nc = nc
        self.P = P
        self.A = arena
        self.psum = psum
        self.bank = [T(psum[:, b * 512:(b + 1) * 512], Res(f"bank{b}")) for b in range(8)]

    def dma(self, q, out, in_, dsem, reads=None, writes=None, slow=False):
        rd = [in_] if reads is None else reads
        wr = [out] if writes is None else writes
        if slow:
            return self.P.op(q, lambda e: e.dma_start(out=out.ap, in_=in_.ap, allow_slow_non_contiguous=True),
                             reads=rd, writes=wr, dsem=dsem)
        return self.P.op(q, lambda e: e.dma_start(out=out.ap, in_=in_.ap), reads=rd, writes=wr, dsem=dsem)

    def mm(self, out, lhsT, rhs, start=True, stop=True):
        return self.P.op("pe", lambda e: e.matmul(out.ap, lhsT.ap, rhs.ap, start=start, stop=stop),
                         reads=[lhsT, rhs], writes=[out])

    def tr(self, out, in_, ident):
        return self.P.op("pe", lambda e: e.transpose(out.ap, in_.ap, ident.ap), reads=[in_, ident], writes=[out])

    def act(self, out, in_, func, scale=1.0, bias=0.0, accum=None):
        rd = [in_]
        wr = [out]
        sc = scale
        if isinstance(scale, T):
            rd.append(scale)
            sc = scale.ap
        bi = bias
        if isinstance(bias, T):
            rd.append(bias)
            bi = bias.ap
        if accum is not None:
            wr.append(accum)
            return self.P.op("act", lambda e: e.activation(out=out.ap, in_=in_.ap, func=func, bias=bi, scale=sc,
                                                           accum_out=accum.ap), reads=rd, writes=wr)
        return self.P.op("act", lambda e: e.activation(out=out.ap, in_=in_.ap, func=func, bias=bi, scale=sc),
                         reads=rd, writes=wr)

    def tt(self, out, a, b, op, eng="dve"):
        return self.P.op(eng, lambda e: e.tensor_tensor(out=out.ap, in0=a.ap, in1=b.ap, op=op), reads=[a, b], writes=[out])

    def ts(self, out, a, s1, op0, s2=None, op1=None, eng="dve", accum=None):
        rd = [a]
        v1 = s1
        if isinstance(s1, T):
            rd.append(s1)
            v1 = s1.ap
        v2 = s2
        if isinstance(s2, T):
            rd.append(s2)
            v2 = s2.ap
        wr = [out]
        kw = {}
        if op1 is not None:
            kw["op1"] = op1
        if accum is not None:
            wr.append(accum)
            kw["accum_out"] = accum.ap
        return self.P.op(eng, lambda e: e.tensor_scalar(out=out.ap, in0=a.ap, scalar1=v1, scalar2=v2, op0=op0, **kw),
                         reads=rd, writes=wr)

    def stt(self, out, a, s, b, op0, op1, accum=None):
        rd = [a, b]
        sv = s
        if isinstance(s, T):
            rd.append(s)
            sv = s.ap
        wr = [out]
        kw = {}
        if accum is not None:
            wr.append(accum)
            kw["accum_out"] = accum.ap
        return self.P.op("dve", lambda e: e.scalar_tensor_tensor(out=out.ap, in0=a.ap, scalar=sv, in1=b.ap, op0=op0,
                                                                 op1=op1, **kw), reads=rd, writes=wr)

    def copy(self, out, in_, eng="dve"):
        return self.P.op(eng, lambda e: e.tensor_copy(out=out.ap, in_=in_.ap), reads=[in_], writes=[out])

    def reduce(self, out, in_, op=ALU.add, axis=AX.X):
        return self.P.op("dve", lambda e: e.tensor_reduce(out=out.ap, in_=in_.ap, axis=axis, op=op), reads=[in_], writes=[out])

    def recip(self, out, in_):
        return self.P.op("dve", lambda e: e.reciprocal(out=out.ap, in_=in_.ap), reads=[in_], writes=[out])

    def memset(self, out, val, eng="dve"):
        return self.P.op(eng, lambda e: e.memset(out.ap, val), reads=[], writes=[out])

    def rstd(self, out, ss, invd):
        if isinstance(invd, T):
            self.tt(out, ss, invd, ALU.mult)
            self.ts(out, out, EPS, ALU.add)
        else:
            self.ts(out, ss, invd, ALU.mult, EPS, ALU.add)
        self.act(out, out, AF.Sqrt)
        self.recip(out, out)


class Ctx:
    pass


def norm_to_hT(k, c, xt_j, ss_col, rstd_col, xn, tbank, hT_dst, gT, junk):
    k.act(junk, xt_j, AF.Square)
    k.reduce(ss_col, junk)
    k.rstd(rstd_col, ss_col, 1.0 / D)
    k.ts(xn, xt_j, rstd_col, ALU.mult)
    tb = tbank.bc(BF16)
    for ch in range(8):
        k.tr(tb[:, ch * 128:(ch + 1) * 128], xn[:, ch * 128:(ch + 1) * 128], c.ident_bf)
    k.tt(hT_dst, tb.r("p (c t) -> p c t", c=8), gT.bcast(2, [128, 8, 128]), ALU.mult)


def ffn_phase(k, c, src, src_res, dst, dst_res, w_in_d, w_out_d, g_d, fin_g_d=None):
    A, P = k.A, k.P
    m = A.mark()
    Win = A.alloc("Win", 8 * 2 * DFF, BF16)
    Wout = A.alloc("Wout", NFC * D, BF16)
    gT = A.alloc("gT", 8)
    junk = A.alloc("junk", D)
    fing = A.alloc("fing", D) if fin_g_d is not None else None
    xts = [A.alloc(f"xt{i}", 2 * D) for i in range(2)]
    xns = [A.alloc(f"xn{i}", D, BF16) for i in range(2)]
    hTs = [A.alloc(f"hT{i}", 8 * 256, BF16) for i in range(2)]
    sgs = [A.alloc(f"sg{i}", 256) for i in range(2)]
    aTs = [A.alloc(f"aT{i}", 256, BF16) for i in range(3)]
    sst = A.alloc("sst", 8)
    ds_w = [P.dsem(f"ffw{i}") for i in range(2)]
    ds_x = [P.dsem(f"ffx{i}") for i in range(2)]
    ds_g = P.dsem("ffg")
    Win3 = Win.r("p (c n) -> p c n", c=8)
    Wout3 = Wout.r("p (c n) -> p c n", c=NFC)
    k.dma("sp", gT, T(g_d.rearrange("(c p) -> p c", p=128), c.wres), ds_g, slow=True)
    if fing is not None:
        k.dma("sp", fing, T(fin_g_d.partition_broadcast(128), c.wres), ds_g)
    for ch in range(8):
        k.dma("pool", Win3[:, ch, :], T(w_in_d[ch * 128:(ch + 1) * 128, :], c.wres), ds_w[0])
    k.dma("pool", Wout3, T(w_out_d.rearrange("(c p) n -> p c n", p=128), c.wres), ds_w[1])
    src_t = src.rearrange("(t p) d -> p t d", p=128)
    dst_t = dst.rearrange("(t p) d -> p t d", p=128)
    NG = NT // 2
    gu_banks = [k.bank[0], k.bank[1]]
    tbanks = [k.bank[2], k.bank[3]]
    obanks = [k.bank[4], k.bank[5], k.bank[6], k.bank[7]]
    def ld(g):
        x3 = xts[g % 2].r("p (j d) -> p j d", j=2)
        k.dma("sp", x3, T(src_t[:, 2 * g:2 * g + 2, :], src_res[2 * g]), ds_x[g % 2],
              reads=[src_res[2 * g], src_res[2 * g + 1]])

    for g in range(NG):
        xt = xts[g % 2]
        hT = hTs[g % 2]
        xt3 = xt.r("p (j d) -> p j d", j=2)
        hT3 = hT.r("p (c t) -> p c t", c=8)
        if g == 0:
            ld(0)
        if g + 1 < NG:
            ld(g + 1)
        for j in range(2):
            norm_to_hT(k, c, xt3[:, j, :], sst[:, j:j + 1], sst[:, 2 + j:3 + j], xns[j], tbanks[j],
                       hT3[:, :, j * 128:(j + 1) * 128], gT, junk)

        def gu(fc):
            bk = gu_banks[fc % 2]
            for half in range(2):
                col = half * DFF + fc * 128
                for ch in range(8):
                    k.mm(bk[:, half * 256:(half + 1) * 256], Win3[:, ch, col:col + 128], hT3[:, ch, :],
                         start=(ch == 0), stop=(ch == 7))

        def outmm(fc):
            bk = gu_banks[fc % 2]
            sg = sgs[fc % 2]
            aT = aTs[fc % 3]
            k.act(sg, bk[:, 0:256], AF.Silu)
            k.tt(aT, sg, bk[:, 256:512], ALU.mult)
            for j in range(2):
                for dh in range(2):
                    k.mm(obanks[j * 2 + dh], aT[:, j * 128:(j + 1) * 128], Wout3[:, fc, dh * 512:(dh + 1) * 512],
                         start=(fc == 0), stop=(fc == NFC - 1))

        gu(0)
        for fc in range(NFC):
            if fc + 1 < NFC:
                gu(fc + 1)
            outmm(fc)
        for j in range(2):
            for dh in range(2):
                xs_ = xt3[:, j, dh * 512:(dh + 1) * 512]
                k.stt(xs_, obanks[j * 2 + dh], 0.5, xs_, ALU.mult, ALU.add)
            if fing is not None:
                xj = xt3[:, j, :]
                k.act(junk, xj, AF.Square)
                k.reduce(sst[:, 4 + j:5 + j], junk)
                k.rstd(sst[:, 6 + j:7 + j], sst[:, 4 + j:5 + j], 1.0 / D)
                k.stt(xj, xj, sst[:, 6 + j:7 + j], fing, ALU.mult, ALU.mult)
        k.dma("sp", T(dst_t[:, 2 * g:2 * g + 2, :], dst_res[2 * g]), xt3, ds_x[g % 2],
              writes=[dst_res[2 * g], dst_res[2 * g + 1]])
    P.barrier()
    A.release(m)


def sumsq_heads(k, src, n, hd, junk, ss):
    k.act(junk[:, 0:n * hd], src, AF.Square)
    k.reduce(ss, junk[:, 0:n * hd].r("p (h d) -> p h d", h=n))


def mla_head_path(k, c, f, g_rep, t, outb, junk, tmp, ss8, r8):
    f3 = f.r("p (h d) -> p h d", h=8)
    o3 = outb.r("p (h d) -> p h d", h=8)
    k.tt(junk[:, 0:768], f, f, ALU.mult)
    k.reduce(ss8, junk[:, 0:768].r("p (h d) -> p h d", h=8))
    k.rstd(r8, ss8, 1.0 / 96)
    k.tt(f3, f3, r8.bcast(2, [128, 8, 96]), ALU.mult)
    k.tt(f3, f3, g_rep.bcast(1, [128, 8, 96]), ALU.mult)
    cs = c.cos[:, t * 16:(t + 1) * 16].bcast(1, [128, 8, 16])
    sn = c.sin[:, t * 16:(t + 1) * 16].bcast(1, [128, 8, 16])
    x1 = f3[:, :, 64:80]
    x2 = f3[:, :, 80:96]
    t4 = tmp.r("p (a h d) -> p a h d", a=4, h=8)
    k.tt(t4[:, 0], x1, cs, ALU.mult)
    k.tt(t4[:, 1], x2, sn, ALU.mult)
    k.tt(t4[:, 2], x1, sn, ALU.mult)
    k.tt(t4[:, 3], x2, cs, ALU.mult)
    k.tt(o3[:, :, 64:80], t4[:, 0], t4[:, 1], ALU.subtract)
    k.tt(o3[:, :, 80:96], t4[:, 2], t4[:, 3], ALU.add)
    k.act(o3[:, :, 0:64], f3[:, :, 0:64], AF.Copy)


def phase_a(k, c, l):
    A, P = k.A, k.P
    w = c.w
    m = A.mark()
    Wmi = A.alloc("Wmi", 8 * 2048, BF16)
    Wuq = A.alloc("Wuq", 2 * 768, BF16)
    Wukv = A.alloc("Wukv", 1024, BF16)
    gT = A.alloc("gTmix", 8)
    gqn = A.alloc("gqn", 2)
    gkvn = A.alloc("gkvn", 1)
    gq_rep = A.alloc("gq_rep", 96)
    gk_rep = A.alloc("gk_rep", 96)
    gcol = A.alloc("gcol", 4)
    junk = A.alloc("junkA", D)
    tmp = A.alloc("tmpA", 4 * 8 * 16)
    sst = A.alloc("sstA", 64)
    xts = [A.alloc(f"xtA{i}", D) for i in range(2)]
    xn = A.alloc("xnA", D, BF16)
    hT = A.alloc("hTA", 8 * 128, BF16)
    cqn = A.alloc("cqn", 256, BF16)
    cqT = A.alloc("cqT", 256, BF16)
    ckvn = A.alloc("ckvn", 128, BF16)
    ckvT = A.alloc("ckvT", 128, BF16)
    qf = A.alloc("qf", 768)
    kf = A.alloc("kf", 768)
    qb = A.alloc("qb", 768, BF16)
    kb = A.alloc("kb", 768, BF16)
    qkb = A.alloc("qkb", 14 * 64, BF16)
    QTst = [A.alloc(f"QTst{i}", 8 * 128, BF16) for i in range(2)]
    KTst = [A.alloc(f"KTst{i}", 8 * 128, BF16) for i in range(2)]
    QKst = [A.alloc(f"QKst{i}", 14 * 128, BF16) for i in range(2)]
    Vmst = [A.alloc(f"Vmst{i}", 8 * 66, BF16) for i in range(2)]
    Vsst = [A.alloc(f"Vsst{i}", 2 * 65, BF16) for i in range(2)]
    Vnst = [A.alloc(f"Vnst{i}", 4 * 65, BF16) for i in range(2)]
    ds_w = P.dsem("aw")
    ds_x = [P.dsem(f"ax{i}") for i in range(2)]
    ds_st = [P.dsem(f"ast{i}") for i in range(2)]
    Wmi3 = Wmi.r("p (c n) -> p c n", c=8)
    Wuq3 = Wuq.r("p (c n) -> p c n", c=2)
    wmi = w["w_mix_in"][l].rearrange("(c p) n -> p c n", p=128)
    for (c0, s0, n) in ((0, 0, 416), (512, 416, 512), (1024, 928, 512), (1536, 1440, 256)):
        k.dma("pool", Wmi3[:, :, c0:c0 + n], T(wmi[:, :, s0:s0 + n], c.wres), ds_w)
    k.dma("pool", Wuq3, T(w["mla_w_uq"][l].rearrange("(c p) n -> p c n", p=128), c.wres), ds_w)
    k.dma("pool", Wukv, T(w["mla_w_ukv"][l], c.wres), ds_w)
    k.dma("sp", gT, T(w["mix_norm"][l].rearrange("(c p) -> p c", p=128), c.wres), ds_w, slow=True)
    k.dma("sp", gqn, T(w["mla_q_norm"][l].rearrange("(c p) -> p c", p=128), c.wres), ds_w, slow=True)
    k.dma("sp", gkvn, T(w["mla_kv_norm"][l].rearrange("(c p) -> p c", p=128), c.wres), ds_w, slow=True)
    k.dma("sp", gq_rep, T(w["mla_q_gain"][l].partition_broadcast(128), c.wres), ds_w)
    k.dma("sp", gk_rep, T(w["mla_k_gain"][l].partition_broadcast(128), c.wres), ds_w)
    for i, nm in enumerate(("swa_q_gain", "swa_k_gain", "na_q_gain", "na_k_gain")):
        k.dma("sp", gcol[0:64, i:i + 1], T(w[nm][l].rearrange("(p o) -> p o", o=1), c.wres), ds_w, slow=True)
    for i in range(2):
        for vt, nh, e in ((Vmst[i], 8, 66), (Vsst[i], 2, 65), (Vnst[i], 4, 65)):
            k.ts(vt.r("p (h e) -> p h e", h=nh)[:, :, 64:65], c.ident_f[:, 0:nh].r("p (h o) -> p h o", o=1),
                 0.0, ALU.mult, 1.0, ALU.add)
    xs_t = c.xs.rearrange("(t p) d -> p t d", p=128)
    B = k.bank

    def ld(t):
        k.dma("sp", xts[t % 2], T(xs_t[:, t, :], c.xs_res[t]), ds_x[t % 2])

    for t in range(NT):
        if t == 0:
            ld(0)
        if t + 1 < NT:
            ld(t + 1)
        xt = xts[t % 2]
        i2 = t % 2
        tsl = slice(t * 128, (t + 1) * 128)
        hT3 = hT.r("p (c t) -> p c t", c=8)
        norm_to_hT(k, c, xt, sst[:, 0:1], sst[:, 1:2], xn, B[4], hT3, gT, junk)
        for b, (c0, n) in enumerate(((0, 416), (512, 512), (1024, 512), (1536, 256))):
            for ch in range(8):
                k.mm(B[b][:, 0:n], hT3[:, ch, :], Wmi3[:, ch, c0:c0 + n], start=(ch == 0), stop=(ch == 7))
        if c.stop == "a1":
            continue
        sumsq_heads(k, B[0][:, 0:256], 1, 256, junk, sst[:, 2:3])
        k.rstd(sst[:, 3:4], sst[:, 2:3], 1.0 / 256)
        k.ts(cqn, B[0][:, 0:256], sst[:, 3:4], ALU.mult)
        tb4 = B[4].bc(BF16)
        for c2 in range(2):
            k.tr(tb4[:, c2 * 128:(c2 + 1) * 128], cqn[:, c2 * 128:(c2 + 1) * 128], c.ident_bf)
        k.tt(cqT.r("p (c t) -> p c t", c=2), tb4[:, 0:256].r("p (c t) -> p c t", c=2),
             gqn.bcast(2, [128, 2, 128]), ALU.mult)
        cqT3 = cqT.r("p (c t) -> p c t", c=2)
        for half in range(2):
            for c2 in range(2):
                k.mm(B[6 + half][:, 0:384], cqT3[:, c2, :], Wuq3[:, c2, half * 384:(half + 1) * 384],
                     start=(c2 == 0), stop=(c2 == 1))
        k.act(qf[:, 0:384], B[6][:, 0:384], AF.Copy)
        k.act(qf[:, 384:768], B[7][:, 0:384], AF.Copy)
        mla_head_path(k, c, qf, gq_rep, t, qb, junk, tmp, sst[:, 8:16], sst[:, 16:24])
        tb5 = B[5].bc(BF16)
        for h in range(8):
            k.tr(tb5[0:96, h * 128:(h + 1) * 128], qb[:, h * 96:(h + 1) * 96], c.ident_bf)
        k.copy(QTst[i2][0:96, :], tb5[0:96, :])
        k.dma("sp", T(c.QTm[:, :, tsl].rearrange("h d t -> d h t"), c.QTm_res),
              QTst[i2][0:96, :].r("p (h t) -> p h t", h=8), ds_st[i2])
        if c.stop == "a2":
            continue
        sumsq_heads(k, B[0][:, 256:384], 1, 128, junk, sst[:, 4:5])
        k.rstd(sst[:, 5:6], sst[:, 4:5], 1.0 / 128)
        k.ts(ckvn, B[0][:, 256:384], sst[:, 5:6], ALU.mult)
        if c.stop == "k0":
            continue
        k.tr(tb4[:, 0:128], ckvn, c.ident_bf)
        k.ts(ckvT, tb4[:, 0:128], gkvn[:, 0:1], ALU.mult)
        if c.stop == "k1":
            continue
        for half in range(2):
            k.mm(B[6 + half], ckvT, Wukv[:, half * 512:(half + 1) * 512])
        kf3 = kf.r("p (h d) -> p h d", h=8)
        vm3 = Vmst[i2].r("p (h e) -> p h e", h=8)
        if c.stop == "k2":
            continue
        for half in range(2):
            kv4 = B[6 + half].r("p (h d) -> p h d", h=4)
            if c.stop != "k3d":
                k.act(kf3[:, half * 4:(half + 1) * 4, 0:64], kv4[:, :, 0:64], AF.Copy)
            if c.stop != "k3a":
                k.copy(vm3[:, half * 4:(half + 1) * 4, 0:64], kv4[:, :, 64:128])
        if c.stop in ("k3", "k3a", "k3d"):
            continue
        if c.stop == "k4x":
            k.copy(tmp[:, 0:32], B[0][:, 384:416])
            continue
        if c.stop == "k4y":
            k.copy(kf3[:, :, 64:96], tmp[:, 0:32].bcast(1, [128, 8, 32]))
            continue
        k.copy(kf3[:, :, 64:96], B[0][:, 384:416].bcast(1, [128, 8, 32]))
        if c.stop == "k4":
            continue
        mla_head_path(k, c, kf, gk_rep, t, kb, junk, tmp, sst[:, 24:32], sst[:, 32:40])
        if c.stop == "k5":
            continue
        for h in range(8):
            k.tr(tb5[0:96, h * 128:(h + 1) * 128], kb[:, h * 96:(h + 1) * 96], c.ident_bf)
        k.copy(KTst[i2][0:96, :], tb5[0:96, :])
        if c.stop == "k6":
            continue
        kst3 = KTst[i2][0:96, :].r("p (h t) -> p h t", h=8)
        for j in range(4):
            k.dma("sp", T(c.KTm_l[j][:, :, tsl].rearrange("h d t -> d h t"), c.KTm_l_res),
                  kst3[:, 2 * j:2 * j + 2, :], ds_st[i2])
            k.dma("sp", T(c.Vm_l[j][:, :, t, :].rearrange("h p e -> p h e"), c.Vm_l_res),
                  vm3[:, 2 * j:2 * j + 2, :], ds_st[i2])
        if c.stop == "a3":
            continue
        k.act(junk[:, 0:384], B[1][:, 0:384], AF.Square)
        k.act(junk[:, 384:896], B[2][:, 0:512], AF.Square)
        k.reduce(sst[:, 40:54], junk[:, 0:896].r("p (h d) -> p h d", h=14))
        k.rstd(sst[:, 40:54], sst[:, 40:54], 1.0 / 64)
        qkb3 = qkb.r("p (h d) -> p h d", h=14)
        k.tt(qkb3[:, 0:6, :], B[1][:, 0:384].r("p (h d) -> p h d", h=6), sst[:, 40:46].bcast(2, [128, 6, 64]), ALU.mult)
        k.tt(qkb3[:, 6:14, :], B[2][:, 0:512].r("p (h d) -> p h d", h=8), sst[:, 46:54].bcast(2, [128, 8, 64]), ALU.mult)
        st = QKst[i2]
        for h in range(8):
            k.tr(tb5[0:64, h * 128:(h + 1) * 128], qkb[:, h * 64:(h + 1) * 64], c.ident_bf)
        for h in range(8, 14):
            k.tr(tb4[0:64, (h - 8) * 128:(h - 7) * 128], qkb[:, h * 64:(h + 1) * 64], c.ident_bf)
        k.ts(st[0:64, 0:512], tb5[0:64, 0:512], gcol[0:64, 0:1], ALU.mult)
        k.ts(st[0:64, 512:768], tb5[0:64, 512:768], gcol[0:64, 1:2], ALU.mult)
        k.ts(st[0:64, 768:1024], tb5[0:64, 768:1024], gcol[0:64, 2:3], ALU.mult)
        k.ts(st[0:64, 1024:1280], tb4[0:64, 0:256], gcol[0:64, 2:3], ALU.mult)
        k.ts(st[0:64, 1280:1792], tb4[0:64, 256:768], gcol[0:64, 3:4], ALU.mult)
        st3 = st[0:64, :].r("p (h t) -> p h t", h=14)
        k.dma("sp", T(c.QTs[:, :, tsl].rearrange("h d t -> d h t"), c.QTs_res), st3[:, 0:4, :], ds_st[i2])
        k.dma("sp", T(c.KTs_l[0][:, :, tsl].rearrange("h d t -> d h t"), c.KTs_l_res), st3[:, 4:6, :], ds_st[i2])
        k.dma("sp", T(c.QTn[:, :, tsl].rearrange("h d t -> d h t"), c.QTn_res), st3[:, 6:10, :], ds_st[i2])
        k.dma("sp", T(c.KTn_l[0][:, :, tsl].rearrange("h d t -> d h t"), c.KTn_l_res), st3[:, 10:14, :], ds_st[i2])
        vs3 = Vsst[i2].r("p (h e) -> p h e", h=2)
        vn3 = Vnst[i2].r("p (h e) -> p h e", h=4)
        k.copy(vs3[:, :, 0:64], B[1][:, 384:512].r("p (h d) -> p h d", h=2))
        k.copy(vn3[:, :, 0:64], B[3][:, 0:256].r("p (h d) -> p h d", h=4))
        k.dma("sp", T(c.Vs_l[0][t], c.Vs_l_res), Vsst[i2], ds_st[i2])
        k.dma("sp", T(c.Vn_l[t // (NT // 2)][t % (NT // 2)], c.Vn_l_res), Vnst[i2], ds_st[i2])
    P.barrier()
    A.release(m)
    if c.stop in ("a0", "a1", "a2", "a3") or c.stop.startswith("k"):
        return
    for (cn, t_l, t_g, nm) in c.cc_list:
        ds = P.dsem(f"cc_{cn}_{l}", inc=1)
        P.op("pool", (lambda s_, d_: (lambda e: e.collective_compute(
            "AllGather", ALU.bypass, replica_groups=[[0, 1], [2, 3], [4, 5], [6, 7]][:c.ncores // 2],
            ins=[s_.ap().opt()], outs=[d_.ap().opt()])))(t_l, t_g),
            reads=[getattr(c, nm + "_l_res")], writes=[getattr(c, nm + "_g_res")], dsem=ds)
    P.barrier()


def phase_m(k, c, l):
    A, P = k.A, k.P
    m = A.mark()
    KTs_ = [A.alloc(f"KTm{i}", 2 * TOK, BF16) for i in range(2)]
    VAs = [A.alloc(f"VAm{i}", 2 * NT * 66, BF16) for i in range(2)]
    QTs_ = [A.alloc(f"QTm{i}", TOK, BF16) for i in range(2)]
    PTs = [A.alloc(f"PT{i}", 1024, BF16) for i in range(3)]
    OTs = [A.alloc(f"OT{i}", 512) for i in range(2)]
    ost = [A.alloc(f"ost{i}", 4 * 64) for i in range(2)]
    rc = A.alloc("rcm", 8)
    ds_h = [P.dsem(f"mh{i}") for i in range(2)]
    ds_o = [P.dsem(f"mo{i}") for i in range(2)]
    B = k.bank
    scale = 96 ** -0.5
    om = c.omla.rearrange("(t p) h e -> p t h e", p=128)

    def ld(h):
        i = h % 2
        k.dma("sp", KTs_[i][0:96, :].r("p (r t) -> p r t", r=2), T(c.KTm_g[h // 2][:, h % 2].rearrange("r d t -> d r t"), c.KTm_g_res), ds_h[i])
        k.dma("sp", VAs[i].r("p (r t e) -> p r t e", r=2, t=NT), T(c.Vm_g[h // 2][:, h % 2].rearrange("r p t e -> p r t e"), c.Vm_g_res), ds_h[i])
        k.dma("sp", QTs_[i][0:96, :], T(c.QTm[h], c.QTm_res), ds_h[i])

    cnt = [0]
    pend = []

    def fin_pe(args):
        qg, h, ob, n = args
        OT = OTs[n % 2]
        tb = B[5]
        for j in range(4):
            k.tr(tb[:, j * 65:(j + 1) * 65], OT[0:65, j * 128:(j + 1) * 128], c.ident_f[0:65, 0:65])
        t3 = tb[:, 0:260].r("p (j e) -> p j e", j=4)
        k.recip(rc[:, 0:4], t3[:, :, 64])
        o3 = ost[n % 2].r("p (j e) -> p j e", j=4)
        k.tt(o3, t3[:, :, 0:64], rc[:, 0:4].bcast(2, [128, 4, 64]), ALU.mult)
        k.dma("sp", T(om[:, qg * 4:(qg + 1) * 4, h, :], c.omla_res), o3, ds_o[n % 2])

    items = [(h, qg, kp) for h in range(MLA_H) for qg in range(8) for kp in range(32)]

    def bufs(h):
        return KTs_[h % 2], VAs[h % 2].r("p (r t e) -> p r t e", r=2, t=NT), QTs_[h % 2]

    def S_(it, idx):
        h, qg, kp = it
        KT, VA, QT = bufs(h)
        b0 = (idx % 2) * 2
        for j in range(2):
            kt = 2 * kp + j
            r, lt = kt // NT, kt % NT
            k.mm(B[b0 + j], KT[0:96, r * TOK + lt * 128:r * TOK + (lt + 1) * 128], QT[0:96, qg * 512:(qg + 1) * 512])

    def EXP_(it, idx):
        b0 = (idx % 2) * 2
        PT = PTs[idx % 3]
        src2 = T(k.psum[:, b0 * 512:(b0 + 2) * 512], B[b0].res)
        P.op("act", (lambda o_, i_: (lambda e: e.activation(out=o_.ap, in_=i_.ap, func=AF.Exp, scale=scale)))(PT, src2),
             reads=[B[b0], B[b0 + 1]], writes=[PT])

    def PV_(it, idx, ob):
        h, qg, kp = it
        KT, VA, QT = bufs(h)
        PT = PTs[idx % 3]
        for j in range(2):
            kt = 2 * kp + j
            r, lt = kt // NT, kt % NT
            k.mm(ob[0:65, :], VA[:, r, lt, 0:65], PT[:, j * 512:(j + 1) * 512], start=(kt == 0), stop=(kt == 63))

    ld(0)
    S_(items[0], 0)
    for idx, it in enumerate(items):
        h, qg, kp = it
        if qg == 0 and kp == 0 and h + 1 < MLA_H:
            ld(h + 1)
        n = h * 8 + qg
        ob = B[6 + n % 2]
        if idx + 1 < len(items):
            S_(items[idx + 1], idx + 1)
        EXP_(it, idx)
        PV_(it, idx, ob)
        if kp == 31:
            k.copy(OTs[n % 2][0:65, :], ob[0:65, :])
            if pend:
                fin_pe(pend.pop())
            pend.append((qg, h, ob, n))
    fin_pe(pend.pop())
    P.barrier()
    A.release(m)


def phase_b(k, c, l):
    A, P = k.A, k.P
    w = c.w
    m = A.mark()
    B = k.bank
    Wmo = A.alloc("Wmo", 8 * D, BF16)
    Wq = A.alloc("Wq", 8 * 256, BF16)
    Wo = A.alloc("Wo", 2 * D, BF16)
    Wkv = A.alloc("Wkv", 8 * 512, BF16)
    gT_grp = A.alloc("gT_grp", 8)
    gT_mx = A.alloc("gT_mx", 8)
    gT_mm = A.alloc("gT_mm", 8)
    gcm = A.alloc("gcm", 2)
    esink = A.alloc("esink", 4)
    KTmem = A.alloc("KTmem", 4 * 256, BF16)
    VAmem = A.alloc("VAmem", 2 * 4 * 65, BF16)
    swat = A.alloc("swat", 5 * 512)
    nagen = A.alloc("nagen", 5 * 512)
    naedge = A.alloc("naedge", 7 * 512)
    junk = A.alloc("junkB", D)
    sst = A.alloc("sstB", 32)
    xn = A.alloc("xnB", D, BF16)
    hT = A.alloc("hTB", 8 * 128, BF16)
    ocb = A.alloc("ocb", D, BF16)
    oT = A.alloc("oTB", 8 * 128, BF16)
    qmb = A.alloc("qmb", 256, BF16)
    QTmem = A.alloc("QTmem", 4 * 128, BF16)
    omb = A.alloc("omb", 256, BF16)
    omf = A.alloc("omf", 256)
    omT = A.alloc("omT", 2 * 128, BF16)
    sbs = [A.alloc(f"sb{i}", 512) for i in range(2)]
    PTs = [A.alloc(f"PTb{i}", 512, BF16) for i in range(7)]
    xts = [A.alloc(f"xtB{i}", D) for i in range(2)]
    ocs = [A.alloc(f"oc{i}", D) for i in range(2)]
    QTs_ = [A.alloc(f"QTsB{i}", 4 * 128, BF16) for i in range(2)]
    KTs_ = [A.alloc(f"KTsB{i}", 2 * 3 * 128, BF16) for i in range(2)]
    Vs_ = [A.alloc(f"VsB{i}", 3 * 130, BF16) for i in range(2)]
    QTn_ = [A.alloc(f"QTnB{i}", 4 * 128, BF16) for i in range(2)]
    KTn_ = [A.alloc(f"KTnB{i}", 4 * 7 * 128, BF16) for i in range(2)]
    Vn_ = [A.alloc(f"VnB{i}", 7 * 260, BF16) for i in range(2)]
    ds_w = P.dsem("bw")
    ds_t = [P.dsem(f"bt{i}") for i in range(2)]
    ds_e = P.dsem("be")
    ds_s = [P.dsem(f"bs{i}") for i in range(2)]
    Wmo3 = Wmo.r("p (c n) -> p c n", c=8)
    Wq3 = Wq.r("p (c n) -> p c n", c=8)
    Wo3 = Wo.r("p (c n) -> p c n", c=2)
    Wkv3 = Wkv.r("p (c n) -> p c n", c=8)
    k.dma("pool", Wmo3, T(w["w_mix_out"][l].rearrange("(c p) n -> p c n", p=128), c.wres), ds_w)
    k.dma("pool", Wq3, T(w["mem_w_q"][l].rearrange("(c p) n -> p c n", p=128), c.wres), ds_w)
    k.dma("pool", Wo3, T(w["mem_w_o"][l].rearrange("(c p) n -> p c n", p=128), c.wres), ds_w)
    k.dma("pool", Wkv3, T(w["mem_w_kv"][l].rearrange("(c p) n -> p c n", p=128), c.wres), ds_w)
    for tl, nm in ((gT_grp, "grp_out_gain"), (gT_mx, "mem_norm_x"), (gT_mm, "mem_norm_m")):
        k.dma("sp", tl, T(w[nm][l].rearrange("(c p) -> p c", p=128), c.wres), ds_w, slow=True)
    k.dma("sp", gcm[0:64, 0:1], T(w["mem_q_gain"][l].rearrange("(p o) -> p o", o=1), c.wres), ds_w, slow=True)
    k.dma("sp", gcm[0:64, 1:2], T(w["mem_k_gain"][l].rearrange("(p o) -> p o", o=1), c.wres), ds_w, slow=True)
    k.dma("sp", esink, T(w["swa_sink"][l].partition_broadcast(128), c.wres), ds_w)
    k.dma("sp", swat.r("p (i n) -> p i n", i=5), T(c.swa_tab.rearrange("i p n -> p i n"), c.wres), ds_w)
    k.dma("sp", nagen.r("p (i n) -> p i n", i=5), T(c.na_gen[l].rearrange("i p n -> p i n"), c.wres), ds_w)
    k.act(esink, esink, AF.Exp)
    k.ts(VAmem.r("p (m e) -> p m e", m=8)[:, :, 64:65], c.ident_f[:, 0:8].r("p (h o) -> p h o", o=1),
         0.0, ALU.mult, 1.0, ALU.add)
    va4 = VAmem.r("p (m h e) -> p m h e", m=2, h=4)
    hT3 = hT.r("p (c t) -> p c t", c=8)
    tb4 = B[4].bc(BF16)
    for mt in range(2):
        k.dma("sp", xts[mt], T(c.mem[mt * 128:(mt + 1) * 128, :], c.wres), ds_t[mt])
        norm_to_hT(k, c, xts[mt], sst[:, 0:1], sst[:, 1:2], xn, B[4], hT3, gT_mm, junk)
        for ch in range(8):
            k.mm(B[5], hT3[:, ch, :], Wkv3[:, ch, :], start=(ch == 0), stop=(ch == 7))
        sumsq_heads(k, B[5][:, 0:256], 4, 64, junk, sst[:, 2:6])
        k.rstd(sst[:, 2:6], sst[:, 2:6], 1.0 / 64)
        k.tt(qmb.r("p (h d) -> p h d", h=4), B[5][:, 0:256].r("p (h d) -> p h d", h=4),
             sst[:, 2:6].bcast(2, [128, 4, 64]), ALU.mult)
        for hd in range(4):
            k.tr(tb4[0:64, hd * 128:(hd + 1) * 128], qmb[:, hd * 64:(hd + 1) * 64], c.ident_bf)
        k.ts(KTmem[0:64, :].r("p (h m) -> p h m", h=4)[:, :, mt * 128:(mt + 1) * 128],
             tb4[0:64, 0:512].r("p (h m) -> p h m", h=4), gcm[0:64, 1:2], ALU.mult)
        k.copy(va4[:, mt, :, 0:64], B[5][:, 256:512].r("p (h d) -> p h d", h=4))
    KTmem3 = KTmem[0:64, :].r("p (h m) -> p h m", h=4)
    xs_t = c.xs.rearrange("(t p) d -> p t d", p=128)
    om_t = c.omla.rearrange("(t p) h e -> p t (h e)", p=128)
    na_edge_t = {0: 0, 1: 1, NT - 2: 2, NT - 1: 3}

    def ktile_src(loc, gat, lt):
        if 0 <= lt < NT:
            return loc[0][:, :, lt * 128:(lt + 1) * 128]
        if lt < 0:
            return gat[0][0][:, :, (NT + lt) * 128:(NT + lt + 1) * 128]
        return gat[0][1][:, :, (lt - NT) * 128:(lt - NT + 1) * 128]

    def vtile_src(loc, gat, lt):
        nch = len(loc)
        per = NT // nch

        def pick(lst, tile, r=None):
            a = lst[tile // per]
            return a[tile % per] if r is None else a[r][tile % per]
        if 0 <= lt < NT:
            return pick(loc, lt)
        if lt < 0:
            return pick(gat, NT + lt, 0)
        return pick(gat, lt - NT, 1)

    def win_n(t):
        return list(range(-2, 3)) if t not in na_edge_t else list(range(-3, 4))

    def ld(t):
        i2 = t % 2
        ds = ds_t[i2]
        tsl = slice(t * 128, (t + 1) * 128)
        k.dma("sp", xts[i2], T(xs_t[:, t, :], c.xs_res[t]), ds)
        k.dma("sp", ocs[i2][:, 0:512], T(om_t[:, t, :], c.omla_res), ds)
        k.dma("sp", QTs_[i2][0:64, :].r("p (h t) -> p h t", h=4), T(c.QTs[:, :, tsl].rearrange("h d t -> d h t"), c.QTs_res), ds)
        k.dma("sp", QTn_[i2][0:64, :].r("p (h t) -> p h t", h=4), T(c.QTn[:, :, tsl].rearrange("h d t -> d h t"), c.QTn_res), ds)
        ks3 = KTs_[i2][0:64, :].r("p (h i t) -> p h i t", h=2, i=3)
        vs3 = Vs_[i2].r("p (i e) -> p i e", i=3)
        for j, i in enumerate((-1, 0, 1)):
            k.dma("sp", ks3[:, :, j, :], T(ktile_src(c.KTs_l, c.KTs_g, t + i).rearrange("h d t -> d h t"), c.KTs_l_res), ds,
                  reads=[c.KTs_l_res, c.KTs_g_res])
            k.dma("sp", vs3[:, j, :], T(vtile_src(c.Vs_l, c.Vs_g, t + i), c.Vs_l_res), ds, reads=[c.Vs_l_res, c.Vs_g_res])
        kn3 = KTn_[i2][0:64, :].r("p (h i t) -> p h i t", h=4, i=7)
        vn3 = Vn_[i2].r("p (i e) -> p i e", i=7)
        for j, i in enumerate(win_n(t)):
            k.dma("sp", kn3[:, :, j, :], T(ktile_src(c.KTn_l, c.KTn_g, t + i).rearrange("h d t -> d h t"), c.KTn_l_res), ds,
                  reads=[c.KTn_l_res, c.KTn_g_res])
            k.dma("sp", vn3[:, j, :], T(vtile_src(c.Vn_l, c.Vn_g, t + i), c.Vn_l_res), ds, reads=[c.Vn_l_res, c.Vn_g_res])

    sbc = [0]

    def attend(Q3, Ksrc, Vsrc, tabs, nk, hkv, scale, Ob):
        for j in range(nk):
            n = sbc[0]
            sbc[0] += 1
            Sb = B[n % 2]
            Kt = Ksrc(j)
            for hd in range(4):
                k.mm(Sb[:, hd * 128:(hd + 1) * 128], Kt[:, hd * hkv // 4, :], Q3[:, hd, :])
            PT = PTs[j]
            tab = tabs(j)
            if tab is not None:
                sb = sbs[n % 2]
                k.stt(sb, Sb, scale, tab, ALU.mult, ALU.add)
                k.act(PT, sb, AF.Exp)
            else:
                k.act(PT, Sb, AF.Exp, scale=scale)
        for hd in range(4):
            for j in range(nk):
                Vt = Vsrc(j)
                k.mm(Ob[:, hd * 65:(hd + 1) * 65], PTs[j][:, hd * 128:(hd + 1) * 128], Vt[:, hd * hkv // 4, :],
                     start=(j == 0), stop=(j == nk - 1))

    def finish(Ob, dst, den_add, rcol):
        o3 = Ob[:, 0:260].r("p (h e) -> p h e", h=4)
        if den_add is not None:
            k.tt(rcol, o3[:, :, 64], den_add, ALU.add)
            k.recip(rcol, rcol)
        else:
            k.recip(rcol, o3[:, :, 64])
        k.tt(dst.r("p (h d) -> p h d", h=4), o3[:, :, 0:64], rcol.bcast(2, [128, 4, 64]), ALU.mult)

    ld(0)
    for t in range(NT):
        if t + 1 < NT:
            ld(t + 1)
        i2 = t % 2
        xt = xts[i2]
        oc = ocs[i2]
        Qs3 = QTs_[i2][0:64, :].r("p (h t) -> p h t", h=4)
        ks3 = KTs_[i2][0:64, :].r("p (h i t) -> p h i t", h=2, i=3)
        vs4 = Vs_[i2].r("p (i h e) -> p i h e", i=3, h=2)
        sw3 = swat.r("p (i n) -> p i n", i=5)

        def swa_tab(j, t=t):
            if j == 0 and t == 0:
                return sw3[:, 3, :]
            if j == 2 and t == NT - 1:
                return sw3[:, 4, :]
            return sw3[:, j, :]

        attend(Qs3, lambda j: ks3[:, :, j, :], lambda j: vs4[:, j, :, :], swa_tab, 3, 2, 0.125, B[2])
        finish(B[2], oc[:, 512:768], esink, sst[:, 8:12])
        Qn3 = QTn_[i2][0:64, :].r("p (h t) -> p h t", h=4)
        kn3 = KTn_[i2][0:64, :].r("p (h i t) -> p h i t", h=4, i=7)
        vn4 = Vn_[i2].r("p (i h e) -> p i h e", i=7, h=4)
        if t in na_edge_t:
            ne3 = naedge.r("p (i n) -> p i n", i=7)
            k.dma("sp", ne3, T(c.na_edge[l, na_edge_t[t]].rearrange("i p n -> p i n"), c.wres), ds_e)
            nk = 7
            ntab = lambda j: ne3[:, j, :]
        else:
            ng3 = nagen.r("p (i n) -> p i n", i=5)
            nk = 5
            ntab = lambda j: ng3[:, j, :]
        attend(Qn3, lambda j: kn3[:, :, j, :], lambda j: vn4[:, j, :, :], ntab, nk, 4, 0.125, B[3])
        finish(B[3], oc[:, 768:1024], None, sst[:, 12:16])
        if c.debug:
            k.dma("sp", T(c.ocat_dbg.rearrange("(t p) d -> p t d", p=128)[:, t, :], Res("dbg")), oc, ds_s[i2])
        k.act(junk, oc, AF.Square)
        k.reduce(sst[:, 16:17], junk[:, 0:512])
        k.reduce(sst[:, 17:19], junk[:, 512:1024].r("p (g d) -> p g d", g=2))
        k.rstd(sst[:, 20:21], sst[:, 16:17], 1.0 / 512)
        k.rstd(sst[:, 21:23], sst[:, 17:19], 1.0 / 256)
        k.ts(ocb[:, 0:512], oc[:, 0:512], sst[:, 20:21], ALU.mult)
        k.ts(ocb[:, 512:768], oc[:, 512:768], sst[:, 21:22], ALU.mult)
        k.ts(ocb[:, 768:1024], oc[:, 768:1024], sst[:, 22:23], ALU.mult)
        for ch in range(8):
            k.tr(tb4[:, ch * 128:(ch + 1) * 128], ocb[:, ch * 128:(ch + 1) * 128], c.ident_bf)
        oT3 = oT.r("p (c t) -> p c t", c=8)
        k.tt(oT3, tb4.r("p (c t) -> p c t", c=8), gT_grp.bcast(2, [128, 8, 128]), ALU.mult)
        for dh in range(2):
            for ch in range(8):
                k.mm(B[6 + dh], oT3[:, ch, :], Wmo3[:, ch, dh * 512:(dh + 1) * 512], start=(ch == 0), stop=(ch == 7))
        for dh in range(2):
            xs_ = xt[:, dh * 512:(dh + 1) * 512]
            k.tt(xs_, xs_, B[6 + dh], ALU.add)
        norm_to_hT(k, c, xt, sst[:, 0:1], sst[:, 1:2], xn, B[4], hT3, gT_mx, junk)
        for ch in range(8):
            k.mm(B[5][:, 0:256], hT3[:, ch, :], Wq3[:, ch, :], start=(ch == 0), stop=(ch == 7))
        sumsq_heads(k, B[5][:, 0:256], 4, 64, junk, sst[:, 2:6])
        k.rstd(sst[:, 2:6], sst[:, 2:6], 1.0 / 64)
        k.tt(qmb.r("p (h d) -> p h d", h=4), B[5][:, 0:256].r("p (h d) -> p h d", h=4),
             sst[:, 2:6].bcast(2, [128, 4, 64]), ALU.mult)
        for hd in range(4):
            k.tr(tb4[0:64, hd * 128:(hd + 1) * 128], qmb[:, hd * 64:(hd + 1) * 64], c.ident_bf)
        k.ts(QTmem[0:64, :], tb4[0:64, 0:512], gcm[0:64, 0:1], ALU.mult)
        Qm3 = QTmem[0:64, :].r("p (h t) -> p h t", h=4)
        attend(Qm3, lambda j: KTmem3[:, :, j * 128:(j + 1) * 128], lambda j: va4[:, j, :, :], lambda j: None, 2, 4, 0.125, B[2])
        finish(B[2], omf, None, sst[:, 24:28])
        k.act(omb, omf, AF.Copy)
        for c2 in range(2):
            k.tr(tb4[:, c2 * 128:(c2 + 1) * 128], omb[:, c2 * 128:(c2 + 1) * 128], c.ident_bf)
        k.copy(omT, tb4[:, 0:256])
        omT3 = omT.r("p (c t) -> p c t", c=2)
        for dh in range(2):
            for c2 in range(2):
                k.mm(B[6 + dh], omT3[:, c2, :], Wo3[:, c2, dh * 512:(dh + 1) * 512], start=(c2 == 0), stop=(c2 == 1))
        for dh in range(2):
            xs_ = xt[:, dh * 512:(dh + 1) * 512]
            k.tt(xs_, xs_, B[6 + dh], ALU.add)
        k.dma("sp", T(xs_t[:, t, :], c.xs_res[t]), xt, ds_s[i2])
    P.barrier()
    A.release(m)


WNAMES = ["ffn1_norm", "ffn1_w_in", "ffn1_w_out", "mix_norm", "w_mix_in", "mla_q_norm", "mla_w_uq", "mla_kv_norm",
          "mla_w_ukv", "mla_q_gain", "mla_k_gain", "swa_q_gain", "swa_k_gain", "swa_sink", "na_q_gain", "na_k_gain",
          "grp_out_gain", "w_mix_out", "mem_norm_x", "mem_norm_m", "mem_w_q", "mem_w_kv", "mem_q_gain",
          "mem_k_gain", "mem_w_o", "ffn2_norm", "ffn2_w_in", "ffn2_w_out", "block_norm"]
WSHAPES = {
    "ffn1_norm": [DEPTH, D], "ffn1_w_in": [DEPTH, D, 2 * DFF], "ffn1_w_out": [DEPTH, DFF, D], "mix_norm": [DEPTH, D],
    "w_mix_in": [DEPTH, D, MIX_IN], "mla_q_norm": [DEPTH, 256], "mla_w_uq": [DEPTH, 256, 768],
    "mla_kv_norm": [DEPTH, 128], "mla_w_ukv": [DEPTH, 128, 1024], "mla_q_gain": [DEPTH, 96], "mla_k_gain": [DEPTH, 96],
    "swa_q_gain": [DEPTH, 64], "swa_k_gain": [DEPTH, 64], "swa_sink": [DEPTH, 4], "na_q_gain": [DEPTH, 64],
    "na_k_gain": [DEPTH, 64], "grp_out_gain": [DEPTH, D], "w_mix_out": [DEPTH, D, D], "mem_norm_x": [DEPTH, D],
    "mem_norm_m": [DEPTH, D], "mem_w_q": [DEPTH, D, 256], "mem_w_kv": [DEPTH, D, 512], "mem_q_gain": [DEPTH, 64],
    "mem_k_gain": [DEPTH, 64], "mem_w_o": [DEPTH, 256, D], "ffn2_norm": [DEPTH, D], "ffn2_w_in": [DEPTH, D, 2 * DFF],
    "ffn2_w_out": [DEPTH, DFF, D], "block_norm": [DEPTH, D],
}
ARENA_BYTES = 204800


def build(stop="full", nlayers=DEPTH, ncores=8, debug=False):
    nc = bass.Bass("TRN2", target_bir_lowering=False)
    c = Ctx()
    c.ncores = ncores
    c.debug = debug
    dk = {"kind": "ExternalOutput"} if debug else {}
    c.stop = stop
    c.x_in = nc.dram_tensor("x", [TOK, D], F32, kind="ExternalInput").ap()
    c.mem = nc.dram_tensor("mem", [NMEM, D], F32, kind="ExternalInput").ap()
    c.ident_d = nc.dram_tensor("ident", [128, 128], F32, kind="ExternalInput").ap()
    c.cos_d = nc.dram_tensor("cos", [TOK, 16], F32, kind="ExternalInput").ap()
    c.sin_d = nc.dram_tensor("sin", [TOK, 16], F32, kind="ExternalInput").ap()
    c.swa_tab = nc.dram_tensor("swa_tab", [5, 128, 512], F32, kind="ExternalInput").ap()
    c.na_gen = nc.dram_tensor("na_gen", [DEPTH, 5, 128, 512], F32, kind="ExternalInput").ap()
    c.na_edge = nc.dram_tensor("na_edge", [DEPTH, 4, 7, 128, 512], F32, kind="ExternalInput").ap()
    c.w = {n: nc.dram_tensor(n, WSHAPES[n], F32, kind="ExternalInput").ap() for n in WNAMES}
    c.y = nc.dram_tensor("y", [TOK, D], F32, kind="ExternalOutput").ap()
    c.xs = nc.dram_tensor("xs", [TOK, D], F32, **dk).ap()
    c.omla = nc.dram_tensor("omla", [TOK, 8, 64], F32, **dk).ap()
    if debug:
        c.ocat_dbg = nc.dram_tensor("ocat_dbg", [TOK, D], F32, **dk).ap()
    c.omla_res = Res("omla")

    c.cc_list = []

    def scratch(name, nchunk, rows, cols, pat_l, pat_g, **kw):
        ls, gs = [], []
        for j in range(nchunk):
            t_l = nc.dram_tensor(f"{name}_l{j}", [rows, cols], BF16)
            t_g = nc.dram_tensor(f"{name}_g{j}", [2 * rows, cols], BF16)
            ls.append(t_l.ap().rearrange(pat_l, **kw))
            gs.append(t_g.ap().rearrange(pat_g, r=2, **kw))
            c.cc_list.append((f"{name}{j}", t_l, t_g, name))
        setattr(c, name + "_l", ls)
        setattr(c, name + "_g", gs)
        setattr(c, name + "_l_res", Res(name + "_l"))
        setattr(c, name + "_g_res", Res(name + "_g"))

    scratch("KTm", 4, 2 * 96, TOK, "(h d) t -> h d t", "(r h d) t -> r h d t", h=2)
    scratch("Vm", 4, 2 * 128, NT * 66, "(h p) (t e) -> h p t e", "(r h p) (t e) -> r h p t e", h=2, t=NT)
    scratch("KTs", 1, 2 * 64, TOK, "(h d) t -> h d t", "(r h d) t -> r h d t", h=2)
    scratch("Vs", 1, NT * 128, 130, "(t p) e -> t p e", "(r t p) e -> r t p e", t=NT)
    scratch("KTn", 1, 4 * 64, TOK, "(h d) t -> h d t", "(r h d) t -> r h d t", h=4)
    scratch("Vn", 2, (NT // 2) * 128, 260, "(t p) e -> t p e", "(r t p) e -> r t p e", t=NT // 2)
    for nm, shp in (("QTm", [8, 96, TOK]), ("QTs", [4, 64, TOK]), ("QTn", [4, 64, TOK])):
        setattr(c, nm, nc.dram_tensor(nm, shp, BF16, **dk).ap())
        setattr(c, nm + "_res", Res(nm))
    c.wres = Res("weights")
    c.xin_res = [Res(f"xin{t}") for t in range(NT)]
    c.xs_res = [Res(f"xs{t}") for t in range(NT)]
    c.y_res = [Res(f"y{t}") for t in range(NT)]
    with contextlib.ExitStack() as stack:
        arena_t = stack.enter_context(nc.sbuf_tensor("arena", [128, ARENA_BYTES // 4], F32))
        psum = stack.enter_context(nc.psum_tensor("ps", [128, 4096], F32))
        P = Prog(nc, stack)
        A = Arena(arena_t, ARENA_BYTES)
        k = K(nc, P, A, psum)
        ds_c = P.dsem("const")
        c.ident_f = A.alloc("ident_f", 128)
        c.ident_bf = A.alloc("ident_bf", 128, BF16)
        c.cos = A.alloc("cos", NT * 16)
        c.sin = A.alloc("sin", NT * 16)
        k.dma("sp", c.ident_f, T(c.ident_d, c.wres), ds_c)
        k.dma("pool", c.ident_bf, T(c.ident_d, c.wres), ds_c)
        k.dma("sp", c.cos.r("p (t d) -> p t d", t=NT), T(c.cos_d.rearrange("(t p) d -> p t d", p=128), c.wres), ds_c)
        k.dma("sp", c.sin.r("p (t d) -> p t d", t=NT), T(c.sin_d.rearrange("(t p) d -> p t d", p=128), c.wres), ds_c)
        w = c.w
        if stop == "ffn1":
            ffn_phase(k, c, c.x_in, c.xin_res, c.y, c.y_res, w["ffn1_w_in"][0], w["ffn1_w_out"][0], w["ffn1_norm"][0])
        else:
            for l in range(nlayers):
                last = (l == nlayers - 1)
                src, src_res = (c.x_in, c.xin_res) if l == 0 else (c.xs, c.xs_res)
                ffn_phase(k, c, src, src_res, c.xs, c.xs_res, w["ffn1_w_in"][l], w["ffn1_w_out"][l], w["ffn1_norm"][l])
                phase_a(k, c, l)
                if stop in ("a", "a0", "a1", "a2", "a3") or stop.startswith("k"):
                    break
                phase_m(k, c, l)
                if stop == "m":
                    break
                phase_b(k, c, l)
                if stop == "b":
                    break
                dst, dst_res = (c.y, c.y_res) if last else (c.xs, c.xs_res)
                ffn_phase(k, c, c.xs, c.xs_res, dst, dst_res, w["ffn2_w_in"][l], w["ffn2_w_out"][l], w["ffn2_norm"][l],
                          fin_g_d=w["block_norm"][l])
        P.barrier()
        block = stack.enter_context(nc.Block())
        P.emit(block)
    return nc


def _swa_table(gq, gk):
    slopes = (2.0 ** (-8.0 * np.arange(1, 5, dtype=np.float32) / 4)).astype(np.float32)
    q = np.arange(128)[None, :]
    kk = np.arange(128)[:, None]
    dist = np.abs((gq * 128 + q) - (gk * 128 + kk))
    valid = (dist <= 128) & (0 <= gk < SEQ // 128)
    val = -slopes[None, :, None] * dist[:, None, :].astype(np.float32)
    out = np.where(valid[:, None, :], val, np.float32(NEG)).astype(np.float32)
    return out.reshape(128, 512)


def _na_table(rb, gq, gk):
    rows = SEQ // GW
    q = np.arange(128)[None, :]
    kk = np.arange(128)[:, None]
    qrow, qcol = 2 * gq + q // 64, q % 64
    krow, kcol = 2 * gk + kk // 64, kk % 64
    r0 = np.clip(qrow - 4, 0, rows - 8)
    c0 = np.clip(qcol - 8, 0, GW - 16)
    valid = (krow >= r0) & (krow < r0 + 8) & (kcol >= c0) & (kcol < c0 + 16) & (0 <= gk < SEQ // 128)
    dr = np.clip(krow - qrow + 7, 0, 14)
    dc = np.clip(kcol - qcol + 15, 0, 30)
    g = rb[:, dr, dc]
    out = np.where(valid[None], g, np.float32(NEG)).astype(np.float32)
    return np.ascontiguousarray(out.transpose(1, 0, 2)).reshape(128, 512)


def make_tables(inputs, h):
    rb = np.asarray(inputs["na_rel_bias"], dtype=np.float32)
    gmid = 32 * h + 10
    swa = np.stack([_swa_table(gmid, gmid - 1), _swa_table(gmid, gmid), _swa_table(gmid, gmid + 1),
                    _swa_table(32 * h, 32 * h - 1), _swa_table(32 * h + 31, 32 * h + 32)])
    na_gen = np.stack([np.stack([_na_table(rb[l], gmid, gmid + i) for i in range(-2, 3)]) for l in range(DEPTH)])
    na_edge = np.stack([np.stack([np.stack([_na_table(rb[l], 32 * h + t, 32 * h + t + i) for i in range(-3, 4)])
                                  for t in (0, 1, NT - 2, NT - 1)]) for l in range(DEPTH)])
    pos = (h * TOK + np.arange(TOK, dtype=np.float32)).astype(np.float32)
    inv = (1.0 / (np.float32(10000.0) ** (np.arange(0, 32, 2, dtype=np.float32) / np.float32(32)))).astype(np.float32)
    ang = (pos[:, None] * inv[None, :]).astype(np.float32)
    return {"swa_tab": swa, "na_gen": na_gen, "na_edge": na_edge,
            "cos": np.cos(ang).astype(np.float32), "sin": np.sin(ang).astype(np.float32)}


def make_in_maps(inputs):
    x = np.ascontiguousarray(np.asarray(inputs["x"], dtype=np.float32))
    mem = np.ascontiguousarray(np.asarray(inputs["mem"], dtype=np.float32))
    ident = np.eye(128, dtype=np.float32)
    ws = {n: np.ascontiguousarray(np.asarray(inputs[n], dtype=np.float32)) for n in WNAMES}
    tabs = [make_tables(inputs, 0), make_tables(inputs, 1)]
    maps = []
    for core in range(8):
        b, h = core // 2, core % 2
        m = {"x": x[b, h * TOK:(h + 1) * TOK], "mem": mem[b], "ident": ident}
        m.update(tabs[h])
        m.update(ws)
        maps.append(m)
    return maps


def kernel(**inputs):
    nc = build("full")
    res = run_bass_kernel_spmd(nc, make_in_maps(inputs), core_ids=list(range(8)))
    out = np.empty((BATCH, SEQ, D), np.float32)
    for core in range(8):
        b, h = core // 2, core % 2
        out[b, h * TOK:(h + 1) * TOK] = np.asarray(res.results[core]["y"])
    return out
```

```python
import contextlib
import numpy as np
import concourse.bass as bass
import concourse.mybir as mybir
from concourse.bass_utils import run_bass_kernel_spmd

F32 = mybir.dt.float32
BF16 = mybir.dt.bfloat16
AF = mybir.ActivationFunctionType
ALU = mybir.AluOpType
AX = mybir.AxisListType

D = 1024
BATCH = 4
SEQ = 8192
DEPTH = 2
NMEM = 256
GW = 64
EPS = 1e-6
DFF = 2816
NFC = DFF // 128
TOK = SEQ // 2
NT = TOK // 128
NEG = -30000.0

MLA_H = 8
MLA_QL = 256
MLA_KVL = 128
MLA_NOPE = 64
MLA_ROPE = 32
MLA_QK = 96
MLA_V = 64
MIX_IN = 1696


class Res:
    __slots__ = ("name", "w", "rs")

    def __init__(self, name):
        self.name = name
        self.w = None
        self.rs = []


class DSem:
    __slots__ = ("sem", "issued", "inc")

    def __init__(self, sem):
        self.sem = sem
        self.issued = 0
        self.inc = 16


class Ins:
    __slots__ = ("eng", "fn", "waits", "cdeps", "seq", "marked", "dsem")

    def __init__(self, eng, fn):
        self.eng = eng
        self.fn = fn
        self.waits = []
        self.cdeps = []
        self.seq = 0
        self.marked = False
        self.dsem = None


class T:
    __slots__ = ("ap", "res")

    def __init__(self, ap, res):
        self.ap = ap
        self.res = res

    def __getitem__(self, k):
        return T(self.ap[k], self.res)

    def r(self, pat, **kw):
        return T(self.ap.rearrange(pat, **kw), self.res)

    def bc(self, dt):
        return T(self.ap.bitcast(dt), self.res)

    def bcast(self, axis, shape):
        return T(self.ap.unsqueeze(axis).to_broadcast(shape), self.res)


ENGS = ("pe", "act", "dve", "pool", "sp")


class Prog:
    def __init__(self, nc, stack):
        self.nc = nc
        self.stack = stack
        self.q = {e: [] for e in ENGS}
        self.esem = {e: stack.enter_context(nc.semaphore("es_" + e)) for e in ENGS}
        self.dsems = []
        self.dsem_by_name = {}
        self.locks = {}
        self.last = {e: None for e in ENGS}

    def dsem(self, name, inc=16):
        if name in self.dsem_by_name:
            return self.dsem_by_name[name]
        d = DSem(self.stack.enter_context(self.nc.semaphore("ds_" + name)))
        d.inc = inc
        self.dsems.append(d)
        self.dsem_by_name[name] = d
        return d

    def op(self, eng, fn, reads=(), writes=(), dsem=None):
        ins = Ins(eng, fn)
        ins.dsem = dsem
        if eng in ("act", "dve"):
            locks = []
            for x in list(reads) + list(writes):
                rr = x.res if isinstance(x, T) else x
                if rr.name.startswith("bank"):
                    lk = self.locks.setdefault(rr.name, Res("lock_" + rr.name))
                    if lk not in locks:
                        locks.append(lk)
            writes = list(writes) + locks
        deps = []
        for r in reads:
            r = r.res if isinstance(r, T) else r
            if r.w is not None:
                deps.append(r.w)
        for w in writes:
            w = w.res if isinstance(w, T) else w
            if w.w is not None:
                deps.append(w.w)
            deps.extend(w.rs)
        for r in reads:
            r = r.res if isinstance(r, T) else r
            r.rs.append(ins)
        for w in writes:
            w = w.res if isinstance(w, T) else w
            w.w = ins
            w.rs = []
        seen = set()
        for d in deps:
            if d is ins or id(d) in seen:
                continue
            seen.add(id(d))
            if d.dsem is not None:
                ins.waits.append((d.dsem, d.dsem.issued * d.dsem.inc))
            elif d.eng == eng and eng == "pe":
                continue
            else:
                d.marked = True
                ins.cdeps.append(d)
        if dsem is not None:
            dsem.issued += 1
        else:
            self.last[eng] = ins
        self.q[eng].append(ins)
        return ins

    def barrier(self):
        lasts = [self.last[e] for e in ENGS if self.last[e] is not None]
        for e in ENGS:
            ins = Ins(e, None)
            for d in lasts:
                if d.eng != e:
                    d.marked = True
                    ins.cdeps.append(d)
            for ds in self.dsems:
                if ds.issued:
                    ins.waits.append((ds, ds.issued * ds.inc))
            self.q[e].append(ins)

    def emit(self, block):
        for e in ENGS:
            n = 0
            for ins in self.q[e]:
                if ins.marked:
                    n += 1
                    ins.seq = n
        esem = self.esem

        def run(ename):
            def body(eng):
                waited = {}
                for ins in self.q[ename]:
                    ws = {}
                    for ds, v in ins.waits:
                        k = id(ds.sem)
                        if v > ws.get(k, (None, 0))[1]:
                            ws[k] = (ds.sem, v)
                    for d in ins.cdeps:
                        s = esem[d.eng]
                        k = id(s)
                        if d.seq > ws.get(k, (None, 0))[1]:
                            ws[k] = (s, d.seq)
                    for k, (s, v) in ws.items():
                        if waited.get(k, 0) >= v:
                            continue
                        waited[k] = v
                        eng.wait_ge(s, v)
                    if ins.fn is None:
                        continue
                    bi = ins.fn(eng)
                    if ins.dsem is not None:
                        bi.then_inc(ins.dsem.sem, ins.dsem.inc)
                    elif ins.marked:
                        bi.then_inc(esem[ename], 1)
            return body

        block.tensor(run("pe"))
        block.scalar(run("act"))
        block.vector(run("dve"))
        block.gpsimd(run("pool"))
        block.sync(run("sp"))


class Arena:
    def __init__(self, tensor, nbytes):
        self.t = tensor
        self.n = nbytes
        self.off = 0

    def mark(self):
        return self.off

    def release(self, m):
        self.off = m

    def alloc(self, name, cols, dt=F32, parts=128):
        esz = 4 if dt == F32 else 2
        nb = (cols * esz + 63) // 64 * 64
        assert self.off + nb <= self.n, f"SBUF arena overflow at {name}: {self.off}+{nb}>{self.n}"
        a = self.t[0:parts, self.off // 4:(self.off + nb) // 4]
        self.off += nb
        if dt != F32:
            a = a.bitcast(dt)
        a = a[:, 0:cols]
        return T(a, Res(name))


class K:
    def __init__(self, nc, P, arena, psum):
        self.nc = nc
        self.P = P
        self.A = arena
        self.psum = psum
        self.bank = [T(psum[:, b * 512:(b + 1) * 512], Res(f"bank{b}")) for b in range(8)]

    def dma(self, q, out, in_, dsem, reads=None, writes=None, slow=False):
        rd = [in_] if reads is None else reads
        wr = [out] if writes is None else writes
        if slow:
            return self.P.op(q, lambda e: e.dma_start(out=out.ap, in_=in_.ap, allow_slow_non_contiguous=True),
                             reads=rd, writes=wr, dsem=dsem)
        return self.P.op(q, lambda e: e.dma_start(out=out.ap, in_=in_.ap), reads=rd, writes=wr, dsem=dsem)

    def mm(self, out, lhsT, rhs, start=True, stop=True):
        return self.P.op("pe", lambda e: e.matmul(out.ap, lhsT.ap, rhs.ap, start=start, stop=stop),
                         reads=[lhsT, rhs], writes=[out])

    def tr(self, out, in_, ident):
        return self.P.op("pe", lambda e: e.transpose(out.ap, in_.ap, ident.ap), reads=[in_, ident], writes=[out])

    def act(self, out, in_, func, scale=1.0, bias=0.0, accum=None):
        rd = [in_]
        wr = [out]
        sc = scale
        if isinstance(scale, T):
            rd.append(scale)
            sc = scale.ap
        bi = bias
        if isinstance(bias, T):
            rd.append(bias)
            bi = bias.ap
        if accum is not None:
            wr.append(accum)
            return self.P.op("act", lambda e: e.activation(out=out.ap, in_=in_.ap, func=func, bias=bi, scale=sc,
                                                           accum_out=accum.ap), reads=rd, writes=wr)
        return self.P.op("act", lambda e: e.activation(out=out.ap, in_=in_.ap, func=func, bias=bi, scale=sc),
                         reads=rd, writes=wr)

    def tt(self, out, a, b, op, eng="dve"):
        return self.P.op(eng, lambda e: e.tensor_tensor(out=out.ap, in0=a.ap, in1=b.ap, op=op), reads=[a, b], writes=[out])

    def ts(self, out, a, s1, op0, s2=None, op1=None, eng="dve", accum=None):
        rd = [a]
        v1 = s1
        if isinstance(s1, T):
            rd.append(s1)
            v1 = s1.ap
        v2 = s2
        if isinstance(s2, T):
            rd.append(s2)
            v2 = s2.ap
        wr = [out]
        kw = {}
        if op1 is not None:
            kw["op1"] = op1
        if accum is not None:
            wr.append(accum)
            kw["accum_out"] = accum.ap
        return self.P.op(eng, lambda e: e.tensor_scalar(out=out.ap, in0=a.ap, scalar1=v1, scalar2=v2, op0=op0, **kw),
                         reads=rd, writes=wr)

    def stt(self, out, a, s, b, op0, op1, accum=None):
        rd = [a, b]
        sv = s
        if isinstance(s, T):
            rd.append(s)
            sv = s.ap
        wr = [out]
        kw = {}
        if accum is not None:
            wr.append(accum)
            kw["accum_out"] = accum.ap
        return self.P.op("dve", lambda e: e.scalar_tensor_tensor(out=out.ap, in0=a.ap, scalar=sv, in1=b.ap, op0=op0,
                                                                 op1=op1, **kw), reads=rd, writes=wr)

    def copy(self, out, in_, eng="dve"):
        return self.P.op(eng, lambda e: e.tensor_copy(out=out.ap, in_=in_.ap), reads=[in_], writes=[out])

    def reduce(self, out, in_, op=ALU.add, axis=AX.X):
        return self.P.op("dve", lambda e: e.tensor_reduce(out=out.ap, in_=in_.ap, axis=axis, op=op), reads=[in_], writes=[out])

    def recip(self, out, in_):
        return self.P.op("dve", lambda e: e.reciprocal(out=out.ap, in_=in_.ap), reads=[in_], writes=[out])

    def memset(self, out, val, eng="dve"):
        return self.P.op(eng, lambda e: e.memset(out.ap, val), reads=[], writes=[out])

    def rstd(self, out, ss, invd):
        self.act(out, ss, AF.Sqrt, scale=invd, bias=EPS)
        self.recip(out, out)


class Ctx:
    pass


def norm_to_hT(k, c, xt_j, ss_col, rstd_col, xn, tbank, hT_dst, gT, junk):
    k.act(junk, xt_j, AF.Square)
    k.reduce(ss_col, junk)
    k.rstd(rstd_col, ss_col, 1.0 / D)
    k.ts(xn, xt_j, rstd_col, ALU.mult)
    tb = tbank.bc(BF16)
    for ch in range(8):
        k.tr(tb[:, ch * 128:(ch + 1) * 128], xn[:, ch * 128:(ch + 1) * 128], c.ident_bf)
    k.tt(hT_dst, tb.r("p (c t) -> p c t", c=8), gT.bcast(2, [128, 8, 128]), ALU.mult)


def ffn_phase(k, c, src, src_res, dst, dst_res, w_in_d, w_out_d, g_d, fin_g_d=None):
    A, P = k.A, k.P
    m = A.mark()
    Win = A.alloc("Win", 8 * 2 * DFF, BF16)
    Wout = A.alloc("Wout", NFC * D, BF16)
    gT = A.alloc("gT", 8)
    junk = A.alloc("junk", D)
    fing = A.alloc("fing", D) if fin_g_d is not None else None
    xts = [A.alloc(f"xt{i}", 2 * D) for i in range(2)]
    xns = [A.alloc(f"xn{i}", D, BF16) for i in range(2)]
    hTs = [A.alloc(f"hT{i}", 8 * 256, BF16) for i in range(2)]
    sgs = [A.alloc(f"sg{i}", 256) for i in range(2)]
    aTs = [A.alloc(f"aT{i}", 256, BF16) for i in range(3)]
    sst = A.alloc("sst", 8)
    ds_x = [P.dsem(f"ffx{i}") for i in range(2)]
    ds_g = P.dsem("ffg")
    Win3 = Win.r("p (c n) -> p c n", c=8)
    Wout3 = Wout.r("p (c n) -> p c n", c=NFC)
    k.dma("sp", gT, T(g_d.rearrange("(c p) -> p c", p=128), c.wres), ds_g, slow=True)
    if fing is not None:
        k.dma("sp", fing, T(fin_g_d.partition_broadcast(128), c.wres), ds_g)
    NB = NFC // 2
    win_res = [Res(f"Win{b}") for b in range(NB)]
    wout_res = [Res(f"Wout{b}") for b in range(NB)]
    w_in_v = w_in_d.rearrange("(c p) n -> p c n", p=128)
    w_out_v = w_out_d.rearrange("(c p) n -> p c n", p=128)
    for b in range(NB):
        dsi = P.dsem(f"ffwi{b}")
        for half in range(2):
            c0 = half * DFF + b * 256
            k.dma("pool", T(Win3.ap[:, :, c0:c0 + 256], win_res[b]), T(w_in_v[:, :, c0:c0 + 256], c.wres), dsi)
        k.dma("pool", T(Wout3.ap[:, 2 * b:2 * b + 2, :], wout_res[b]), T(w_out_v[:, 2 * b:2 * b + 2, :], c.wres),
              P.dsem(f"ffwo{b}"))
    src_t = src.rearrange("(t p) d -> p t d", p=128)
    dst_t = dst.rearrange("(t p) d -> p t d", p=128)
    NG = NT // 2
    gu_banks = [k.bank[0], k.bank[1]]
    tbanks = [k.bank[2], k.bank[3]]
    obanks = [k.bank[4], k.bank[5], k.bank[6], k.bank[7]]
    def ld(g):
        x3 = xts[g % 2].r("p (j d) -> p j d", j=2)
        k.dma("sp", x3, T(src_t[:, 2 * g:2 * g + 2, :], src_res[2 * g]), ds_x[g % 2],
              reads=[src_res[2 * g], src_res[2 * g + 1]])

    for g in range(NG):
        xt = xts[g % 2]
        hT = hTs[g % 2]
        xt3 = xt.r("p (j d) -> p j d", j=2)
        hT3 = hT.r("p (c t) -> p c t", c=8)
        if g == 0:
            ld(0)
        if g + 1 < NG:
            ld(g + 1)
        for j in range(2):
            norm_to_hT(k, c, xt3[:, j, :], sst[:, j:j + 1], sst[:, 2 + j:3 + j], xns[j], tbanks[j],
                       hT3[:, :, j * 128:(j + 1) * 128], gT, junk)

        def gu(fc):
            bk = gu_banks[fc % 2]
            for half in range(2):
                col = half * DFF + fc * 128
                for ch in range(8):
                    k.mm(bk[:, half * 256:(half + 1) * 256], T(Win3.ap[:, ch, col:col + 128], win_res[fc // 2]), hT3[:, ch, :],
                         start=(ch == 0), stop=(ch == 7))

        def outmm(fc):
            bk = gu_banks[fc % 2]
            sg = sgs[fc % 2]
            aT = aTs[fc % 3]
            k.act(sg, bk[:, 0:256], AF.Silu)
            k.tt(aT, sg, bk[:, 256:512], ALU.mult)
            for j in range(2):
                for dh in range(2):
                    k.mm(obanks[j * 2 + dh], aT[:, j * 128:(j + 1) * 128],
                         T(Wout3.ap[:, fc, dh * 512:(dh + 1) * 512], wout_res[fc // 2]),
                         start=(fc == 0), stop=(fc == NFC - 1))

        gu(0)
        for fc in range(NFC):
            if fc + 1 < NFC:
                gu(fc + 1)
            outmm(fc)
        for j in range(2):
            for dh in range(2):
                xs_ = xt3[:, j, dh * 512:(dh + 1) * 512]
                k.stt(xs_, obanks[j * 2 + dh], 0.5, xs_, ALU.mult, ALU.add)
            if fing is not None:
                xj = xt3[:, j, :]
                k.act(junk, xj, AF.Square)
                k.reduce(sst[:, 4 + j:5 + j], junk)
                k.rstd(sst[:, 6 + j:7 + j], sst[:, 4 + j:5 + j], 1.0 / D)
                k.stt(xj, xj, sst[:, 6 + j:7 + j], fing, ALU.mult, ALU.mult)
        k.dma("sp", T(dst_t[:, 2 * g:2 * g + 2, :], dst_res[2 * g]), xt3, ds_x[g % 2],
              writes=[dst_res[2 * g], dst_res[2 * g + 1]])
    P.barrier()
    A.release(m)


def sumsq_heads(k, src, n, hd, junk, ss):
    k.act(junk[:, 0:n * hd], src, AF.Square)
    k.reduce(ss, junk[:, 0:n * hd].r("p (h d) -> p h d", h=n))


def mla_head_path(k, c, f, g_rep, t, outb, junk, tmp, ss8, r8):
    f3 = f.r("p (h d) -> p h d", h=8)
    o3 = outb.r("p (h d) -> p h d", h=8)
    k.tt(junk[:, 0:768], f, f, ALU.mult)
    k.reduce(ss8, junk[:, 0:768].r("p (h d) -> p h d", h=8))
    k.rstd(r8, ss8, 1.0 / 96)
    k.tt(f3, f3, r8.bcast(2, [128, 8, 96]), ALU.mult)
    k.tt(f3, f3, g_rep.bcast(1, [128, 8, 96]), ALU.mult)
    cs = c.cos[:, t * 16:(t + 1) * 16].bcast(1, [128, 8, 16])
    sn = c.sin[:, t * 16:(t + 1) * 16].bcast(1, [128, 8, 16])
    x1 = f3[:, :, 64:80]
    x2 = f3[:, :, 80:96]
    t4 = tmp.r("p (a h d) -> p a h d", a=4, h=8)
    k.tt(t4[:, 0], x1, cs, ALU.mult)
    k.tt(t4[:, 1], x2, sn, ALU.mult)
    k.tt(t4[:, 2], x1, sn, ALU.mult)
    k.tt(t4[:, 3], x2, cs, ALU.mult)
    k.tt(o3[:, :, 64:80], t4[:, 0], t4[:, 1], ALU.subtract)
    k.tt(o3[:, :, 80:96], t4[:, 2], t4[:, 3], ALU.add)
    k.act(o3[:, :, 0:64], f3[:, :, 0:64], AF.Copy)


def phase_a(k, c, l):
    A, P = k.A, k.P
    w = c.w
    m = A.mark()
    Wmi = A.alloc("Wmi", 8 * 2048, BF16)
    Wuq = A.alloc("Wuq", 2 * 768, BF16)
    Wukv = A.alloc("Wukv", 1024, BF16)
    gT = A.alloc("gTmix", 8)
    gqn = A.alloc("gqn", 2)
    gkvn = A.alloc("gkvn", 1)
    gq_rep = A.alloc("gq_rep", 96)
    gk_rep = A.alloc("gk_rep", 96)
    gcol = A.alloc("gcol", 4)
    junk = A.alloc("junkA", D)
    tmp = A.alloc("tmpA", 4 * 8 * 16)
    sst = A.alloc("sstA", 64)
    xts = [A.alloc(f"xtA{i}", D) for i in range(2)]
    xn = A.alloc("xnA", D, BF16)
    hT = A.alloc("hTA", 8 * 128, BF16)
    cqn = A.alloc("cqn", 256, BF16)
    cqT = A.alloc("cqT", 256, BF16)
    ckvn = A.alloc("ckvn", 128, BF16)
    ckvT = A.alloc("ckvT", 128, BF16)
    qf = A.alloc("qf", 768)
    kf = A.alloc("kf", 768)
    qb = A.alloc("qb", 768, BF16)
    kb = A.alloc("kb", 768, BF16)
    qkb = A.alloc("qkb", 14 * 64, BF16)
    QTst = [A.alloc(f"QTst{i}", 8 * 128, BF16) for i in range(2)]
    KTst = [A.alloc(f"KTst{i}", 8 * 128, BF16) for i in range(2)]
    QKst = [A.alloc(f"QKst{i}", 14 * 128, BF16) for i in range(2)]
    Vmst = [A.alloc(f"Vmst{i}", 8 * 66, BF16) for i in range(2)]
    Vsst = [A.alloc(f"Vsst{i}", 2 * 65, BF16) for i in range(2)]
    Vnst = [A.alloc(f"Vnst{i}", 4 * 65, BF16) for i in range(2)]
    ds_w = P.dsem("aw")
    ds_x = [P.dsem(f"ax{i}") for i in range(2)]
    ds_st = [P.dsem(f"ast{i}") for i in range(2)]
    Wmi3 = Wmi.r("p (c n) -> p c n", c=8)
    Wuq3 = Wuq.r("p (c n) -> p c n", c=2)
    wmi = w["w_mix_in"][l].rearrange("(c p) n -> p c n", p=128)
    for (c0, s0, n) in ((0, 0, 416), (512, 416, 512), (1024, 928, 512), (1536, 1440, 256)):
        k.dma("pool", Wmi3[:, :, c0:c0 + n], T(wmi[:, :, s0:s0 + n], c.wres), ds_w)
    k.dma("pool", Wuq3, T(w["mla_w_uq"][l].rearrange("(c p) n -> p c n", p=128), c.wres), ds_w)
    k.dma("pool", Wukv, T(w["mla_w_ukv"][l], c.wres), ds_w)
    k.dma("sp", gT, T(w["mix_norm"][l].rearrange("(c p) -> p c", p=128), c.wres), ds_w, slow=True)
    k.dma("sp", gqn, T(w["mla_q_norm"][l].rearrange("(c p) -> p c", p=128), c.wres), ds_w, slow=True)
    k.dma("sp", gkvn, T(w["mla_kv_norm"][l].rearrange("(c p) -> p c", p=128), c.wres), ds_w, slow=True)
    k.dma("sp", gq_rep, T(w["mla_q_gain"][l].partition_broadcast(128), c.wres), ds_w)
    k.dma("sp", gk_rep, T(w["mla_k_gain"][l].partition_broadcast(128), c.wres), ds_w)
    for i, nm in enumerate(("swa_q_gain", "swa_k_gain", "na_q_gain", "na_k_gain")):
        k.dma("sp", gcol[0:64, i:i + 1], T(w[nm][l].rearrange("(p o) -> p o", o=1), c.wres), ds_w, slow=True)
    for i in range(2):
        for vt, nh, e in ((Vmst[i], 8, 66), (Vsst[i], 2, 65), (Vnst[i], 4, 65)):
            k.ts(vt.r("p (h e) -> p h e", h=nh)[:, :, 64:65], c.ident_f[:, 0:nh].r("p (h o) -> p h o", o=1),
                 0.0, ALU.mult, 1.0, ALU.add)
    xs_t = c.xs.rearrange("(t p) d -> p t d", p=128)
    B = k.bank

    def ld(t):
        k.dma("sp", xts[t % 2], T(xs_t[:, t, :], c.xs_res[t]), ds_x[t % 2])

    for t in range(NT):
        if t == 0:
            ld(0)
        if t + 1 < NT:
            ld(t + 1)
        xt = xts[t % 2]
        i2 = t % 2
        tsl = slice(t * 128, (t + 1) * 128)
        hT3 = hT.r("p (c t) -> p c t", c=8)
        norm_to_hT(k, c, xt, sst[:, 0:1], sst[:, 1:2], xn, B[4], hT3, gT, junk)
        for b, (c0, n) in enumerate(((0, 416), (512, 512), (1024, 512), (1536, 256))):
            for ch in range(8):
                k.mm(B[b][:, 0:n], hT3[:, ch, :], Wmi3[:, ch, c0:c0 + n], start=(ch == 0), stop=(ch == 7))
        if c.stop == "a1":
            continue
        sumsq_heads(k, B[0][:, 0:256], 1, 256, junk, sst[:, 2:3])
        k.rstd(sst[:, 3:4], sst[:, 2:3], 1.0 / 256)
        k.ts(cqn, B[0][:, 0:256], sst[:, 3:4], ALU.mult)
        tb4 = B[4].bc(BF16)
        for c2 in range(2):
            k.tr(tb4[:, c2 * 128:(c2 + 1) * 128], cqn[:, c2 * 128:(c2 + 1) * 128], c.ident_bf)
        k.tt(cqT.r("p (c t) -> p c t", c=2), tb4[:, 0:256].r("p (c t) -> p c t", c=2),
             gqn.bcast(2, [128, 2, 128]), ALU.mult)
        cqT3 = cqT.r("p (c t) -> p c t", c=2)
        for half in range(2):
            for c2 in range(2):
                k.mm(B[6 + half][:, 0:384], cqT3[:, c2, :], Wuq3[:, c2, half * 384:(half + 1) * 384],
                     start=(c2 == 0), stop=(c2 == 1))
        k.act(qf[:, 0:384], B[6][:, 0:384], AF.Copy)
        k.act(qf[:, 384:768], B[7][:, 0:384], AF.Copy)
        mla_head_path(k, c, qf, gq_rep, t, qb, junk, tmp, sst[:, 8:16], sst[:, 16:24])
        tb5 = B[5].bc(BF16)
        for h in range(8):
            k.tr(tb5[0:96, h * 128:(h + 1) * 128], qb[:, h * 96:(h + 1) * 96], c.ident_bf)
        k.copy(QTst[i2][0:96, :], tb5[0:96, :])
        k.dma("sp", T(c.QTm[:, :, tsl].rearrange("h d t -> d h t"), c.QTm_res),
              QTst[i2][0:96, :].r("p (h t) -> p h t", h=8), ds_st[i2])
        if c.stop == "a2":
            continue
        sumsq_heads(k, B[0][:, 256:384], 1, 128, junk, sst[:, 4:5])
        k.rstd(sst[:, 5:6], sst[:, 4:5], 1.0 / 128)
        k.ts(ckvn, B[0][:, 256:384], sst[:, 5:6], ALU.mult)
        if c.stop == "k0":
            continue
        k.tr(tb4[:, 0:128], ckvn, c.ident_bf)
        k.ts(ckvT, tb4[:, 0:128], gkvn[:, 0:1], ALU.mult)
        if c.stop == "k1":
            continue
        for half in range(2):
            k.mm(B[6 + half], ckvT, Wukv[:, half * 512:(half + 1) * 512])
        kf3 = kf.r("p (h d) -> p h d", h=8)
        vm3 = Vmst[i2].r("p (h e) -> p h e", h=8)
        if c.stop == "k2":
            continue
        for half in range(2):
            kv4 = B[6 + half].r("p (h d) -> p h d", h=4)
            if c.stop != "k3d":
                k.act(kf3[:, half * 4:(half + 1) * 4, 0:64], kv4[:, :, 0:64], AF.Copy)
            if c.stop != "k3a":
                k.copy(vm3[:, half * 4:(half + 1) * 4, 0:64], kv4[:, :, 64:128])
        if c.stop in ("k3", "k3a", "k3d"):
            continue
        if c.stop == "k4x":
            k.copy(tmp[:, 0:32], B[0][:, 384:416])
            continue
        if c.stop == "k4y":
            k.copy(kf3[:, :, 64:96], tmp[:, 0:32].bcast(1, [128, 8, 32]))
            continue
        k.copy(kf3[:, :, 64:96], B[0][:, 384:416].bcast(1, [128, 8, 32]))
        if c.stop == "k4":
            continue
        mla_head_path(k, c, kf, gk_rep, t, kb, junk, tmp, sst[:, 24:32], sst[:, 32:40])
        if c.stop == "k5":
            continue
        for h in range(8):
            k.tr(tb5[0:96, h * 128:(h + 1) * 128], kb[:, h * 96:(h + 1) * 96], c.ident_bf)
        k.copy(KTst[i2][0:96, :], tb5[0:96, :])
        if c.stop == "k6":
            continue
        kst3 = KTst[i2][0:96, :].r("p (h t) -> p h t", h=8)
        for j in range(4):
            k.dma("sp", T(c.KTm_l[j][:, :, tsl].rearrange("h d t -> d h t"), c.KTm_l_res),
                  kst3[:, 2 * j:2 * j + 2, :], ds_st[i2])
            k.dma("sp", T(c.Vm_l[j][:, :, t, :].rearrange("h p e -> p h e"), c.Vm_l_res),
                  vm3[:, 2 * j:2 * j + 2, :], ds_st[i2])
        if c.stop == "a3":
            continue
        k.act(junk[:, 0:384], B[1][:, 0:384], AF.Square)
        k.act(junk[:, 384:896], B[2][:, 0:512], AF.Square)
        k.reduce(sst[:, 40:54], junk[:, 0:896].r("p (h d) -> p h d", h=14))
        k.rstd(sst[:, 40:54], sst[:, 40:54], 1.0 / 64)
        qkb3 = qkb.r("p (h d) -> p h d", h=14)
        k.tt(qkb3[:, 0:6, :], B[1][:, 0:384].r("p (h d) -> p h d", h=6), sst[:, 40:46].bcast(2, [128, 6, 64]), ALU.mult)
        k.tt(qkb3[:, 6:14, :], B[2][:, 0:512].r("p (h d) -> p h d", h=8), sst[:, 46:54].bcast(2, [128, 8, 64]), ALU.mult)
        st = QKst[i2]
        for h in range(8):
            k.tr(tb5[0:64, h * 128:(h + 1) * 128], qkb[:, h * 64:(h + 1) * 64], c.ident_bf)
        for h in range(8, 14):
            k.tr(tb4[0:64, (h - 8) * 128:(h - 7) * 128], qkb[:, h * 64:(h + 1) * 64], c.ident_bf)
        k.ts(st[0:64, 0:512], tb5[0:64, 0:512], gcol[0:64, 0:1], ALU.mult)
        k.ts(st[0:64, 512:768], tb5[0:64, 512:768], gcol[0:64, 1:2], ALU.mult)
        k.ts(st[0:64, 768:1024], tb5[0:64, 768:1024], gcol[0:64, 2:3], ALU.mult)
        k.ts(st[0:64, 1024:1280], tb4[0:64, 0:256], gcol[0:64, 2:3], ALU.mult)
        k.ts(st[0:64, 1280:1792], tb4[0:64, 256:768], gcol[0:64, 3:4], ALU.mult)
        st3 = st[0:64, :].r("p (h t) -> p h t", h=14)
        k.dma("sp", T(c.QTs[:, :, tsl].rearrange("h d t -> d h t"), c.QTs_res), st3[:, 0:4, :], ds_st[i2])
        k.dma("sp", T(c.KTs_l[0][:, :, tsl].rearrange("h d t -> d h t"), c.KTs_l_res), st3[:, 4:6, :], ds_st[i2])
        k.dma("sp", T(c.QTn[:, :, tsl].rearrange("h d t -> d h t"), c.QTn_res), st3[:, 6:10, :], ds_st[i2])
        k.dma("sp", T(c.KTn_l[0][:, :, tsl].rearrange("h d t -> d h t"), c.KTn_l_res), st3[:, 10:14, :], ds_st[i2])
        vs3 = Vsst[i2].r("p (h e) -> p h e", h=2)
        vn3 = Vnst[i2].r("p (h e) -> p h e", h=4)
        k.copy(vs3[:, :, 0:64], B[1][:, 384:512].r("p (h d) -> p h d", h=2))
        k.copy(vn3[:, :, 0:64], B[3][:, 0:256].r("p (h d) -> p h d", h=4))
        k.dma("sp", T(c.Vs_l[0][t], c.Vs_l_res), Vsst[i2], ds_st[i2])
        k.dma("sp", T(c.Vn_l[t // (NT // 2)][t % (NT // 2)], c.Vn_l_res), Vnst[i2], ds_st[i2])
    P.barrier()
    A.release(m)
    if c.stop in ("a0", "a1", "a2", "a3") or c.stop.startswith("k"):
        return
    for (cn, t_l, t_g, nm) in c.cc_list:
        ds = P.dsem(f"cc_{cn}_{l}", inc=1)
        P.op("pool", (lambda s_, d_: (lambda e: e.collective_compute(
            "AllGather", ALU.bypass, replica_groups=[[0, 1], [2, 3], [4, 5], [6, 7]][:c.ncores // 2],
            ins=[s_.ap().opt()], outs=[d_.ap().opt()])))(t_l, t_g),
            reads=[getattr(c, nm + "_l_res")], writes=[getattr(c, nm + "_g_res")], dsem=ds)
    P.barrier()


def phase_m(k, c, l):
    A, P = k.A, k.P
    m = A.mark()
    KTs_ = [A.alloc(f"KTm{i}", 2 * TOK, BF16) for i in range(2)]
    VAs = [A.alloc(f"VAm{i}", 2 * NT * 66, BF16) for i in range(2)]
    QTs_ = [A.alloc(f"QTm{i}", TOK, BF16) for i in range(2)]
    PTs = [A.alloc(f"PT{i}", 1024, BF16) for i in range(3)]
    OTs = [A.alloc(f"OT{i}", 512) for i in range(2)]
    ost = [A.alloc(f"ost{i}", 4 * 64) for i in range(2)]
    rc = A.alloc("rcm", 8)
    ds_h = [P.dsem(f"mh{i}") for i in range(2)]
    ds_o = [P.dsem(f"mo{i}") for i in range(2)]
    B = k.bank
    scale = 96 ** -0.5
    om = c.omla.rearrange("(t p) h e -> p t h e", p=128)

    def ld(h):
        i = h % 2
        k.dma("sp", KTs_[i][0:96, :].r("p (r t) -> p r t", r=2), T(c.KTm_g[h // 2][:, h % 2].rearrange("r d t -> d r t"), c.KTm_g_res), ds_h[i])
        k.dma("sp", VAs[i].r("p (r t e) -> p r t e", r=2, t=NT), T(c.Vm_g[h // 2][:, h % 2].rearrange("r p t e -> p r t e"), c.Vm_g_res), ds_h[i])
        k.dma("sp", QTs_[i][0:96, :], T(c.QTm[h], c.QTm_res), ds_h[i])

    cnt = [0]
    pend = []

    def fin_pe(args):
        qg, h, ob, n = args
        OT = OTs[n % 2]
        tb = B[5]
        for j in range(4):
            k.tr(tb[:, j * 65:(j + 1) * 65], OT[0:65, j * 128:(j + 1) * 128], c.ident_f[0:65, 0:65])
        t3 = tb[:, 0:260].r("p (j e) -> p j e", j=4)
        k.recip(rc[:, 0:4], t3[:, :, 64])
        o3 = ost[n % 2].r("p (j e) -> p j e", j=4)
        k.tt(o3, t3[:, :, 0:64], rc[:, 0:4].bcast(2, [128, 4, 64]), ALU.mult)
        k.dma("sp", T(om[:, qg * 4:(qg + 1) * 4, h, :], c.omla_res), o3, ds_o[n % 2])

    items = [(h, qg, kp) for h in range(MLA_H) for qg in range(8) for kp in range(32)]

    def bufs(h):
        return KTs_[h % 2], VAs[h % 2].r("p (r t e) -> p r t e", r=2, t=NT), QTs_[h % 2]

    def S_(it, idx):
        h, qg, kp = it
        KT, VA, QT = bufs(h)
        b0 = (idx % 2) * 2
        for j in range(2):
            kt = 2 * kp + j
            r, lt = kt // NT, kt % NT
            k.mm(B[b0 + j], KT[0:96, r * TOK + lt * 128:r * TOK + (lt + 1) * 128], QT[0:96, qg * 512:(qg + 1) * 512])

    def EXP_(it, idx):
        b0 = (idx % 2) * 2
        PT = PTs[idx % 3]
        src2 = T(k.psum[:, b0 * 512:(b0 + 2) * 512], B[b0].res)
        P.op("act", (lambda o_, i_: (lambda e: e.activation(out=o_.ap, in_=i_.ap, func=AF.Exp, scale=scale)))(PT, src2),
             reads=[B[b0], B[b0 + 1]], writes=[PT])

    def PV_(it, idx, ob):
        h, qg, kp = it
        KT, VA, QT = bufs(h)
        PT = PTs[idx % 3]
        for j in range(2):
            kt = 2 * kp + j
            r, lt = kt // NT, kt % NT
            k.mm(ob[0:65, :], VA[:, r, lt, 0:65], PT[:, j * 512:(j + 1) * 512], start=(kt == 0), stop=(kt == 63))

    ld(0)
    S_(items[0], 0)
    for idx, it in enumerate(items):
        h, qg, kp = it
        if qg == 0 and kp == 0 and h + 1 < MLA_H:
            ld(h + 1)
        n = h * 8 + qg
        ob = B[6 + n % 2]
        if idx + 1 < len(items):
            S_(items[idx + 1], idx + 1)
        EXP_(it, idx)
        PV_(it, idx, ob)
        if kp == 31:
            k.copy(OTs[n % 2][0:65, :], ob[0:65, :])
            if pend:
                fin_pe(pend.pop())
            pend.append((qg, h, ob, n))
    fin_pe(pend.pop())
    P.barrier()
    A.release(m)


def phase_b(k, c, l):
    A, P = k.A, k.P
    w = c.w
    m = A.mark()
    B = k.bank
    Wmo = A.alloc("Wmo", 8 * D, BF16)
    Wq = A.alloc("Wq", 8 * 256, BF16)
    Wo = A.alloc("Wo", 2 * D, BF16)
    Wkv = A.alloc("Wkv", 8 * 512, BF16)
    gT_grp = A.alloc("gT_grp", 8)
    gT_mx = A.alloc("gT_mx", 8)
    gT_mm = A.alloc("gT_mm", 8)
    gcm = A.alloc("gcm", 2)
    esink = A.alloc("esink", 4)
    KTmem = A.alloc("KTmem", 4 * 256, BF16)
    VAmem = A.alloc("VAmem", 2 * 4 * 65, BF16)
    swat = A.alloc("swat", 5 * 512)
    nagen = A.alloc("nagen", 5 * 512)
    naedge = A.alloc("naedge", 7 * 512)
    junk = A.alloc("junkB", D)
    sst = A.alloc("sstB", 32)
    xn = A.alloc("xnB", D, BF16)
    hT = A.alloc("hTB", 8 * 128, BF16)
    ocb = A.alloc("ocb", D, BF16)
    oT = A.alloc("oTB", 8 * 128, BF16)
    qmb = A.alloc("qmb", 256, BF16)
    QTmem = A.alloc("QTmem", 4 * 128, BF16)
    omb = A.alloc("omb", 256, BF16)
    omf = A.alloc("omf", 256)
    omT = A.alloc("omT", 2 * 128, BF16)
    sbs = [A.alloc(f"sb{i}", 512) for i in range(2)]
    PTs = [A.alloc(f"PTb{i}", 512, BF16) for i in range(7)]
    xts = [A.alloc(f"xtB{i}", D) for i in range(2)]
    ocs = [A.alloc(f"oc{i}", D) for i in range(2)]
    QTs_ = [A.alloc(f"QTsB{i}", 4 * 128, BF16) for i in range(2)]
    KTs_ = [A.alloc(f"KTsB{i}", 2 * 3 * 128, BF16) for i in range(2)]
    Vs_ = [A.alloc(f"VsB{i}", 3 * 130, BF16) for i in range(2)]
    QTn_ = [A.alloc(f"QTnB{i}", 4 * 128, BF16) for i in range(2)]
    KTn_ = [A.alloc(f"KTnB{i}", 4 * 7 * 128, BF16) for i in range(2)]
    Vn_ = [A.alloc(f"VnB{i}", 7 * 260, BF16) for i in range(2)]
    ds_w = P.dsem("bw")
    ds_t = [P.dsem(f"bt{i}") for i in range(2)]
    ds_e = P.dsem("be")
    ds_s = [P.dsem(f"bs{i}") for i in range(2)]
    Wmo3 = Wmo.r("p (c n) -> p c n", c=8)
    Wq3 = Wq.r("p (c n) -> p c n", c=8)
    Wo3 = Wo.r("p (c n) -> p c n", c=2)
    Wkv3 = Wkv.r("p (c n) -> p c n", c=8)
    k.dma("pool", Wmo3, T(w["w_mix_out"][l].rearrange("(c p) n -> p c n", p=128), c.wres), ds_w)
    k.dma("pool", Wq3, T(w["mem_w_q"][l].rearrange("(c p) n -> p c n", p=128), c.wres), ds_w)
    k.dma("pool", Wo3, T(w["mem_w_o"][l].rearrange("(c p) n -> p c n", p=128), c.wres), ds_w)
    k.dma("pool", Wkv3, T(w["mem_w_kv"][l].rearrange("(c p) n -> p c n", p=128), c.wres), ds_w)
    for tl, nm in ((gT_grp, "grp_out_gain"), (gT_mx, "mem_norm_x"), (gT_mm, "mem_norm_m")):
        k.dma("sp", tl, T(w[nm][l].rearrange("(c p) -> p c", p=128), c.wres), ds_w, slow=True)
    k.dma("sp", gcm[0:64, 0:1], T(w["mem_q_gain"][l].rearrange("(p o) -> p o", o=1), c.wres), ds_w, slow=True)
    k.dma("sp", gcm[0:64, 1:2], T(w["mem_k_gain"][l].rearrange("(p o) -> p o", o=1), c.wres), ds_w, slow=True)
    k.dma("sp", esink, T(w["swa_sink"][l].partition_broadcast(128), c.wres), ds_w)
    k.dma("sp", swat.r("p (i n) -> p i n", i=5), T(c.swa_tab.rearrange("i p n -> p i n"), c.wres), ds_w)
    k.dma("sp", nagen.r("p (i n) -> p i n", i=5), T(c.na_gen[l].rearrange("i p n -> p i n"), c.wres), ds_w)
    k.act(esink, esink, AF.Exp)
    k.ts(VAmem.r("p (m e) -> p m e", m=8)[:, :, 64:65], c.ident_f[:, 0:8].r("p (h o) -> p h o", o=1),
         0.0, ALU.mult, 1.0, ALU.add)
    va4 = VAmem.r("p (m h e) -> p m h e", m=2, h=4)
    hT3 = hT.r("p (c t) -> p c t", c=8)
    tb4 = B[4].bc(BF16)
    for mt in range(2):
        k.dma("sp", xts[mt], T(c.mem[mt * 128:(mt + 1) * 128, :], c.wres), ds_t[mt])
        norm_to_hT(k, c, xts[mt], sst[:, 0:1], sst[:, 1:2], xn, B[4], hT3, gT_mm, junk)
        for ch in range(8):
            k.mm(B[5], hT3[:, ch, :], Wkv3[:, ch, :], start=(ch == 0), stop=(ch == 7))
        sumsq_heads(k, B[5][:, 0:256], 4, 64, junk, sst[:, 2:6])
        k.rstd(sst[:, 2:6], sst[:, 2:6], 1.0 / 64)
        k.tt(qmb.r("p (h d) -> p h d", h=4), B[5][:, 0:256].r("p (h d) -> p h d", h=4),
             sst[:, 2:6].bcast(2, [128, 4, 64]), ALU.mult)
        for hd in range(4):
            k.tr(tb4[0:64, hd * 128:(hd + 1) * 128], qmb[:, hd * 64:(hd + 1) * 64], c.ident_bf)
        k.ts(KTmem[0:64, :].r("p (h m) -> p h m", h=4)[:, :, mt * 128:(mt + 1) * 128],
             tb4[0:64, 0:512].r("p (h m) -> p h m", h=4), gcm[0:64, 1:2], ALU.mult)
        k.copy(va4[:, mt, :, 0:64], B[5][:, 256:512].r("p (h d) -> p h d", h=4))
    KTmem3 = KTmem[0:64, :].r("p (h m) -> p h m", h=4)
    xs_t = c.xs.rearrange("(t p) d -> p t d", p=128)
    om_t = c.omla.rearrange("(t p) h e -> p t (h e)", p=128)
    na_edge_t = {0: 0, 1: 1, NT - 2: 2, NT - 1: 3}

    def ktile_src(loc, gat, lt):
        if 0 <= lt < NT:
            return loc[0][:, :, lt * 128:(lt + 1) * 128]
        if lt < 0:
            return gat[0][0][:, :, (NT + lt) * 128:(NT + lt + 1) * 128]
        return gat[0][1][:, :, (lt - NT) * 128:(lt - NT + 1) * 128]

    def vtile_src(loc, gat, lt):
        nch = len(loc)
        per = NT // nch

        def pick(lst, tile, r=None):
            a = lst[tile // per]
            return a[tile % per] if r is None else a[r][tile % per]
        if 0 <= lt < NT:
            return pick(loc, lt)
        if lt < 0:
            return pick(gat, NT + lt, 0)
        return pick(gat, lt - NT, 1)

    def win_n(t):
        return list(range(-2, 3)) if t not in na_edge_t else list(range(-3, 4))

    def ld(t):
        i2 = t % 2
        ds = ds_t[i2]
        tsl = slice(t * 128, (t + 1) * 128)
        k.dma("sp", xts[i2], T(xs_t[:, t, :], c.xs_res[t]), ds)
        k.dma("sp", ocs[i2][:, 0:512], T(om_t[:, t, :], c.omla_res), ds)
        k.dma("sp", QTs_[i2][0:64, :].r("p (h t) -> p h t", h=4), T(c.QTs[:, :, tsl].rearrange("h d t -> d h t"), c.QTs_res), ds)
        k.dma("sp", QTn_[i2][0:64, :].r("p (h t) -> p h t", h=4), T(c.QTn[:, :, tsl].rearrange("h d t -> d h t"), c.QTn_res), ds)
        ks3 = KTs_[i2][0:64, :].r("p (h i t) -> p h i t", h=2, i=3)
        vs3 = Vs_[i2].r("p (i e) -> p i e", i=3)
        for j, i in enumerate((-1, 0, 1)):
            k.dma("sp", ks3[:, :, j, :], T(ktile_src(c.KTs_l, c.KTs_g, t + i).rearrange("h d t -> d h t"), c.KTs_l_res), ds,
                  reads=[c.KTs_l_res, c.KTs_g_res])
            k.dma("sp", vs3[:, j, :], T(vtile_src(c.Vs_l, c.Vs_g, t + i), c.Vs_l_res), ds, reads=[c.Vs_l_res, c.Vs_g_res])
        kn3 = KTn_[i2][0:64, :].r("p (h i t) -> p h i t", h=4, i=7)
        vn3 = Vn_[i2].r("p (i e) -> p i e", i=7)
        for j, i in enumerate(win_n(t)):
            k.dma("sp", kn3[:, :, j, :], T(ktile_src(c.KTn_l, c.KTn_g, t + i).rearrange("h d t -> d h t"), c.KTn_l_res), ds,
                  reads=[c.KTn_l_res, c.KTn_g_res])
            k.dma("sp", vn3[:, j, :], T(vtile_src(c.Vn_l, c.Vn_g, t + i), c.Vn_l_res), ds, reads=[c.Vn_l_res, c.Vn_g_res])

    sbc = [0]

    def attend(Q3, Ksrc, Vsrc, tabs, nk, hkv, scale, Ob):
        for j in range(nk):
            n = sbc[0]
            sbc[0] += 1
            Sb = B[n % 2]
            Kt = Ksrc(j)
            for hd in range(4):
                k.mm(Sb[:, hd * 128:(hd + 1) * 128], Kt[:, hd * hkv // 4, :], Q3[:, hd, :])
            PT = PTs[j]
            tab = tabs(j)
            if tab is not None:
                sb = sbs[n % 2]
                k.stt(sb, Sb, scale, tab, ALU.mult, ALU.add)
                k.act(PT, sb, AF.Exp)
            else:
                k.act(PT, Sb, AF.Exp, scale=scale)
        for hd in range(4):
            for j in range(nk):
                Vt = Vsrc(j)
                k.mm(Ob[:, hd * 65:(hd + 1) * 65], PTs[j][:, hd * 128:(hd + 1) * 128], Vt[:, hd * hkv // 4, :],
                     start=(j == 0), stop=(j == nk - 1))

    def finish(Ob, dst, den_add, rcol):
        o3 = Ob[:, 0:260].r("p (h e) -> p h e", h=4)
        if den_add is not None:
            k.tt(rcol, o3[:, :, 64], den_add, ALU.add)
            k.recip(rcol, rcol)
        else:
            k.recip(rcol, o3[:, :, 64])
        k.tt(dst.r("p (h d) -> p h d", h=4), o3[:, :, 0:64], rcol.bcast(2, [128, 4, 64]), ALU.mult)

    ld(0)
    for t in range(NT):
        if t + 1 < NT:
            ld(t + 1)
        i2 = t % 2
        xt = xts[i2]
        oc = ocs[i2]
        Qs3 = QTs_[i2][0:64, :].r("p (h t) -> p h t", h=4)
        ks3 = KTs_[i2][0:64, :].r("p (h i t) -> p h i t", h=2, i=3)
        vs4 = Vs_[i2].r("p (i h e) -> p i h e", i=3, h=2)
        sw3 = swat.r("p (i n) -> p i n", i=5)

        def swa_tab(j, t=t):
            if j == 0 and t == 0:
                return sw3[:, 3, :]
            if j == 2 and t == NT - 1:
                return sw3[:, 4, :]
            return sw3[:, j, :]

        attend(Qs3, lambda j: ks3[:, :, j, :], lambda j: vs4[:, j, :, :], swa_tab, 3, 2, 0.125, B[2])
        finish(B[2], oc[:, 512:768], esink, sst[:, 8:12])
        Qn3 = QTn_[i2][0:64, :].r("p (h t) -> p h t", h=4)
        kn3 = KTn_[i2][0:64, :].r("p (h i t) -> p h i t", h=4, i=7)
        vn4 = Vn_[i2].r("p (i h e) -> p i h e", i=7, h=4)
        if t in na_edge_t:
            ne3 = naedge.r("p (i n) -> p i n", i=7)
            k.dma("sp", ne3, T(c.na_edge[l, na_edge_t[t]].rearrange("i p n -> p i n"), c.wres), ds_e)
            nk = 7
            ntab = lambda j: ne3[:, j, :]
        else:
            ng3 = nagen.r("p (i n) -> p i n", i=5)
            nk = 5
            ntab = lambda j: ng3[:, j, :]
        attend(Qn3, lambda j: kn3[:, :, j, :], lambda j: vn4[:, j, :, :], ntab, nk, 4, 0.125, B[3])
        finish(B[3], oc[:, 768:1024], None, sst[:, 12:16])
        if c.debug:
            k.dma("sp", T(c.ocat_dbg.rearrange("(t p) d -> p t d", p=128)[:, t, :], Res("dbg")), oc, ds_s[i2])
        k.act(junk, oc, AF.Square)
        k.reduce(sst[:, 16:17], junk[:, 0:512])
        k.reduce(sst[:, 17:19], junk[:, 512:1024].r("p (g d) -> p g d", g=2))
        k.rstd(sst[:, 20:21], sst[:, 16:17], 1.0 / 512)
        k.rstd(sst[:, 21:23], sst[:, 17:19], 1.0 / 256)
        k.ts(ocb[:, 0:512], oc[:, 0:512], sst[:, 20:21], ALU.mult)
        k.ts(ocb[:, 512:768], oc[:, 512:768], sst[:, 21:22], ALU.mult)
        k.ts(ocb[:, 768:1024], oc[:, 768:1024], sst[:, 22:23], ALU.mult)
        for ch in range(8):
            k.tr(tb4[:, ch * 128:(ch + 1) * 128], ocb[:, ch * 128:(ch + 1) * 128], c.ident_bf)
        oT3 = oT.r("p (c t) -> p c t", c=8)
        k.tt(oT3, tb4.r("p (c t) -> p c t", c=8), gT_grp.bcast(2, [128, 8, 128]), ALU.mult)
        for dh in range(2):
            for ch in range(8):
                k.mm(B[6 + dh], oT3[:, ch, :], Wmo3[:, ch, dh * 512:(dh + 1) * 512], start=(ch == 0), stop=(ch == 7))
        for dh in range(2):
            xs_ = xt[:, dh * 512:(dh + 1) * 512]
            k.tt(xs_, xs_, B[6 + dh], ALU.add)
        norm_to_hT(k, c, xt, sst[:, 0:1], sst[:, 1:2], xn, B[4], hT3, gT_mx, junk)
        for ch in range(8):
            k.mm(B[5][:, 0:256], hT3[:, ch, :], Wq3[:, ch, :], start=(ch == 0), stop=(ch == 7))
        sumsq_heads(k, B[5][:, 0:256], 4, 64, junk, sst[:, 2:6])
        k.rstd(sst[:, 2:6], sst[:, 2:6], 1.0 / 64)
        k.tt(qmb.r("p (h d) -> p h d", h=4), B[5][:, 0:256].r("p (h d) -> p h d", h=4),
             sst[:, 2:6].bcast(2, [128, 4, 64]), ALU.mult)
        for hd in range(4):
            k.tr(tb4[0:64, hd * 128:(hd + 1) * 128], qmb[:, hd * 64:(hd + 1) * 64], c.ident_bf)
        k.ts(QTmem[0:64, :], tb4[0:64, 0:512], gcm[0:64, 0:1], ALU.mult)
        Qm3 = QTmem[0:64, :].r("p (h t) -> p h t", h=4)
        attend(Qm3, lambda j: KTmem3[:, :, j * 128:(j + 1) * 128], lambda j: va4[:, j, :, :], lambda j: None, 2, 4, 0.125, B[2])
        finish(B[2], omf, None, sst[:, 24:28])
        k.act(omb, omf, AF.Copy)
        for c2 in range(2):
            k.tr(tb4[:, c2 * 128:(c2 + 1) * 128], omb[:, c2 * 128:(c2 + 1) * 128], c.ident_bf)
        k.copy(omT, tb4[:, 0:256])
        omT3 = omT.r("p (c t) -> p c t", c=2)
        for dh in range(2):
            for c2 in range(2):
                k.mm(B[6 + dh], omT3[:, c2, :], Wo3[:, c2, dh * 512:(dh + 1) * 512], start=(c2 == 0), stop=(c2 == 1))
        for dh in range(2):
            xs_ = xt[:, dh * 512:(dh + 1) * 512]
            k.tt(xs_, xs_, B[6 + dh], ALU.add)
        k.dma("sp", T(xs_t[:, t, :], c.xs_res[t]), xt, ds_s[i2])
    P.barrier()
    A.release(m)


WNAMES = ["ffn1_norm", "ffn1_w_in", "ffn1_w_out", "mix_norm", "w_mix_in", "mla_q_norm", "mla_w_uq", "mla_kv_norm",
          "mla_w_ukv", "mla_q_gain", "mla_k_gain", "swa_q_gain", "swa_k_gain", "swa_sink", "na_q_gain", "na_k_gain",
          "grp_out_gain", "w_mix_out", "mem_norm_x", "mem_norm_m", "mem_w_q", "mem_w_kv", "mem_q_gain",
          "mem_k_gain", "mem_w_o", "ffn2_norm", "ffn2_w_in", "ffn2_w_out", "block_norm"]
WSHAPES = {
    "ffn1_norm": [DEPTH, D], "ffn1_w_in": [DEPTH, D, 2 * DFF], "ffn1_w_out": [DEPTH, DFF, D], "mix_norm": [DEPTH, D],
    "w_mix_in": [DEPTH, D, MIX_IN], "mla_q_norm": [DEPTH, 256], "mla_w_uq": [DEPTH, 256, 768],
    "mla_kv_norm": [DEPTH, 128], "mla_w_ukv": [DEPTH, 128, 1024], "mla_q_gain": [DEPTH, 96], "mla_k_gain": [DEPTH, 96],
    "swa_q_gain": [DEPTH, 64], "swa_k_gain": [DEPTH, 64], "swa_sink": [DEPTH, 4], "na_q_gain": [DEPTH, 64],
    "na_k_gain": [DEPTH, 64], "grp_out_gain": [DEPTH, D], "w_mix_out": [DEPTH, D, D], "mem_norm_x": [DEPTH, D],
    "mem_norm_m": [DEPTH, D], "mem_w_q": [DEPTH, D, 256], "mem_w_kv": [DEPTH, D, 512], "mem_q_gain": [DEPTH, 64],
    "mem_k_gain": [DEPTH, 64], "mem_w_o": [DEPTH, 256, D], "ffn2_norm": [DEPTH, D], "ffn2_w_in": [DEPTH, D, 2 * DFF],
    "ffn2_w_out": [DEPTH, DFF, D], "block_norm": [DEPTH, D],
}
ARENA_BYTES = 204800


def build(stop="full", nlayers=DEPTH, ncores=8, debug=False):
    nc = bass.Bass("TRN2", target_bir_lowering=False)
    c = Ctx()
    c.ncores = ncores
    c.debug = debug
    dk = {"kind": "ExternalOutput"} if debug else {}
    c.stop = stop
    c.x_in = nc.dram_tensor("x", [TOK, D], F32, kind="ExternalInput").ap()
    c.mem = nc.dram_tensor("mem", [NMEM, D], F32, kind="ExternalInput").ap()
    c.ident_d = nc.dram_tensor("ident", [128, 128], F32, kind="ExternalInput").ap()
    c.cos_d = nc.dram_tensor("cos", [TOK, 16], F32, kind="ExternalInput").ap()
    c.sin_d = nc.dram_tensor("sin", [TOK, 16], F32, kind="ExternalInput").ap()
    c.swa_tab = nc.dram_tensor("swa_tab", [5, 128, 512], F32, kind="ExternalInput").ap()
    c.na_gen = nc.dram_tensor("na_gen", [DEPTH, 5, 128, 512], F32, kind="ExternalInput").ap()
    c.na_edge = nc.dram_tensor("na_edge", [DEPTH, 4, 7, 128, 512], F32, kind="ExternalInput").ap()
    c.w = {n: nc.dram_tensor(n, WSHAPES[n], F32, kind="ExternalInput").ap() for n in WNAMES}
    c.y = nc.dram_tensor("y", [TOK, D], F32, kind="ExternalOutput").ap()
    c.xs = nc.dram_tensor("xs", [TOK, D], F32, **dk).ap()
    c.omla = nc.dram_tensor("omla", [TOK, 8, 64], F32, **dk).ap()
    if debug:
        c.ocat_dbg = nc.dram_tensor("ocat_dbg", [TOK, D], F32, **dk).ap()
    c.omla_res = Res("omla")

    c.cc_list = []

    def scratch(name, nchunk, rows, cols, pat_l, pat_g, **kw):
        ls, gs = [], []
        for j in range(nchunk):
            t_l = nc.dram_tensor(f"{name}_l{j}", [rows, cols], BF16)
            t_g = nc.dram_tensor(f"{name}_g{j}", [2 * rows, cols], BF16)
            ls.append(t_l.ap().rearrange(pat_l, **kw))
            gs.append(t_g.ap().rearrange(pat_g, r=2, **kw))
            c.cc_list.append((f"{name}{j}", t_l, t_g, name))
        setattr(c, name + "_l", ls)
        setattr(c, name + "_g", gs)
        setattr(c, name + "_l_res", Res(name + "_l"))
        setattr(c, name + "_g_res", Res(name + "_g"))

    scratch("KTm", 4, 2 * 96, TOK, "(h d) t -> h d t", "(r h d) t -> r h d t", h=2)
    scratch("Vm", 4, 2 * 128, NT * 66, "(h p) (t e) -> h p t e", "(r h p) (t e) -> r h p t e", h=2, t=NT)
    scratch("KTs", 1, 2 * 64, TOK, "(h d) t -> h d t", "(r h d) t -> r h d t", h=2)
    scratch("Vs", 1, NT * 128, 130, "(t p) e -> t p e", "(r t p) e -> r t p e", t=NT)
    scratch("KTn", 1, 4 * 64, TOK, "(h d) t -> h d t", "(r h d) t -> r h d t", h=4)
    scratch("Vn", 2, (NT // 2) * 128, 260, "(t p) e -> t p e", "(r t p) e -> r t p e", t=NT // 2)
    for nm, shp in (("QTm", [8, 96, TOK]), ("QTs", [4, 64, TOK]), ("QTn", [4, 64, TOK])):
        setattr(c, nm, nc.dram_tensor(nm, shp, BF16, **dk).ap())
        setattr(c, nm + "_res", Res(nm))
    c.wres = Res("weights")
    c.xin_res = [Res(f"xin{t}") for t in range(NT)]
    c.xs_res = [Res(f"xs{t}") for t in range(NT)]
    c.y_res = [Res(f"y{t}") for t in range(NT)]
    with contextlib.ExitStack() as stack:
        arena_t = stack.enter_context(nc.sbuf_tensor("arena", [128, ARENA_BYTES // 4], F32))
        psum = stack.enter_context(nc.psum_tensor("ps", [128, 4096], F32))
        P = Prog(nc, stack)
        A = Arena(arena_t, ARENA_BYTES)
        k = K(nc, P, A, psum)
        ds_c = P.dsem("const")
        c.ident_f = A.alloc("ident_f", 128)
        c.ident_bf = A.alloc("ident_bf", 128, BF16)
        c.cos = A.alloc("cos", NT * 16)
        c.sin = A.alloc("sin", NT * 16)
        k.dma("sp", c.ident_f, T(c.ident_d, c.wres), ds_c)
        k.dma("pool", c.ident_bf, T(c.ident_d, c.wres), ds_c)
        k.dma("sp", c.cos.r("p (t d) -> p t d", t=NT), T(c.cos_d.rearrange("(t p) d -> p t d", p=128), c.wres), ds_c)
        k.dma("sp", c.sin.r("p (t d) -> p t d", t=NT), T(c.sin_d.rearrange("(t p) d -> p t d", p=128), c.wres), ds_c)
        w = c.w
        if stop == "ffn1":
            ffn_phase(k, c, c.x_in, c.xin_res, c.y, c.y_res, w["ffn1_w_in"][0], w["ffn1_w_out"][0], w["ffn1_norm"][0])
        else:
            for l in range(nlayers):
                last = (l == nlayers - 1)
                src, src_res = (c.x_in, c.xin_res) if l == 0 else (c.xs, c.xs_res)
                ffn_phase(k, c, src, src_res, c.xs, c.xs_res, w["ffn1_w_in"][l], w["ffn1_w_out"][l], w["ffn1_norm"][l])
                phase_a(k, c, l)
                if stop in ("a", "a0", "a1", "a2", "a3") or stop.startswith("k"):
                    break
                phase_m(k, c, l)
                if stop == "m":
                    break
                phase_b(k, c, l)
                if stop == "b":
                    break
                dst, dst_res = (c.y, c.y_res) if last else (c.xs, c.xs_res)
                ffn_phase(k, c, c.xs, c.xs_res, dst, dst_res, w["ffn2_w_in"][l], w["ffn2_w_out"][l], w["ffn2_norm"][l],
                          fin_g_d=w["block_norm"][l])
        P.barrier()
        block = stack.enter_context(nc.Block())
        P.emit(block)
    return nc


def _swa_table(gq, gk):
    slopes = (2.0 ** (-8.0 * np.arange(1, 5, dtype=np.float32) / 4)).astype(np.float32)
    q = np.arange(128)[None, :]
    kk = np.arange(128)[:, None]
    dist = np.abs((gq * 128 + q) - (gk * 128 + kk))
    valid = (dist <= 128) & (0 <= gk < SEQ // 128)
    val = -slopes[None, :, None] * dist[:, None, :].astype(np.float32)
    out = np.where(valid[:, None, :], val, np.float32(NEG)).astype(np.float32)
    return out.reshape(128, 512)


def _na_table(rb, gq, gk):
    rows = SEQ // GW
    q = np.arange(128)[None, :]
    kk = np.arange(128)[:, None]
    qrow, qcol = 2 * gq + q // 64, q % 64
    krow, kcol = 2 * gk + kk // 64, kk % 64
    r0 = np.clip(qrow - 4, 0, rows - 8)
    c0 = np.clip(qcol - 8, 0, GW - 16)
    valid = (krow >= r0) & (krow < r0 + 8) & (kcol >= c0) & (kcol < c0 + 16) & (0 <= gk < SEQ // 128)
    dr = np.clip(krow - qrow + 7, 0, 14)
    dc = np.clip(kcol - qcol + 15, 0, 30)
    g = rb[:, dr, dc]
    out = np.where(valid[None], g, np.float32(NEG)).astype(np.float32)
    return np.ascontiguousarray(out.transpose(1, 0, 2)).reshape(128, 512)


def make_tables(inputs, h):
    rb = np.asarray(inputs["na_rel_bias"], dtype=np.float32)
    gmid = 32 * h + 10
    swa = np.stack([_swa_table(gmid, gmid - 1), _swa_table(gmid, gmid), _swa_table(gmid, gmid + 1),
                    _swa_table(32 * h, 32 * h - 1), _swa_table(32 * h + 31, 32 * h + 32)])
    na_gen = np.stack([np.stack([_na_table(rb[l], gmid, gmid + i) for i in range(-2, 3)]) for l in range(DEPTH)])
    na_edge = np.stack([np.stack([np.stack([_na_table(rb[l], 32 * h + t, 32 * h + t + i) for i in range(-3, 4)])
                                  for t in (0, 1, NT - 2, NT - 1)]) for l in range(DEPTH)])
    pos = (h * TOK + np.arange(TOK, dtype=np.float32)).astype(np.float32)
    inv = (1.0 / (np.float32(10000.0) ** (np.arange(0, 32, 2, dtype=np.float32) / np.float32(32)))).astype(np.float32)
    ang = (pos[:, None] * inv[None, :]).astype(np.float32)
    return {"swa_tab": swa, "na_gen": na_gen, "na_edge": na_edge,
            "cos": np.cos(ang).astype(np.float32), "sin": np.sin(ang).astype(np.float32)}


def make_in_maps(inputs):
    x = np.ascontiguousarray(np.asarray(inputs["x"], dtype=np.float32))
    mem = np.ascontiguousarray(np.asarray(inputs["mem"], dtype=np.float32))
    ident = np.eye(128, dtype=np.float32)
    ws = {n: np.ascontiguousarray(np.asarray(inputs[n], dtype=np.float32)) for n in WNAMES}
    tabs = [make_tables(inputs, 0), make_tables(inputs, 1)]
    maps = []
    for core in range(8):
        b, h = core // 2, core % 2
        m = {"x": x[b, h * TOK:(h + 1) * TOK], "mem": mem[b], "ident": ident}
        m.update(tabs[h])
        m.update(ws)
        maps.append(m)
    return maps


def kernel(**inputs):
    nc = build("full")
    res = run_bass_kernel_spmd(nc, make_in_maps(inputs), core_ids=list(range(8)))
    out = np.empty((BATCH, SEQ, D), np.float32)
    for core in range(8):
        b, h = core // 2, core % 2
        out[b, h * TOK:(h + 1) * TOK] = np.asarray(res.results[core]["y"])
    return out
```

```python
import contextlib
import numpy as np
import concourse.bass as bass
import concourse.mybir as mybir
from concourse.bass_utils import run_bass_kernel_spmd

F32 = mybir.dt.float32
BF16 = mybir.dt.bfloat16
AF = mybir.ActivationFunctionType
ALU = mybir.AluOpType
AX = mybir.AxisListType

D = 1024
BATCH = 4
SEQ = 8192
DEPTH = 2
NMEM = 256
GW = 64
EPS = 1e-6
DFF = 2816
NFC = DFF // 128
TOK = SEQ // 2
NT = TOK // 128
NEG = -30000.0

MLA_H = 8
MLA_QL = 256
MLA_KVL = 128
MLA_NOPE = 64
MLA_ROPE = 32
MLA_QK = 96
MLA_V = 64
MIX_IN = 1696


class Res:
    __slots__ = ("name", "w", "rs")

    def __init__(self, name):
        self.name = name
        self.w = None
        self.rs = []


class DSem:
    __slots__ = ("sem", "issued", "inc", "sw", "sw_only")

    def __init__(self, sem):
        self.sem = sem
        self.issued = 0
        self.inc = 16
        self.sw = None
        self.sw_only = False


class Ins:
    __slots__ = ("eng", "fn", "waits", "cdeps", "seq", "marked", "dsem")

    def __init__(self, eng, fn):
        self.eng = eng
        self.fn = fn
        self.waits = []
        self.cdeps = []
        self.seq = 0
        self.marked = False
        self.dsem = None


class T:
    __slots__ = ("ap", "res")

    def __init__(self, ap, res):
        self.ap = ap
        self.res = res

    def __getitem__(self, k):
        return T(self.ap[k], self.res)

    def r(self, pat, **kw):
        return T(self.ap.rearrange(pat, **kw), self.res)

    def bc(self, dt):
        return T(self.ap.bitcast(dt), self.res)

    def bcast(self, axis, shape):
        return T(self.ap.unsqueeze(axis).to_broadcast(shape), self.res)


ENGS = ("pe", "act", "dve", "pool", "sp")


class Prog:
    def __init__(self, nc, stack):
        self.nc = nc
        self.stack = stack
        self.q = {e: [] for e in ENGS}
        self.esem = {e: stack.enter_context(nc.semaphore("es_" + e)) for e in ENGS}
        self.dsems = []
        self.dsem_by_name = {}
        self.locks = {}
        self.last = {e: None for e in ENGS}

    def dsem(self, name, inc=16):
        if name in self.dsem_by_name:
            return self.dsem_by_name[name]
        d = DSem(self.stack.enter_context(self.nc.semaphore("ds_" + name)))
        d.inc = inc
        self.dsems.append(d)
        self.dsem_by_name[name] = d
        return d

    def op(self, eng, fn, reads=(), writes=(), dsem=None):
        ins = Ins(eng, fn)
        ins.dsem = dsem
        if eng in ("act", "dve"):
            locks = []
            for x in list(reads) + list(writes):
                rr = x.res if isinstance(x, T) else x
                if rr.name.startswith("bank"):
                    lk = self.locks.setdefault(rr.name, Res("lock_" + rr.name))
                    if lk not in locks:
                        locks.append(lk)
            writes = list(writes) + locks
        deps = []
        for r in reads:
            r = r.res if isinstance(r, T) else r
            if r.w is not None:
                deps.append(r.w)
        for w in writes:
            w = w.res if isinstance(w, T) else w
            if w.w is not None:
                deps.append(w.w)
            deps.extend(w.rs)
        for r in reads:
            r = r.res if isinstance(r, T) else r
            r.rs.append(ins)
        for w in writes:
            w = w.res if isinstance(w, T) else w
            w.w = ins
            w.rs = []
        seen = set()
        for d in deps:
            if d is ins or id(d) in seen:
                continue
            seen.add(id(d))
            if d.dsem is not None:
                ins.waits.append((d.dsem, d.dsem.issued * d.dsem.inc))
            elif d.eng == eng and eng == "pe":
                continue
            else:
                d.marked = True
                ins.cdeps.append(d)
        if dsem is not None:
            dsem.issued += 1
        else:
            self.last[eng] = ins
        self.q[eng].append(ins)
        return ins

    def barrier(self):
        lasts = [self.last[e] for e in ENGS if self.last[e] is not None]
        for e in ENGS:
            ins = Ins(e, None)
            for d in lasts:
                if d.eng != e:
                    d.marked = True
                    ins.cdeps.append(d)
            for ds in self.dsems:
                if ds.issued:
                    ins.waits.append((ds, ds.issued * ds.inc))
            self.q[e].append(ins)

    def emit(self, block):
        for e in ENGS:
            n = 0
            for ins in self.q[e]:
                if ins.marked:
                    n += 1
                    ins.seq = n
        esem = self.esem

        def run(ename):
            def body(eng):
                waited = {}
                for ins in self.q[ename]:
                    ws = {}
                    for ds, v in ins.waits:
                        k = id(ds.sem)
                        if v > ws.get(k, (None, 0))[1]:
                            ws[k] = (ds.sem, v)
                    for d in ins.cdeps:
                        s = esem[d.eng]
                        k = id(s)
                        if d.seq > ws.get(k, (None, 0))[1]:
                            ws[k] = (s, d.seq)
                    for k, (s, v) in ws.items():
                        if waited.get(k, 0) >= v:
                            continue
                        waited[k] = v
                        eng.wait_ge(s, v)
                    if ins.fn is None:
                        continue
                    bi = ins.fn(eng)
                    if ins.dsem is not None:
                        bi.then_inc(ins.dsem.sem, ins.dsem.inc)
                    elif ins.marked:
                        bi.then_inc(esem[ename], 1)
            return body

        block.tensor(run("pe"))
        block.scalar(run("act"))
        block.vector(run("dve"))
        block.gpsimd(run("pool"))
        block.sync(run("sp"))


class Arena:
    def __init__(self, tensor, nbytes):
        self.t = tensor
        self.n = nbytes
        self.off = 0

    def mark(self):
        return self.off

    def release(self, m):
        self.off = m

    def alloc(self, name, cols, dt=F32, parts=128):
        esz = 4 if dt == F32 else 2
        nb = (cols * esz + 63) // 64 * 64
        assert self.off + nb <= self.n, f"SBUF arena overflow at {name}: {self.off}+{nb}>{self.n}"
        a = self.t[0:parts, self.off // 4:(self.off + nb) // 4]
        self.off += nb
        if dt != F32:
            a = a.bitcast(dt)
        a = a[:, 0:cols]
        return T(a, Res(name))


class K:
    def __init__(self, nc, P, arena, psum):
        self.nc = nc
        self.P = P
        self.A = arena
        self.psum = psum
        self.bank = [T(psum[:, b * 512:(b + 1) * 512], Res(f"bank{b}")) for b in range(8)]

    def dma(self, q, out, in_, dsem, reads=None, writes=None, slow=False):
        if q == "pool" and dsem.inc == 16 and not dsem.sw_only:
            if dsem.sw is None:
                dsem.sw = self.P.dsem("sw%d" % len(self.P.dsems))
                dsem.sw.sw_only = True
            dsem = dsem.sw
        rd = [in_] if reads is None else reads
        wr = [out] if writes is None else writes
        if slow:
            return self.P.op(q, lambda e: e.dma_start(out=out.ap, in_=in_.ap, allow_slow_non_contiguous=True),
                             reads=rd, writes=wr, dsem=dsem)
        return self.P.op(q, lambda e: e.dma_start(out=out.ap, in_=in_.ap), reads=rd, writes=wr, dsem=dsem)

    def mm(self, out, lhsT, rhs, start=True, stop=True):
        return self.P.op("pe", lambda e: e.matmul(out.ap, lhsT.ap, rhs.ap, start=start, stop=stop),
                         reads=[lhsT, rhs], writes=[out])

    def tr(self, out, in_, ident):
        return self.P.op("pe", lambda e: e.transpose(out.ap, in_.ap, ident.ap), reads=[in_, ident], writes=[out])

    def act(self, out, in_, func, scale=1.0, bias=0.0, accum=None):
        rd = [in_]
        wr = [out]
        sc = scale
        if isinstance(scale, T):
            rd.append(scale)
            sc = scale.ap
        bi = bias
        if isinstance(bias, T):
            rd.append(bias)
            bi = bias.ap
        if accum is not None:
            wr.append(accum)
            return self.P.op("act", lambda e: e.activation(out=out.ap, in_=in_.ap, func=func, bias=bi, scale=sc,
                                                           accum_out=accum.ap), reads=rd, writes=wr)
        return self.P.op("act", lambda e: e.activation(out=out.ap, in_=in_.ap, func=func, bias=bi, scale=sc),
                         reads=rd, writes=wr)

    def tt(self, out, a, b, op, eng="dve"):
        return self.P.op(eng, lambda e: e.tensor_tensor(out=out.ap, in0=a.ap, in1=b.ap, op=op), reads=[a, b], writes=[out])

    def ts(self, out, a, s1, op0, s2=None, op1=None, eng="dve", accum=None):
        rd = [a]
        v1 = s1
        if isinstance(s1, T):
            rd.append(s1)
            v1 = s1.ap
        v2 = s2
        if isinstance(s2, T):
            rd.append(s2)
            v2 = s2.ap
        wr = [out]
        kw = {}
        if op1 is not None:
            kw["op1"] = op1
        if accum is not None:
            wr.append(accum)
            kw["accum_out"] = accum.ap
        return self.P.op(eng, lambda e: e.tensor_scalar(out=out.ap, in0=a.ap, scalar1=v1, scalar2=v2, op0=op0, **kw),
                         reads=rd, writes=wr)

    def stt(self, out, a, s, b, op0, op1, accum=None):
        rd = [a, b]
        sv = s
        if isinstance(s, T):
            rd.append(s)
            sv = s.ap
        wr = [out]
        kw = {}
        if accum is not None:
            wr.append(accum)
            kw["accum_out"] = accum.ap
        return self.P.op("dve", lambda e: e.scalar_tensor_tensor(out=out.ap, in0=a.ap, scalar=sv, in1=b.ap, op0=op0,
                                                                 op1=op1, **kw), reads=rd, writes=wr)

    def copy(self, out, in_, eng="dve"):
        return self.P.op(eng, lambda e: e.tensor_copy(out=out.ap, in_=in_.ap), reads=[in_], writes=[out])

    def reduce(self, out, in_, op=ALU.add, axis=AX.X):
        return self.P.op("dve", lambda e: e.tensor_reduce(out=out.ap, in_=in_.ap, axis=axis, op=op), reads=[in_], writes=[out])

    def recip(self, out, in_):
        return self.P.op("dve", lambda e: e.reciprocal(out=out.ap, in_=in_.ap), reads=[in_], writes=[out])

    def memset(self, out, val, eng="dve"):
        return self.P.op(eng, lambda e: e.memset(out.ap, val), reads=[], writes=[out])

    def rstd(self, out, ss, invd):
        self.act(out, ss, AF.Sqrt, scale=invd, bias=EPS)
        self.recip(out, out)


class Ctx:
    pass


def norm_to_hT(k, c, xt_j, ss_col, rstd_col, xn, tbank, hT_dst, gT, junk):
    k.act(junk, xt_j, AF.Square)
    k.reduce(ss_col, junk)
    k.rstd(rstd_col, ss_col, 1.0 / D)
    k.ts(xn, xt_j, rstd_col, ALU.mult)
    tb = tbank.bc(BF16)
    for ch in range(8):
        k.tr(tb[:, ch * 128:(ch + 1) * 128], xn[:, ch * 128:(ch + 1) * 128], c.ident_bf)
    k.tt(hT_dst, tb.r("p (c t) -> p c t", c=8), gT.bcast(2, [128, 8, 128]), ALU.mult)


def ffn_phase(k, c, src, src_res, dst, dst_res, w_in_d, w_out_d, g_d, fin_g_d=None):
    A, P = k.A, k.P
    m = A.mark()
    Win = A.alloc("Win", 8 * 2 * DFF, BF16)
    Wout = A.alloc("Wout", NFC * D, BF16)
    gT = A.alloc("gT", 8)
    junk = A.alloc("junk", D)
    fing = A.alloc("fing", D) if fin_g_d is not None else None
    xts = [A.alloc(f"xt{i}", 2 * D) for i in range(2)]
    xns = [A.alloc(f"xn{i}", D, BF16) for i in range(2)]
    hTs = [A.alloc(f"hT{i}", 8 * 256, BF16) for i in range(2)]
    sgs = [A.alloc(f"sg{i}", 256) for i in range(2)]
    aTs = [A.alloc(f"aT{i}", 256, BF16) for i in range(3)]
    sst = A.alloc("sst", 8)
    ds_x = [P.dsem(f"ffx{i}") for i in range(2)]
    ds_g = P.dsem("ffg")
    Win3 = Win.r("p (c n) -> p c n", c=8)
    Wout3 = Wout.r("p (c n) -> p c n", c=NFC)
    k.dma("sp", gT, T(g_d.rearrange("(c p) -> p c", p=128), c.wres), ds_g, slow=True)
    if fing is not None:
        k.dma("sp", fing, T(fin_g_d.partition_broadcast(128), c.wres), ds_g)
    NB = NFC // 2
    win_res = [Res(f"Win{b}") for b in range(NB)]
    wout_res = [Res(f"Wout{b}") for b in range(NB)]
    w_in_v = w_in_d.rearrange("(c p) n -> p c n", p=128)
    w_out_v = w_out_d.rearrange("(c p) n -> p c n", p=128)
    for b in range(NB):
        dsi = P.dsem(f"ffwi{b}")
        dsi.sw_only = True
        for half in range(2):
            c0 = half * DFF + b * 256
            k.dma("pool", T(Win3.ap[:, :, c0:c0 + 256], win_res[b]), T(w_in_v[:, :, c0:c0 + 256], c.wres), dsi)
        dso = P.dsem(f"ffwo{b}")
        dso.sw_only = True
        k.dma("pool", T(Wout3.ap[:, 2 * b:2 * b + 2, :], wout_res[b]), T(w_out_v[:, 2 * b:2 * b + 2, :], c.wres), dso)
    src_t = src.rearrange("(t p) d -> p t d", p=128)
    dst_t = dst.rearrange("(t p) d -> p t d", p=128)
    NG = NT // 2
    gu_banks = [k.bank[0], k.bank[1]]
    tbanks = [k.bank[2], k.bank[3]]
    obanks = [k.bank[4], k.bank[5], k.bank[6], k.bank[7]]
    def ld(g):
        x3 = xts[g % 2].r("p (j d) -> p j d", j=2)
        k.dma("sp", x3, T(src_t[:, 2 * g:2 * g + 2, :], src_res[2 * g]), ds_x[g % 2],
              reads=[src_res[2 * g], src_res[2 * g + 1]])

    for g in range(NG):
        xt = xts[g % 2]
        hT = hTs[g % 2]
        xt3 = xt.r("p (j d) -> p j d", j=2)
        hT3 = hT.r("p (c t) -> p c t", c=8)
        if g == 0:
            ld(0)
        if g + 1 < NG:
            ld(g + 1)
        for j in range(2):
            norm_to_hT(k, c, xt3[:, j, :], sst[:, j:j + 1], sst[:, 2 + j:3 + j], xns[j], tbanks[j],
                       hT3[:, :, j * 128:(j + 1) * 128], gT, junk)

        def gu(fc):
            bk = gu_banks[fc % 2]
            for half in range(2):
                col = half * DFF + fc * 128
                for ch in range(8):
                    k.mm(bk[:, half * 256:(half + 1) * 256], T(Win3.ap[:, ch, col:col + 128], win_res[fc // 2]), hT3[:, ch, :],
                         start=(ch == 0), stop=(ch == 7))

        def outmm(fc):
            bk = gu_banks[fc % 2]
            sg = sgs[fc % 2]
            aT = aTs[fc % 3]
            k.act(sg, bk[:, 0:256], AF.Silu)
            k.tt(aT, sg, bk[:, 256:512], ALU.mult)
            for j in range(2):
                for dh in range(2):
                    k.mm(obanks[j * 2 + dh], aT[:, j * 128:(j + 1) * 128],
                         T(Wout3.ap[:, fc, dh * 512:(dh + 1) * 512], wout_res[fc // 2]),
                         start=(fc == 0), stop=(fc == NFC - 1))

        gu(0)
        for fc in range(NFC):
            if fc + 1 < NFC:
                gu(fc + 1)
            outmm(fc)
        for j in range(2):
            for dh in range(2):
                xs_ = xt3[:, j, dh * 512:(dh + 1) * 512]
                k.stt(xs_, obanks[j * 2 + dh], 0.5, xs_, ALU.mult, ALU.add)
            if fing is not None:
                xj = xt3[:, j, :]
                k.act(junk, xj, AF.Square)
                k.reduce(sst[:, 4 + j:5 + j], junk)
                k.rstd(sst[:, 6 + j:7 + j], sst[:, 4 + j:5 + j], 1.0 / D)
                k.stt(xj, xj, sst[:, 6 + j:7 + j], fing, ALU.mult, ALU.mult)
        k.dma("sp", T(dst_t[:, 2 * g:2 * g + 2, :], dst_res[2 * g]), xt3, ds_x[g % 2],
              writes=[dst_res[2 * g], dst_res[2 * g + 1]])
    P.barrier()
    A.release(m)


def sumsq_heads(k, src, n, hd, junk, ss):
    k.act(junk[:, 0:n * hd], src, AF.Square)
    k.reduce(ss, junk[:, 0:n * hd].r("p (h d) -> p h d", h=n))


def mla_head_path(k, c, f, g_rep, t, outb, junk, tmp, ss8, r8):
    f3 = f.r("p (h d) -> p h d", h=8)
    o3 = outb.r("p (h d) -> p h d", h=8)
    k.tt(junk[:, 0:768], f, f, ALU.mult)
    k.reduce(ss8, junk[:, 0:768].r("p (h d) -> p h d", h=8))
    k.rstd(r8, ss8, 1.0 / 96)
    k.tt(f3, f3, r8.bcast(2, [128, 8, 96]), ALU.mult)
    k.tt(f3, f3, g_rep.bcast(1, [128, 8, 96]), ALU.mult)
    cs = c.cos[:, t * 16:(t + 1) * 16].bcast(1, [128, 8, 16])
    sn = c.sin[:, t * 16:(t + 1) * 16].bcast(1, [128, 8, 16])
    x1 = f3[:, :, 64:80]
    x2 = f3[:, :, 80:96]
    t4 = tmp.r("p (a h d) -> p a h d", a=4, h=8)
    k.tt(t4[:, 0], x1, cs, ALU.mult)
    k.tt(t4[:, 1], x2, sn, ALU.mult)
    k.tt(t4[:, 2], x1, sn, ALU.mult)
    k.tt(t4[:, 3], x2, cs, ALU.mult)
    k.tt(o3[:, :, 64:80], t4[:, 0], t4[:, 1], ALU.subtract)
    k.tt(o3[:, :, 80:96], t4[:, 2], t4[:, 3], ALU.add)
    k.act(o3[:, :, 0:64], f3[:, :, 0:64], AF.Copy)


def phase_a(k, c, l):
    A, P = k.A, k.P
    w = c.w
    m = A.mark()
    Wmi = A.alloc("Wmi", 8 * 2048, BF16)
    Wuq = A.alloc("Wuq", 2 * 768, BF16)
    Wukv = A.alloc("Wukv", 1024, BF16)
    gT = A.alloc("gTmix", 8)
    gqn = A.alloc("gqn", 2)
    gkvn = A.alloc("gkvn", 1)
    gq_rep = A.alloc("gq_rep", 96)
    gk_rep = A.alloc("gk_rep", 96)
    gcol = A.alloc("gcol", 4)
    junk = A.alloc("junkA", D)
    tmp = A.alloc("tmpA", 4 * 8 * 16)
    sst = A.alloc("sstA", 64)
    xts = [A.alloc(f"xtA{i}", D) for i in range(2)]
    xn = A.alloc("xnA", D, BF16)
    hT = A.alloc("hTA", 8 * 128, BF16)
    cqn = A.alloc("cqn", 256, BF16)
    cqT = A.alloc("cqT", 256, BF16)
    ckvn = A.alloc("ckvn", 128, BF16)
    ckvT = A.alloc("ckvT", 128, BF16)
    qf = A.alloc("qf", 768)
    kf = A.alloc("kf", 768)
    qb = A.alloc("qb", 768, BF16)
    kb = A.alloc("kb", 768, BF16)
    qkb = A.alloc("qkb", 14 * 64, BF16)
    QTst = [A.alloc(f"QTst{i}", 8 * 128, BF16) for i in range(2)]
    KTst = [A.alloc(f"KTst{i}", 8 * 128, BF16) for i in range(2)]
    QKst = [A.alloc(f"QKst{i}", 14 * 128, BF16) for i in range(2)]
    Vmst = [A.alloc(f"Vmst{i}", 8 * 66, BF16) for i in range(2)]
    Vsst = [A.alloc(f"Vsst{i}", 2 * 65, BF16) for i in range(2)]
    Vnst = [A.alloc(f"Vnst{i}", 4 * 65, BF16) for i in range(2)]
    ds_w = P.dsem("aw")
    ds_x = [P.dsem(f"ax{i}") for i in range(2)]
    ds_st = [P.dsem(f"ast{i}") for i in range(2)]
    Wmi3 = Wmi.r("p (c n) -> p c n", c=8)
    Wuq3 = Wuq.r("p (c n) -> p c n", c=2)
    wmi = w["w_mix_in"][l].rearrange("(c p) n -> p c n", p=128)
    for (c0, s0, n) in ((0, 0, 416), (512, 416, 512), (1024, 928, 512), (1536, 1440, 256)):
        k.dma("pool", Wmi3[:, :, c0:c0 + n], T(wmi[:, :, s0:s0 + n], c.wres), ds_w)
    k.dma("pool", Wuq3, T(w["mla_w_uq"][l].rearrange("(c p) n -> p c n", p=128), c.wres), ds_w)
    k.dma("pool", Wukv, T(w["mla_w_ukv"][l], c.wres), ds_w)
    k.dma("sp", gT, T(w["mix_norm"][l].rearrange("(c p) -> p c", p=128), c.wres), ds_w, slow=True)
    k.dma("sp", gqn, T(w["mla_q_norm"][l].rearrange("(c p) -> p c", p=128), c.wres), ds_w, slow=True)
    k.dma("sp", gkvn, T(w["mla_kv_norm"][l].rearrange("(c p) -> p c", p=128), c.wres), ds_w, slow=True)
    k.dma("sp", gq_rep, T(w["mla_q_gain"][l].partition_broadcast(128), c.wres), ds_w)
    k.dma("sp", gk_rep, T(w["mla_k_gain"][l].partition_broadcast(128), c.wres), ds_w)
    for i, nm in enumerate(("swa_q_gain", "swa_k_gain", "na_q_gain", "na_k_gain")):
        k.dma("sp", gcol[0:64, i:i + 1], T(w[nm][l].rearrange("(p o) -> p o", o=1), c.wres), ds_w, slow=True)
    for i in range(2):
        for vt, nh, e in ((Vmst[i], 8, 66), (Vsst[i], 2, 65), (Vnst[i], 4, 65)):
            k.ts(vt.r("p (h e) -> p h e", h=nh)[:, :, 64:65], c.ident_f[:, 0:nh].r("p (h o) -> p h o", o=1),
                 0.0, ALU.mult, 1.0, ALU.add)
    xs_t = c.xs.rearrange("(t p) d -> p t d", p=128)
    B = k.bank

    def ld(t):
        k.dma("sp", xts[t % 2], T(xs_t[:, t, :], c.xs_res[t]), ds_x[t % 2])

    for t in range(NT):
        if t == 0:
            ld(0)
        if t + 1 < NT:
            ld(t + 1)
        xt = xts[t % 2]
        i2 = t % 2
        tsl = slice(t * 128, (t + 1) * 128)
        hT3 = hT.r("p (c t) -> p c t", c=8)
        norm_to_hT(k, c, xt, sst[:, 0:1], sst[:, 1:2], xn, B[4], hT3, gT, junk)
        for b, (c0, n) in enumerate(((0, 416), (512, 512), (1024, 512), (1536, 256))):
            for ch in range(8):
                k.mm(B[b][:, 0:n], hT3[:, ch, :], Wmi3[:, ch, c0:c0 + n], start=(ch == 0), stop=(ch == 7))
        if c.stop == "a1":
            continue
        sumsq_heads(k, B[0][:, 0:256], 1, 256, junk, sst[:, 2:3])
        k.rstd(sst[:, 3:4], sst[:, 2:3], 1.0 / 256)
        k.ts(cqn, B[0][:, 0:256], sst[:, 3:4], ALU.mult)
        tb4 = B[4].bc(BF16)
        for c2 in range(2):
            k.tr(tb4[:, c2 * 128:(c2 + 1) * 128], cqn[:, c2 * 128:(c2 + 1) * 128], c.ident_bf)
        k.tt(cqT.r("p (c t) -> p c t", c=2), tb4[:, 0:256].r("p (c t) -> p c t", c=2),
             gqn.bcast(2, [128, 2, 128]), ALU.mult)
        cqT3 = cqT.r("p (c t) -> p c t", c=2)
        for half in range(2):
            for c2 in range(2):
                k.mm(B[6 + half][:, 0:384], cqT3[:, c2, :], Wuq3[:, c2, half * 384:(half + 1) * 384],
                     start=(c2 == 0), stop=(c2 == 1))
        k.act(qf[:, 0:384], B[6][:, 0:384], AF.Copy)
        k.act(qf[:, 384:768], B[7][:, 0:384], AF.Copy)
        mla_head_path(k, c, qf, gq_rep, t, qb, junk, tmp, sst[:, 8:16], sst[:, 16:24])
        tb5 = B[5].bc(BF16)
        for h in range(8):
            k.tr(tb5[0:96, h * 128:(h + 1) * 128], qb[:, h * 96:(h + 1) * 96], c.ident_bf)
        k.copy(QTst[i2][0:96, :], tb5[0:96, :])
        k.dma("sp", T(c.QTm[:, :, tsl].rearrange("h d t -> d h t"), c.QTm_res),
              QTst[i2][0:96, :].r("p (h t) -> p h t", h=8), ds_st[i2])
        if c.stop == "a2":
            continue
        sumsq_heads(k, B[0][:, 256:384], 1, 128, junk, sst[:, 4:5])
        k.rstd(sst[:, 5:6], sst[:, 4:5], 1.0 / 128)
        k.ts(ckvn, B[0][:, 256:384], sst[:, 5:6], ALU.mult)
        if c.stop == "k0":
            continue
        k.tr(tb4[:, 0:128], ckvn, c.ident_bf)
        k.ts(ckvT, tb4[:, 0:128], gkvn[:, 0:1], ALU.mult)
        if c.stop == "k1":
            continue
        for half in range(2):
            k.mm(B[6 + half], ckvT, Wukv[:, half * 512:(half + 1) * 512])
        kf3 = kf.r("p (h d) -> p h d", h=8)
        vm3 = Vmst[i2].r("p (h e) -> p h e", h=8)
        if c.stop == "k2":
            continue
        for half in range(2):
            kv4 = B[6 + half].r("p (h d) -> p h d", h=4)
            if c.stop != "k3d":
                k.act(kf3[:, half * 4:(half + 1) * 4, 0:64], kv4[:, :, 0:64], AF.Copy)
            if c.stop != "k3a":
                k.copy(vm3[:, half * 4:(half + 1) * 4, 0:64], kv4[:, :, 64:128])
        if c.stop in ("k3", "k3a", "k3d"):
            continue
        if c.stop == "k4x":
            k.copy(tmp[:, 0:32], B[0][:, 384:416])
            continue
        if c.stop == "k4y":
            k.copy(kf3[:, :, 64:96], tmp[:, 0:32].bcast(1, [128, 8, 32]))
            continue
        k.copy(kf3[:, :, 64:96], B[0][:, 384:416].bcast(1, [128, 8, 32]))
        if c.stop == "k4":
            continue
        mla_head_path(k, c, kf, gk_rep, t, kb, junk, tmp, sst[:, 24:32], sst[:, 32:40])
        if c.stop == "k5":
            continue
        for h in range(8):
            k.tr(tb5[0:96, h * 128:(h + 1) * 128], kb[:, h * 96:(h + 1) * 96], c.ident_bf)
        k.copy(KTst[i2][0:96, :], tb5[0:96, :])
        if c.stop == "k6":
            continue
        kst3 = KTst[i2][0:96, :].r("p (h t) -> p h t", h=8)
        for j in range(4):
            k.dma("sp", T(c.KTm_l[j][:, :, tsl].rearrange("h d t -> d h t"), c.KTm_l_res),
                  kst3[:, 2 * j:2 * j + 2, :], ds_st[i2])
            k.dma("sp", T(c.Vm_l[j][:, :, t, :].rearrange("h p e -> p h e"), c.Vm_l_res),
                  vm3[:, 2 * j:2 * j + 2, :], ds_st[i2])
        if c.stop == "a3":
            continue
        k.act(junk[:, 0:384], B[1][:, 0:384], AF.Square)
        k.act(junk[:, 384:896], B[2][:, 0:512], AF.Square)
        k.reduce(sst[:, 40:54], junk[:, 0:896].r("p (h d) -> p h d", h=14))
        k.rstd(sst[:, 40:54], sst[:, 40:54], 1.0 / 64)
        qkb3 = qkb.r("p (h d) -> p h d", h=14)
        k.tt(qkb3[:, 0:6, :], B[1][:, 0:384].r("p (h d) -> p h d", h=6), sst[:, 40:46].bcast(2, [128, 6, 64]), ALU.mult)
        k.tt(qkb3[:, 6:14, :], B[2][:, 0:512].r("p (h d) -> p h d", h=8), sst[:, 46:54].bcast(2, [128, 8, 64]), ALU.mult)
        st = QKst[i2]
        for h in range(8):
            k.tr(tb5[0:64, h * 128:(h + 1) * 128], qkb[:, h * 64:(h + 1) * 64], c.ident_bf)
        for h in range(8, 14):
            k.tr(tb4[0:64, (h - 8) * 128:(h - 7) * 128], qkb[:, h * 64:(h + 1) * 64], c.ident_bf)
        k.ts(st[0:64, 0:512], tb5[0:64, 0:512], gcol[0:64, 0:1], ALU.mult)
        k.ts(st[0:64, 512:768], tb5[0:64, 512:768], gcol[0:64, 1:2], ALU.mult)
        k.ts(st[0:64, 768:1024], tb5[0:64, 768:1024], gcol[0:64, 2:3], ALU.mult)
        k.ts(st[0:64, 1024:1280], tb4[0:64, 0:256], gcol[0:64, 2:3], ALU.mult)
        k.ts(st[0:64, 1280:1792], tb4[0:64, 256:768], gcol[0:64, 3:4], ALU.mult)
        st3 = st[0:64, :].r("p (h t) -> p h t", h=14)
        k.dma("sp", T(c.QTs[:, :, tsl].rearrange("h d t -> d h t"), c.QTs_res), st3[:, 0:4, :], ds_st[i2])
        k.dma("sp", T(c.KTs_l[0][:, :, tsl].rearrange("h d t -> d h t"), c.KTs_l_res), st3[:, 4:6, :], ds_st[i2])
        k.dma("sp", T(c.QTn[:, :, tsl].rearrange("h d t -> d h t"), c.QTn_res), st3[:, 6:10, :], ds_st[i2])
        k.dma("sp", T(c.KTn_l[0][:, :, tsl].rearrange("h d t -> d h t"), c.KTn_l_res), st3[:, 10:14, :], ds_st[i2])
        vs3 = Vsst[i2].r("p (h e) -> p h e", h=2)
        vn3 = Vnst[i2].r("p (h e) -> p h e", h=4)
        k.copy(vs3[:, :, 0:64], B[1][:, 384:512].r("p (h d) -> p h d", h=2))
        k.copy(vn3[:, :, 0:64], B[3][:, 0:256].r("p (h d) -> p h d", h=4))
        k.dma("sp", T(c.Vs_l[0][t], c.Vs_l_res), Vsst[i2], ds_st[i2])
        k.dma("sp", T(c.Vn_l[t // (NT // 2)][t % (NT // 2)], c.Vn_l_res), Vnst[i2], ds_st[i2])
    P.barrier()
    A.release(m)
    if c.stop in ("a0", "a1", "a2", "a3") or c.stop.startswith("k"):
        return
    for (cn, t_l, t_g, nm) in c.cc_list:
        ds = P.dsem(f"cc_{cn}_{l}", inc=1)
        P.op("pool", (lambda s_, d_: (lambda e: e.collective_compute(
            "AllGather", ALU.bypass, replica_groups=[[0, 1], [2, 3], [4, 5], [6, 7]][:c.ncores // 2],
            ins=[s_.ap().opt()], outs=[d_.ap().opt()])))(t_l, t_g),
            reads=[getattr(c, nm + "_l_res")], writes=[getattr(c, nm + "_g_res")], dsem=ds)
    P.barrier()


def phase_m(k, c, l):
    A, P = k.A, k.P
    m = A.mark()
    KTs_ = [A.alloc(f"KTm{i}", 2 * TOK, BF16) for i in range(2)]
    VAs = [A.alloc(f"VAm{i}", 2 * NT * 66, BF16) for i in range(2)]
    QTs_ = [A.alloc(f"QTm{i}", TOK, BF16) for i in range(2)]
    PTs = [A.alloc(f"PT{i}", 1024, BF16) for i in range(3)]
    OTs = [A.alloc(f"OT{i}", 512) for i in range(2)]
    ost = [A.alloc(f"ost{i}", 4 * 64) for i in range(2)]
    rc = A.alloc("rcm", 8)
    ds_h = [P.dsem(f"mh{i}") for i in range(2)]
    ds_o = [P.dsem(f"mo{i}") for i in range(2)]
    B = k.bank
    scale = 96 ** -0.5
    om = c.omla.rearrange("(t p) h e -> p t h e", p=128)

    def ld(h):
        i = h % 2
        k.dma("sp", KTs_[i][0:96, :].r("p (r t) -> p r t", r=2), T(c.KTm_g[h // 2][:, h % 2].rearrange("r d t -> d r t"), c.KTm_g_res), ds_h[i])
        k.dma("sp", VAs[i].r("p (r t e) -> p r t e", r=2, t=NT), T(c.Vm_g[h // 2][:, h % 2].rearrange("r p t e -> p r t e"), c.Vm_g_res), ds_h[i])
        k.dma("sp", QTs_[i][0:96, :], T(c.QTm[h], c.QTm_res), ds_h[i])

    cnt = [0]
    pend = []

    def fin_pe(args):
        qg, h, ob, n = args
        OT = OTs[n % 2]
        tb = B[5]
        for j in range(4):
            k.tr(tb[:, j * 65:(j + 1) * 65], OT[0:65, j * 128:(j + 1) * 128], c.ident_f[0:65, 0:65])
        t3 = tb[:, 0:260].r("p (j e) -> p j e", j=4)
        k.recip(rc[:, 0:4], t3[:, :, 64])
        o3 = ost[n % 2].r("p (j e) -> p j e", j=4)
        k.tt(o3, t3[:, :, 0:64], rc[:, 0:4].bcast(2, [128, 4, 64]), ALU.mult)
        k.dma("sp", T(om[:, qg * 4:(qg + 1) * 4, h, :], c.omla_res), o3, ds_o[n % 2])

    items = [(h, qg, kp) for h in range(MLA_H) for qg in range(8) for kp in range(32)]

    def bufs(h):
        return KTs_[h % 2], VAs[h % 2].r("p (r t e) -> p r t e", r=2, t=NT), QTs_[h % 2]

    def S_(it, idx):
        h, qg, kp = it
        KT, VA, QT = bufs(h)
        b0 = (idx % 2) * 2
        for j in range(2):
            kt = 2 * kp + j
            r, lt = kt // NT, kt % NT
            k.mm(B[b0 + j], KT[0:96, r * TOK + lt * 128:r * TOK + (lt + 1) * 128], QT[0:96, qg * 512:(qg + 1) * 512])

    def EXP_(it, idx):
        b0 = (idx % 2) * 2
        PT = PTs[idx % 3]
        src2 = T(k.psum[:, b0 * 512:(b0 + 2) * 512], B[b0].res)
        P.op("act", (lambda o_, i_: (lambda e: e.activation(out=o_.ap, in_=i_.ap, func=AF.Exp, scale=scale)))(PT, src2),
             reads=[B[b0], B[b0 + 1]], writes=[PT])

    def PV_(it, idx, ob):
        h, qg, kp = it
        KT, VA, QT = bufs(h)
        PT = PTs[idx % 3]
        for j in range(2):
            kt = 2 * kp + j
            r, lt = kt // NT, kt % NT
            k.mm(ob[0:65, :], VA[:, r, lt, 0:65], PT[:, j * 512:(j + 1) * 512], start=(kt == 0), stop=(kt == 63))

    ld(0)
    S_(items[0], 0)
    for idx, it in enumerate(items):
        h, qg, kp = it
        if qg == 0 and kp == 0 and h + 1 < MLA_H:
            ld(h + 1)
        n = h * 8 + qg
        ob = B[6 + n % 2]
        if idx + 1 < len(items):
            S_(items[idx + 1], idx + 1)
        EXP_(it, idx)
        PV_(it, idx, ob)
        if kp == 31:
            k.copy(OTs[n % 2][0:65, :], ob[0:65, :])
            if pend:
                fin_pe(pend.pop())
            pend.append((qg, h, ob, n))
    fin_pe(pend.pop())
    P.barrier()
    A.release(m)


def phase_b(k, c, l):
    A, P = k.A, k.P
    w = c.w
    m = A.mark()
    B = k.bank
    Wmo = A.alloc("Wmo", 8 * D, BF16)
    Wq = A.alloc("Wq", 8 * 256, BF16)
    Wo = A.alloc("Wo", 2 * D, BF16)
    Wkv = A.alloc("Wkv", 8 * 512, BF16)
    gT_grp = A.alloc("gT_grp", 8)
    gT_mx = A.alloc("gT_mx", 8)
    gT_mm = A.alloc("gT_mm", 8)
    gcm = A.alloc("gcm", 2)
    esink = A.alloc("esink", 4)
    KTmem = A.alloc("KTmem", 4 * 256, BF16)
    VAmem = A.alloc("VAmem", 2 * 4 * 65, BF16)
    swat = A.alloc("swat", 5 * 512)
    nagen = A.alloc("nagen", 5 * 512)
    naedge = A.alloc("naedge", 7 * 512)
    junk = A.alloc("junkB", D)
    sst = A.alloc("sstB", 32)
    xn = A.alloc("xnB", D, BF16)
    hT = A.alloc("hTB", 8 * 128, BF16)
    ocb = A.alloc("ocb", D, BF16)
    oT = A.alloc("oTB", 8 * 128, BF16)
    qmb = A.alloc("qmb", 256, BF16)
    QTmem = A.alloc("QTmem", 4 * 128, BF16)
    omb = A.alloc("omb", 256, BF16)
    omf = A.alloc("omf", 256)
    omT = A.alloc("omT", 2 * 128, BF16)
    sbs = [A.alloc(f"sb{i}", 512) for i in range(2)]
    PTs = [A.alloc(f"PTb{i}", 512, BF16) for i in range(7)]
    xts = [A.alloc(f"xtB{i}", D) for i in range(2)]
    ocs = [A.alloc(f"oc{i}", D) for i in range(2)]
    QTs_ = [A.alloc(f"QTsB{i}", 4 * 128, BF16) for i in range(2)]
    KTs_ = [A.alloc(f"KTsB{i}", 2 * 3 * 128, BF16) for i in range(2)]
    Vs_ = [A.alloc(f"VsB{i}", 3 * 130, BF16) for i in range(2)]
    QTn_ = [A.alloc(f"QTnB{i}", 4 * 128, BF16) for i in range(2)]
    KTn_ = [A.alloc(f"KTnB{i}", 4 * 7 * 128, BF16) for i in range(2)]
    Vn_ = [A.alloc(f"VnB{i}", 7 * 260, BF16) for i in range(2)]
    ds_w = P.dsem("bw")
    ds_t = [P.dsem(f"bt{i}") for i in range(2)]
    ds_e = P.dsem("be")
    ds_s = [P.dsem(f"bs{i}") for i in range(2)]
    Wmo3 = Wmo.r("p (c n) -> p c n", c=8)
    Wq3 = Wq.r("p (c n) -> p c n", c=8)
    Wo3 = Wo.r("p (c n) -> p c n", c=2)
    Wkv3 = Wkv.r("p (c n) -> p c n", c=8)
    k.dma("pool", Wmo3, T(w["w_mix_out"][l].rearrange("(c p) n -> p c n", p=128), c.wres), ds_w)
    k.dma("pool", Wq3, T(w["mem_w_q"][l].rearrange("(c p) n -> p c n", p=128), c.wres), ds_w)
    k.dma("pool", Wo3, T(w["mem_w_o"][l].rearrange("(c p) n -> p c n", p=128), c.wres), ds_w)
    k.dma("pool", Wkv3, T(w["mem_w_kv"][l].rearrange("(c p) n -> p c n", p=128), c.wres), ds_w)
    for tl, nm in ((gT_grp, "grp_out_gain"), (gT_mx, "mem_norm_x"), (gT_mm, "mem_norm_m")):
        k.dma("sp", tl, T(w[nm][l].rearrange("(c p) -> p c", p=128), c.wres), ds_w, slow=True)
    k.dma("sp", gcm[0:64, 0:1], T(w["mem_q_gain"][l].rearrange("(p o) -> p o", o=1), c.wres), ds_w, slow=True)
    k.dma("sp", gcm[0:64, 1:2], T(w["mem_k_gain"][l].rearrange("(p o) -> p o", o=1), c.wres), ds_w, slow=True)
    k.dma("sp", esink, T(w["swa_sink"][l].partition_broadcast(128), c.wres), ds_w)
    k.dma("sp", swat.r("p (i n) -> p i n", i=5), T(c.swa_tab.rearrange("i p n -> p i n"), c.wres), ds_w)
    k.dma("sp", nagen.r("p (i n) -> p i n", i=5), T(c.na_gen[l].rearrange("i p n -> p i n"), c.wres), ds_w)
    k.act(esink, esink, AF.Exp)
    k.ts(VAmem.r("p (m e) -> p m e", m=8)[:, :, 64:65], c.ident_f[:, 0:8].r("p (h o) -> p h o", o=1),
         0.0, ALU.mult, 1.0, ALU.add)
    va4 = VAmem.r("p (m h e) -> p m h e", m=2, h=4)
    hT3 = hT.r("p (c t) -> p c t", c=8)
    tb4 = B[4].bc(BF16)
    for mt in range(2):
        k.dma("sp", xts[mt], T(c.mem[mt * 128:(mt + 1) * 128, :], c.wres), ds_t[mt])
        norm_to_hT(k, c, xts[mt], sst[:, 0:1], sst[:, 1:2], xn, B[4], hT3, gT_mm, junk)
        for ch in range(8):
            k.mm(B[5], hT3[:, ch, :], Wkv3[:, ch, :], start=(ch == 0), stop=(ch == 7))
        sumsq_heads(k, B[5][:, 0:256], 4, 64, junk, sst[:, 2:6])
        k.rstd(sst[:, 2:6], sst[:, 2:6], 1.0 / 64)
        k.tt(qmb.r("p (h d) -> p h d", h=4), B[5][:, 0:256].r("p (h d) -> p h d", h=4),
             sst[:, 2:6].bcast(2, [128, 4, 64]), ALU.mult)
        for hd in range(4):
            k.tr(tb4[0:64, hd * 128:(hd + 1) * 128], qmb[:, hd * 64:(hd + 1) * 64], c.ident_bf)
        k.ts(KTmem[0:64, :].r("p (h m) -> p h m", h=4)[:, :, mt * 128:(mt + 1) * 128],
             tb4[0:64, 0:512].r("p (h m) -> p h m", h=4), gcm[0:64, 1:2], ALU.mult)
        k.copy(va4[:, mt, :, 0:64], B[5][:, 256:512].r("p (h d) -> p h d", h=4))
    KTmem3 = KTmem[0:64, :].r("p (h m) -> p h m", h=4)
    xs_t = c.xs.rearrange("(t p) d -> p t d", p=128)
    om_t = c.omla.rearrange("(t p) h e -> p t (h e)", p=128)
    na_edge_t = {0: 0, 1: 1, NT - 2: 2, NT - 1: 3}

    def ktile_src(loc, gat, lt):
        if 0 <= lt < NT:
            return loc[0][:, :, lt * 128:(lt + 1) * 128]
        if lt < 0:
            return gat[0][0][:, :, (NT + lt) * 128:(NT + lt + 1) * 128]
        return gat[0][1][:, :, (lt - NT) * 128:(lt - NT + 1) * 128]

    def vtile_src(loc, gat, lt):
        nch = len(loc)
        per = NT // nch

        def pick(lst, tile, r=None):
            a = lst[tile // per]
            return a[tile % per] if r is None else a[r][tile % per]
        if 0 <= lt < NT:
            return pick(loc, lt)
        if lt < 0:
            return pick(gat, NT + lt, 0)
        return pick(gat, lt - NT, 1)

    def win_n(t):
        return list(range(-2, 3)) if t not in na_edge_t else list(range(-3, 4))

    def ld(t):
        i2 = t % 2
        ds = ds_t[i2]
        tsl = slice(t * 128, (t + 1) * 128)
        k.dma("sp", xts[i2], T(xs_t[:, t, :], c.xs_res[t]), ds)
        k.dma("sp", ocs[i2][:, 0:512], T(om_t[:, t, :], c.omla_res), ds)
        k.dma("sp", QTs_[i2][0:64, :].r("p (h t) -> p h t", h=4), T(c.QTs[:, :, tsl].rearrange("h d t -> d h t"), c.QTs_res), ds)
        k.dma("sp", QTn_[i2][0:64, :].r("p (h t) -> p h t", h=4), T(c.QTn[:, :, tsl].rearrange("h d t -> d h t"), c.QTn_res), ds)
        ks3 = KTs_[i2][0:64, :].r("p (h i t) -> p h i t", h=2, i=3)
        vs3 = Vs_[i2].r("p (i e) -> p i e", i=3)
        for j, i in enumerate((-1, 0, 1)):
            k.dma("sp", ks3[:, :, j, :], T(ktile_src(c.KTs_l, c.KTs_g, t + i).rearrange("h d t -> d h t"), c.KTs_l_res), ds,
                  reads=[c.KTs_l_res, c.KTs_g_res])
            k.dma("sp", vs3[:, j, :], T(vtile_src(c.Vs_l, c.Vs_g, t + i), c.Vs_l_res), ds, reads=[c.Vs_l_res, c.Vs_g_res])
        kn3 = KTn_[i2][0:64, :].r("p (h i t) -> p h i t", h=4, i=7)
        vn3 = Vn_[i2].r("p (i e) -> p i e", i=7)
        for j, i in enumerate(win_n(t)):
            k.dma("sp", kn3[:, :, j, :], T(ktile_src(c.KTn_l, c.KTn_g, t + i).rearrange("h d t -> d h t"), c.KTn_l_res), ds,
                  reads=[c.KTn_l_res, c.KTn_g_res])
            k.dma("sp", vn3[:, j, :], T(vtile_src(c.Vn_l, c.Vn_g, t + i), c.Vn_l_res), ds, reads=[c.Vn_l_res, c.Vn_g_res])

    sbc = [0]

    def attend(Q3, Ksrc, Vsrc, tabs, nk, hkv, scale, Ob):
        for j in range(nk):
            n = sbc[0]
            sbc[0] += 1
            Sb = B[n % 2]
            Kt = Ksrc(j)
            for hd in range(4):
                k.mm(Sb[:, hd * 128:(hd + 1) * 128], Kt[:, hd * hkv // 4, :], Q3[:, hd, :])
            PT = PTs[j]
            tab = tabs(j)
            if tab is not None:
                sb = sbs[n % 2]
                k.stt(sb, Sb, scale, tab, ALU.mult, ALU.add)
                k.act(PT, sb, AF.Exp)
            else:
                k.act(PT, Sb, AF.Exp, scale=scale)
        for hd in range(4):
            for j in range(nk):
                Vt = Vsrc(j)
                k.mm(Ob[:, hd * 65:(hd + 1) * 65], PTs[j][:, hd * 128:(hd + 1) * 128], Vt[:, hd * hkv // 4, :],
                     start=(j == 0), stop=(j == nk - 1))

    def finish(Ob, dst, den_add, rcol):
        o3 = Ob[:, 0:260].r("p (h e) -> p h e", h=4)
        if den_add is not None:
            k.tt(rcol, o3[:, :, 64], den_add, ALU.add)
            k.recip(rcol, rcol)
        else:
            k.recip(rcol, o3[:, :, 64])
        k.tt(dst.r("p (h d) -> p h d", h=4), o3[:, :, 0:64], rcol.bcast(2, [128, 4, 64]), ALU.mult)

    ld(0)
    for t in range(NT):
        if t + 1 < NT:
            ld(t + 1)
        i2 = t % 2
        xt = xts[i2]
        oc = ocs[i2]
        Qs3 = QTs_[i2][0:64, :].r("p (h t) -> p h t", h=4)
        ks3 = KTs_[i2][0:64, :].r("p (h i t) -> p h i t", h=2, i=3)
        vs4 = Vs_[i2].r("p (i h e) -> p i h e", i=3, h=2)
        sw3 = swat.r("p (i n) -> p i n", i=5)

        def swa_tab(j, t=t):
            if j == 0 and t == 0:
                return sw3[:, 3, :]
            if j == 2 and t == NT - 1:
                return sw3[:, 4, :]
            return sw3[:, j, :]

        attend(Qs3, lambda j: ks3[:, :, j, :], lambda j: vs4[:, j, :, :], swa_tab, 3, 2, 0.125, B[2])
        finish(B[2], oc[:, 512:768], esink, sst[:, 8:12])
        Qn3 = QTn_[i2][0:64, :].r("p (h t) -> p h t", h=4)
        kn3 = KTn_[i2][0:64, :].r("p (h i t) -> p h i t", h=4, i=7)
        vn4 = Vn_[i2].r("p (i h e) -> p i h e", i=7, h=4)
        if t in na_edge_t:
            ne3 = naedge.r("p (i n) -> p i n", i=7)
            k.dma("sp", ne3, T(c.na_edge[l, na_edge_t[t]].rearrange("i p n -> p i n"), c.wres), ds_e)
            nk = 7
            ntab = lambda j: ne3[:, j, :]
        else:
            ng3 = nagen.r("p (i n) -> p i n", i=5)
            nk = 5
            ntab = lambda j: ng3[:, j, :]
        attend(Qn3, lambda j: kn3[:, :, j, :], lambda j: vn4[:, j, :, :], ntab, nk, 4, 0.125, B[3])
        finish(B[3], oc[:, 768:1024], None, sst[:, 12:16])
        if c.debug:
            k.dma("sp", T(c.ocat_dbg.rearrange("(t p) d -> p t d", p=128)[:, t, :], Res("dbg")), oc, ds_s[i2])
        k.act(junk, oc, AF.Square)
        k.reduce(sst[:, 16:17], junk[:, 0:512])
        k.reduce(sst[:, 17:19], junk[:, 512:1024].r("p (g d) -> p g d", g=2))
        k.rstd(sst[:, 20:21], sst[:, 16:17], 1.0 / 512)
        k.rstd(sst[:, 21:23], sst[:, 17:19], 1.0 / 256)
        k.ts(ocb[:, 0:512], oc[:, 0:512], sst[:, 20:21], ALU.mult)
        k.ts(ocb[:, 512:768], oc[:, 512:768], sst[:, 21:22], ALU.mult)
        k.ts(ocb[:, 768:1024], oc[:, 768:1024], sst[:, 22:23], ALU.mult)
        for ch in range(8):
            k.tr(tb4[:, ch * 128:(ch + 1) * 128], ocb[:, ch * 128:(ch + 1) * 128], c.ident_bf)
        oT3 = oT.r("p (c t) -> p c t", c=8)
        k.tt(oT3, tb4.r("p (c t) -> p c t", c=8), gT_grp.bcast(2, [128, 8, 128]), ALU.mult)
        for dh in range(2):
            for ch in range(8):
                k.mm(B[6 + dh], oT3[:, ch, :], Wmo3[:, ch, dh * 512:(dh + 1) * 512], start=(ch == 0), stop=(ch == 7))
        for dh in range(2):
            xs_ = xt[:, dh * 512:(dh + 1) * 512]
            k.tt(xs_, xs_, B[6 + dh], ALU.add)
        norm_to_hT(k, c, xt, sst[:, 0:1], sst[:, 1:2], xn, B[4], hT3, gT_mx, junk)
        for ch in range(8):
            k.mm(B[5][:, 0:256], hT3[:, ch, :], Wq3[:, ch, :], start=(ch == 0), stop=(ch == 7))
        sumsq_heads(k, B[5][:, 0:256], 4, 64, junk, sst[:, 2:6])
        k.rstd(sst[:, 2:6], sst[:, 2:6], 1.0 / 64)
        k.tt(qmb.r("p (h d) -> p h d", h=4), B[5][:, 0:256].r("p (h d) -> p h d", h=4),
             sst[:, 2:6].bcast(2, [128, 4, 64]), ALU.mult)
        for hd in range(4):
            k.tr(tb4[0:64, hd * 128:(hd + 1) * 128], qmb[:, hd * 64:(hd + 1) * 64], c.ident_bf)
        k.ts(QTmem[0:64, :], tb4[0:64, 0:512], gcm[0:64, 0:1], ALU.mult)
        Qm3 = QTmem[0:64, :].r("p (h t) -> p h t", h=4)
        attend(Qm3, lambda j: KTmem3[:, :, j * 128:(j + 1) * 128], lambda j: va4[:, j, :, :], lambda j: None, 2, 4, 0.125, B[2])
        finish(B[2], omf, None, sst[:, 24:28])
        k.act(omb, omf, AF.Copy)
        for c2 in range(2):
            k.tr(tb4[:, c2 * 128:(c2 + 1) * 128], omb[:, c2 * 128:(c2 + 1) * 128], c.ident_bf)
        k.copy(omT, tb4[:, 0:256])
        omT3 = omT.r("p (c t) -> p c t", c=2)
        for dh in range(2):
            for c2 in range(2):
                k.mm(B[6 + dh], omT3[:, c2, :], Wo3[:, c2, dh * 512:(dh + 1) * 512], start=(c2 == 0), stop=(c2 == 1))
        for dh in range(2):
            xs_ = xt[:, dh * 512:(dh + 1) * 512]
            k.tt(xs_, xs_, B[6 + dh], ALU.add)
        k.dma("sp", T(xs_t[:, t, :], c.xs_res[t]), xt, ds_s[i2])
    P.barrier()
    A.release(m)


WNAMES = ["ffn1_norm", "ffn1_w_in", "ffn1_w_out", "mix_norm", "w_mix_in", "mla_q_norm", "mla_w_uq", "mla_kv_norm",
          "mla_w_ukv", "mla_q_gain", "mla_k_gain", "swa_q_gain", "swa_k_gain", "swa_sink", "na_q_gain", "na_k_gain",
          "grp_out_gain", "w_mix_out", "mem_norm_x", "mem_norm_m", "mem_w_q", "mem_w_kv", "mem_q_gain",
          "mem_k_gain", "mem_w_o", "ffn2_norm", "ffn2_w_in", "ffn2_w_out", "block_norm"]
WSHAPES = {
    "ffn1_norm": [DEPTH, D], "ffn1_w_in": [DEPTH, D, 2 * DFF], "ffn1_w_out": [DEPTH, DFF, D], "mix_norm": [DEPTH, D],
    "w_mix_in": [DEPTH, D, MIX_IN], "mla_q_norm": [DEPTH, 256], "mla_w_uq": [DEPTH, 256, 768],
    "mla_kv_norm": [DEPTH, 128], "mla_w_ukv": [DEPTH, 128, 1024], "mla_q_gain": [DEPTH, 96], "mla_k_gain": [DEPTH, 96],
    "swa_q_gain": [DEPTH, 64], "swa_k_gain": [DEPTH, 64], "swa_sink": [DEPTH, 4], "na_q_gain": [DEPTH, 64],
    "na_k_gain": [DEPTH, 64], "grp_out_gain": [DEPTH, D], "w_mix_out": [DEPTH, D, D], "mem_norm_x": [DEPTH, D],
    "mem_norm_m": [DEPTH, D], "mem_w_q": [DEPTH, D, 256], "mem_w_kv": [DEPTH, D, 512], "mem_q_gain": [DEPTH, 64],
    "mem_k_gain": [DEPTH, 64], "mem_w_o": [DEPTH, 256, D], "ffn2_norm": [DEPTH, D], "ffn2_w_in": [DEPTH, D, 2 * DFF],
    "ffn2_w_out": [DEPTH, DFF, D], "block_norm": [DEPTH, D],
}
ARENA_BYTES = 204800


def build(stop="full", nlayers=DEPTH, ncores=8, debug=False):
    nc = bass.Bass("TRN2", target_bir_lowering=False)
    c = Ctx()
    c.ncores = ncores
    c.debug = debug
    dk = {"kind": "ExternalOutput"} if debug else {}
    c.stop = stop
    c.x_in = nc.dram_tensor("x", [TOK, D], F32, kind="ExternalInput").ap()
    c.mem = nc.dram_tensor("mem", [NMEM, D], F32, kind="ExternalInput").ap()
    c.ident_d = nc.dram_tensor("ident", [128, 128], F32, kind="ExternalInput").ap()
    c.cos_d = nc.dram_tensor("cos", [TOK, 16], F32, kind="ExternalInput").ap()
    c.sin_d = nc.dram_tensor("sin", [TOK, 16], F32, kind="ExternalInput").ap()
    c.swa_tab = nc.dram_tensor("swa_tab", [5, 128, 512], F32, kind="ExternalInput").ap()
    c.na_gen = nc.dram_tensor("na_gen", [DEPTH, 5, 128, 512], F32, kind="ExternalInput").ap()
    c.na_edge = nc.dram_tensor("na_edge", [DEPTH, 4, 7, 128, 512], F32, kind="ExternalInput").ap()
    c.w = {n: nc.dram_tensor(n, WSHAPES[n], F32, kind="ExternalInput").ap() for n in WNAMES}
    c.y = nc.dram_tensor("y", [TOK, D], F32, kind="ExternalOutput").ap()
    c.xs = nc.dram_tensor("xs", [TOK, D], F32, **dk).ap()
    c.omla = nc.dram_tensor("omla", [TOK, 8, 64], F32, **dk).ap()
    if debug:
        c.ocat_dbg = nc.dram_tensor("ocat_dbg", [TOK, D], F32, **dk).ap()
    c.omla_res = Res("omla")

    c.cc_list = []

    def scratch(name, nchunk, rows, cols, pat_l, pat_g, **kw):
        ls, gs = [], []
        for j in range(nchunk):
            t_l = nc.dram_tensor(f"{name}_l{j}", [rows, cols], BF16)
            t_g = nc.dram_tensor(f"{name}_g{j}", [2 * rows, cols], BF16)
            ls.append(t_l.ap().rearrange(pat_l, **kw))
            gs.append(t_g.ap().rearrange(pat_g, r=2, **kw))
            c.cc_list.append((f"{name}{j}", t_l, t_g, name))
        setattr(c, name + "_l", ls)
        setattr(c, name + "_g", gs)
        setattr(c, name + "_l_res", Res(name + "_l"))
        setattr(c, name + "_g_res", Res(name + "_g"))

    scratch("KTm", 4, 2 * 96, TOK, "(h d) t -> h d t", "(r h d) t -> r h d t", h=2)
    scratch("Vm", 4, 2 * 128, NT * 66, "(h p) (t e) -> h p t e", "(r h p) (t e) -> r h p t e", h=2, t=NT)
    scratch("KTs", 1, 2 * 64, TOK, "(h d) t -> h d t", "(r h d) t -> r h d t", h=2)
    scratch("Vs", 1, NT * 128, 130, "(t p) e -> t p e", "(r t p) e -> r t p e", t=NT)
    scratch("KTn", 1, 4 * 64, TOK, "(h d) t -> h d t", "(r h d) t -> r h d t", h=4)
    scratch("Vn", 2, (NT // 2) * 128, 260, "(t p) e -> t p e", "(r t p) e -> r t p e", t=NT // 2)
    for nm, shp in (("QTm", [8, 96, TOK]), ("QTs", [4, 64, TOK]), ("QTn", [4, 64, TOK])):
        setattr(c, nm, nc.dram_tensor(nm, shp, BF16, **dk).ap())
        setattr(c, nm + "_res", Res(nm))
    c.wres = Res("weights")
    c.xin_res = [Res(f"xin{t}") for t in range(NT)]
    c.xs_res = [Res(f"xs{t}") for t in range(NT)]
    c.y_res = [Res(f"y{t}") for t in range(NT)]
    with contextlib.ExitStack() as stack:
        arena_t = stack.enter_context(nc.sbuf_tensor("arena", [128, ARENA_BYTES // 4], F32))
        psum = stack.enter_context(nc.psum_tensor("ps", [128, 4096], F32))
        P = Prog(nc, stack)
        A = Arena(arena_t, ARENA_BYTES)
        k = K(nc, P, A, psum)
        ds_c = P.dsem("const")
        c.ident_f = A.alloc("ident_f", 128)
        c.ident_bf = A.alloc("ident_bf", 128, BF16)
        c.cos = A.alloc("cos", NT * 16)
        c.sin = A.alloc("sin", NT * 16)
        k.dma("sp", c.ident_f, T(c.ident_d, c.wres), ds_c)
        k.dma("pool", c.ident_bf, T(c.ident_d, c.wres), ds_c)
        k.dma("sp", c.cos.r("p (t d) -> p t d", t=NT), T(c.cos_d.rearrange("(t p) d -> p t d", p=128), c.wres), ds_c)
        k.dma("sp", c.sin.r("p (t d) -> p t d", t=NT), T(c.sin_d.rearrange("(t p) d -> p t d", p=128), c.wres), ds_c)
        w = c.w
        if stop == "ffn1":
            ffn_phase(k, c, c.x_in, c.xin_res, c.y, c.y_res, w["ffn1_w_in"][0], w["ffn1_w_out"][0], w["ffn1_norm"][0])
        else:
            for l in range(nlayers):
                last = (l == nlayers - 1)
                src, src_res = (c.x_in, c.xin_res) if l == 0 else (c.xs, c.xs_res)
                ffn_phase(k, c, src, src_res, c.xs, c.xs_res, w["ffn1_w_in"][l], w["ffn1_w_out"][l], w["ffn1_norm"][l])
                phase_a(k, c, l)
                if stop in ("a", "a0", "a1", "a2", "a3") or stop.startswith("k"):
                    break
                phase_m(k, c, l)
                if stop == "m":
                    break
                phase_b(k, c, l)
                if stop == "b":
                    break
                dst, dst_res = (c.y, c.y_res) if last else (c.xs, c.xs_res)
                ffn_phase(k, c, c.xs, c.xs_res, dst, dst_res, w["ffn2_w_in"][l], w["ffn2_w_out"][l], w["ffn2_norm"][l],
                          fin_g_d=w["block_norm"][l])
        P.barrier()
        block = stack.enter_context(nc.Block())
        P.emit(block)
    return nc


def _swa_table(gq, gk):
    slopes = (2.0 ** (-8.0 * np.arange(1, 5, dtype=np.float32) / 4)).astype(np.float32)
    q = np.arange(128)[None, :]
    kk = np.arange(128)[:, None]
    dist = np.abs((gq * 128 + q) - (gk * 128 + kk))
    valid = (dist <= 128) & (0 <= gk < SEQ // 128)
    val = -slopes[None, :, None] * dist[:, None, :].astype(np.float32)
    out = np.where(valid[:, None, :], val, np.float32(NEG)).astype(np.float32)
    return out.reshape(128, 512)


def _na_table(rb, gq, gk):
    rows = SEQ // GW
    q = np.arange(128)[None, :]
    kk = np.arange(128)[:, None]
    qrow, qcol = 2 * gq + q // 64, q % 64
    krow, kcol = 2 * gk + kk // 64, kk % 64
    r0 = np.clip(qrow - 4, 0, rows - 8)
    c0 = np.clip(qcol - 8, 0, GW - 16)
    valid = (krow >= r0) & (krow < r0 + 8) & (kcol >= c0) & (kcol < c0 + 16) & (0 <= gk < SEQ // 128)
    dr = np.clip(krow - qrow + 7, 0, 14)
    dc = np.clip(kcol - qcol + 15, 0, 30)
    g = rb[:, dr, dc]
    out = np.where(valid[None], g, np.float32(NEG)).astype(np.float32)
    return np.ascontiguousarray(out.transpose(1, 0, 2)).reshape(128, 512)


def make_tables(inputs, h):
    rb = np.asarray(inputs["na_rel_bias"], dtype=np.float32)
    gmid = 32 * h + 10
    swa = np.stack([_swa_table(gmid, gmid - 1), _swa_table(gmid, gmid), _swa_table(gmid, gmid + 1),
                    _swa_table(32 * h, 32 * h - 1), _swa_table(32 * h + 31, 32 * h + 32)])
    na_gen = np.stack([np.stack([_na_table(rb[l], gmid, gmid + i) for i in range(-2, 3)]) for l in range(DEPTH)])
    na_edge = np.stack([np.stack([np.stack([_na_table(rb[l], 32 * h + t, 32 * h + t + i) for i in range(-3, 4)])
                                  for t in (0, 1, NT - 2, NT - 1)]) for l in range(DEPTH)])
    pos = (h * TOK + np.arange(TOK, dtype=np.float32)).astype(np.float32)
    inv = (1.0 / (np.float32(10000.0) ** (np.arange(0, 32, 2, dtype=np.float32) / np.float32(32)))).astype(np.float32)
    ang = (pos[:, None] * inv[None, :]).astype(np.float32)
    return {"swa_tab": swa, "na_gen": na_gen, "na_edge": na_edge,
            "cos": np.cos(ang).astype(np.float32), "sin": np.sin(ang).astype(np.float32)}


def make_in_maps(inputs):
    x = np.ascontiguousarray(np.asarray(inputs["x"], dtype=np.float32))
    mem = np.ascontiguousarray(np.asarray(inputs["mem"], dtype=np.float32))
    ident = np.eye(128, dtype=np.float32)
    ws = {n: np.ascontiguousarray(np.asarray(inputs[n], dtype=np.float32)) for n in WNAMES}
    tabs = [make_tables(inputs, 0), make_tables(inputs, 1)]
    maps = []
    for core in range(8):
        b, h = core // 2, core % 2
        m = {"x": x[b, h * TOK:(h + 1) * TOK], "mem": mem[b], "ident": ident}
        m.update(tabs[h])
        m.update(ws)
        maps.append(m)
    return maps


def kernel(**inputs):
    nc = build("full")
    res = run_bass_kernel_spmd(nc, make_in_maps(inputs), core_ids=list(range(8)))
    out = np.empty((BATCH, SEQ, D), np.float32)
    for core in range(8):
        b, h = core // 2, core % 2
        out[b, h * TOK:(h + 1) * TOK] = np.asarray(res.results[core]["y"])
    return out
```
